# Optimizing a Trainium2 kernel written in Bass

```python
import math
import jax
import jax.numpy as jnp
from jax import lax
import numpy as np


D_MODEL = 1024
BATCH = 4
SEQ = 4096
DEPTH = 4

GRID_W = 64
CTX_LEN = 256
QBLOCK = 128

DIFF_HEADS = 4
DIFF_QK_DIM = 64
DIFF_V_DIM = 2 * DIFF_QK_DIM
DIFF_WIDTH = DIFF_HEADS * DIFF_V_DIM

S5_WIDTH = 512
S5_GROUP = 16
S5_GROUPS = S5_WIDTH // S5_GROUP
S5_STATE = 64

WIN_Q_HEADS = 8
WIN_KV_HEADS = 2
WIN_HEAD_DIM = 64
WIN_GROUP = WIN_Q_HEADS // WIN_KV_HEADS
WIN_WIDTH = WIN_Q_HEADS * WIN_HEAD_DIM
WINDOW = 128

ROPE_DIM = 64
N_BRANCHES = 3
D_FF = 4 * D_MODEL
ROPE_BASE = 10000.0
EPS = 1e-6
NEG_INF = -1e30

IN_WIDTHS = (2 * DIFF_HEADS * DIFF_QK_DIM, 2 * DIFF_HEADS * DIFF_QK_DIM, DIFF_WIDTH, S5_WIDTH,
             WIN_WIDTH, WIN_KV_HEADS * WIN_HEAD_DIM, WIN_KV_HEADS * WIN_HEAD_DIM, N_BRANCHES * D_MODEL)
D_IN = sum(IN_WIDTHS)

kernel_name = 'hybrid_diffattn_s5_swa_prefix_dit'


def _in_offsets():
    offs, acc = [], 0
    for w in IN_WIDTHS[:-1]:
        acc += w
        offs.append(acc)
    return offs


def rms_norm(x, g):
    xf = x.astype(jnp.float32)
    y = xf * lax.rsqrt(jnp.mean(jnp.square(xf), -1, keepdims=True) + EPS)
    return (y * g.astype(jnp.float32)).astype(x.dtype)


def axial_rope(rows, head_dim):
    n_freq = head_dim // 4
    inv = ROPE_BASE ** (-jnp.arange(n_freq, dtype=jnp.float32) / n_freq)
    r = jnp.repeat(jnp.arange(rows, dtype=jnp.float32), GRID_W)
    col = jnp.tile(jnp.arange(GRID_W, dtype=jnp.float32), rows)
    ang = jnp.concatenate([r[:, None] * inv, col[:, None] * inv], -1)
    return jnp.cos(ang), jnp.sin(ang)


def apply_rope(x, cos, sin):
    bshape = (1, x.shape[1]) + (1,) * (x.ndim - 3) + (cos.shape[-1],)
    cos = cos.reshape(bshape)
    sin = sin.reshape(bshape)
    x1, x2 = jnp.split(x.astype(jnp.float32), 2, -1)
    return jnp.concatenate([x1 * cos - x2 * sin, x2 * cos + x1 * sin], -1).astype(x.dtype)


def diff_attention(q_lat, k_lat, v_lat, q_ctx, k_ctx, v_ctx, cos, sin, q_g, k_g, lam, lam_init, out_g, need_ctx):
    b, n, _ = q_lat.shape
    nc = q_ctx.shape[1]
    sh = (2, DIFF_HEADS, DIFF_QK_DIM)
    scale = DIFF_QK_DIM ** -0.5
    ql = apply_rope(rms_norm(q_lat.reshape((b, n) + sh), q_g), cos, sin)
    kl = apply_rope(rms_norm(k_lat.reshape((b, n) + sh), k_g), cos, sin)
    kc = rms_norm(k_ctx.reshape((b, nc) + sh), k_g)
    vl = v_lat.reshape(b, n, DIFF_HEADS, DIFF_V_DIM)
    vc = v_ctx.reshape(b, nc, DIFF_HEADS, DIFF_V_DIM)
    k_all = jnp.concatenate([kc, kl], 1)
    v_all = jnp.concatenate([vc, vl], 1)

    def attend(q, k, v):
        s = jnp.einsum('bqmhd,bkmhd->bmhqk', q, k).astype(jnp.float32) * scale
        p = jax.nn.softmax(s, -1)
        p = p[:, 0] - lam * p[:, 1]
        return jnp.einsum('bhqk,bkhe->bqhe', p.astype(v.dtype), v)

    def post(o, length):
        o = rms_norm(o, out_g) * (1.0 - lam_init)
        return o.reshape(b, length, DIFF_WIDTH)

    nb = n // QBLOCK
    q_blocks = ql.reshape((b, nb, QBLOCK) + sh).swapaxes(0, 1)
    o = lax.map(lambda qb: attend(qb, k_all, v_all), q_blocks)
    y_lat = post(o.swapaxes(0, 1).reshape(b, n, DIFF_HEADS, DIFF_V_DIM), n)
    y_ctx = None
    if need_ctx:
        qc = rms_norm(q_ctx.reshape((b, nc) + sh), q_g)
        y_ctx = post(attend(qc, kc, vc), nc)
    return y_lat, y_ctx


def window_gqa(q_lat, k_lat, v_lat, q_ctx, k_ctx, v_ctx, cos, sin, q_g, k_g, sink, need_ctx):
    b, n, _ = q_lat.shape
    nc = q_ctx.shape[1]
    scale = WIN_HEAD_DIM ** -0.5
    ql = apply_rope(rms_norm(q_lat.reshape(b, n, WIN_KV_HEADS, WIN_GROUP, WIN_HEAD_DIM), q_g), cos, sin)
    kl = apply_rope(rms_norm(k_lat.reshape(b, n, WIN_KV_HEADS, WIN_HEAD_DIM), k_g), cos, sin)
    vl = v_lat.reshape(b, n, WIN_KV_HEADS, WIN_HEAD_DIM)
    kc = rms_norm(k_ctx.reshape(b, nc, WIN_KV_HEADS, WIN_HEAD_DIM), k_g)
    vc = v_ctx.reshape(b, nc, WIN_KV_HEADS, WIN_HEAD_DIM)
    sink_l = sink.astype(jnp.float32).reshape(WIN_KV_HEADS, WIN_GROUP)

    nb = n // QBLOCK
    qb = ql.reshape(b, nb, QBLOCK, WIN_KV_HEADS, WIN_GROUP, WIN_HEAD_DIM)
    pad = ((0, 0), (QBLOCK, QBLOCK), (0, 0), (0, 0))
    kp = jnp.pad(kl, pad).reshape(b, nb + 2, QBLOCK, WIN_KV_HEADS, WIN_HEAD_DIM)
    vp = jnp.pad(vl, pad).reshape(b, nb + 2, QBLOCK, WIN_KV_HEADS, WIN_HEAD_DIM)
    k_band = jnp.concatenate([kp[:, :-2], kp[:, 1:-1], kp[:, 2:]], 2)
    v_band = jnp.concatenate([vp[:, :-2], vp[:, 1:-1], vp[:, 2:]], 2)
    qi = jnp.arange(QBLOCK)
    sj = jnp.arange(3 * QBLOCK)
    key_pos = jnp.arange(nb)[:, None] * QBLOCK - QBLOCK + sj[None, :]
    rel = sj[None, :] - QBLOCK - qi[:, None]
    valid = (jnp.abs(rel) <= WINDOW)[None] & ((key_pos >= 0) & (key_pos < n))[:, None, :]

    s_band = jnp.einsum('bnqkgd,bnskd->bnkgqs', qb, k_band).astype(jnp.float32) * scale
    s_band = jnp.where(valid[None, :, None, None], s_band, NEG_INF)
    s_ctx = jnp.einsum('bnqkgd,bskd->bnkgqs', qb, kc).astype(jnp.float32) * scale
    s_sink = jnp.broadcast_to(sink_l[None, None, :, :, None, None], s_ctx.shape[:-1] + (1,))
    p = jax.nn.softmax(jnp.concatenate([s_ctx, s_band, s_sink], -1), -1).astype(vl.dtype)
    o = (jnp.einsum('bnkgqs,bskd->bnqkgd', p[..., :nc], vc)
         + jnp.einsum('bnkgqs,bnskd->bnqkgd', p[..., nc:nc + 3 * QBLOCK], v_band))
    y_lat = o.reshape(b, n, WIN_WIDTH)
    y_ctx = None
    if need_ctx:
        qc = rms_norm(q_ctx.reshape(b, nc, WIN_KV_HEADS, WIN_GROUP, WIN_HEAD_DIM), q_g)
        s = jnp.einsum('bqkgd,bskd->bkgqs', qc, kc).astype(jnp.float32) * scale
        s_sk = jnp.broadcast_to(sink_l[None, :, :, None, None], s.shape[:-1] + (1,))
        pc = jax.nn.softmax(jnp.concatenate([s, s_sk], -1), -1).astype(vc.dtype)
        y_ctx = jnp.einsum('bkgqs,bskd->bqkgd', pc[..., :nc], vc).reshape(b, nc, WIN_WIDTH)
    return y_lat, y_ctx


def s5_discretize(lam_re, lam_im, log_dt, b_re, b_im):
    dt = jnp.exp(log_dt)[:, None]
    mag = jnp.exp(lam_re * dt)
    ang = lam_im * dt
    a_re = mag * jnp.cos(ang)
    a_im = mag * jnp.sin(ang)
    den = jnp.square(lam_re) + jnp.square(lam_im)
    n_re = a_re - 1.0
    f_re = (n_re * lam_re + a_im * lam_im) / den
    f_im = (a_im * lam_re - n_re * lam_im) / den
    bb_re = f_re[..., None] * b_re - f_im[..., None] * b_im
    bb_im = f_re[..., None] * b_im + f_im[..., None] * b_re
    return a_re, a_im, bb_re, bb_im


def _ssm_combine(left, right):
    a1r, a1i, b1r, b1i = left
    a2r, a2i, b2r, b2i = right
    return (a2r * a1r - a2i * a1i, a2r * a1i + a2i * a1r,
            a2r * b1r - a2i * b1i + b2r, a2r * b1i + a2i * b1r + b2i)


def s5_scan(a_re, a_im, bu_re, bu_im, reverse, init=None):
    if init is not None:
        idx = bu_re.shape[1] - 1 if reverse else 0
        s_re, s_im = init
        bu_re = bu_re.at[:, idx].add(a_re * s_re - a_im * s_im)
        bu_im = bu_im.at[:, idx].add(a_re * s_im + a_im * s_re)
    ar = jnp.broadcast_to(a_re, bu_re.shape)
    ai = jnp.broadcast_to(a_im, bu_re.shape)
    _, _, s_re, s_im = lax.associative_scan(_ssm_combine, (ar, ai, bu_re, bu_im), reverse=reverse, axis=1)
    return s_re, s_im


def s5_readout(c_re, c_im, s_re, s_im):
    y = jnp.einsum('ghp,blgp->blgh', c_re, s_re) - jnp.einsum('ghp,blgp->blgh', c_im, s_im)
    return y.reshape(y.shape[0], y.shape[1], S5_WIDTH)


def s5_mixer(u_lat, u_ctx, lam_re, lam_im, log_dt, b_re, b_im, c_re, c_im, d_skip, w_glu, need_ctx):
    f32 = jnp.float32
    b, n, _ = u_lat.shape
    nc = u_ctx.shape[1]
    ul = u_lat.astype(f32).reshape(b, n, S5_GROUPS, S5_GROUP)
    uc = u_ctx.astype(f32).reshape(b, nc, S5_GROUPS, S5_GROUP)
    dsk = d_skip.astype(f32)
    y_lat = u_lat.astype(f32) * dsk
    y_ctx = u_ctx.astype(f32) * dsk if need_ctx else None
    for direction, reverse in ((0, False), (1, True)):
        a_re, a_im, bb_re, bb_im = s5_discretize(lam_re[direction].astype(f32), lam_im[direction].astype(f32),
                                                 log_dt[direction].astype(f32), b_re[direction].astype(f32),
                                                 b_im[direction].astype(f32))
        cr = c_re[direction].astype(f32)
        ci = c_im[direction].astype(f32)
        cs_re, cs_im = s5_scan(a_re, a_im, jnp.einsum('gph,blgh->blgp', bb_re, uc),
                               jnp.einsum('gph,blgh->blgp', bb_im, uc), reverse)
        edge = 0 if reverse else nc - 1
        ls_re, ls_im = s5_scan(a_re, a_im, jnp.einsum('gph,blgh->blgp', bb_re, ul),
                               jnp.einsum('gph,blgh->blgp', bb_im, ul), reverse,
                               init=(cs_re[:, edge], cs_im[:, edge]))
        y_lat = y_lat + s5_readout(cr, ci, ls_re, ls_im)
        if need_ctx:
            y_ctx = y_ctx + s5_readout(cr, ci, cs_re, cs_im)

    def glu(y):
        g = jax.nn.gelu(y.astype(u_lat.dtype))
        return g * jax.nn.sigmoid(g @ w_glu)

    return glu(y_lat), (glu(y_ctx) if need_ctx else None)


def setup_inputs(seed: int = 0) -> dict:
    key = jax.random.key(seed)
    ks = iter(jax.random.split(key, 48))
    f32 = jnp.float32

    def nrm(shape, scale):
        return jax.random.normal(next(ks), shape, f32) * scale

    def gain(shape):
        return 1.0 + nrm(shape, 0.02)

    L, D = DEPTH, D_MODEL
    G, P, HG = S5_GROUPS, S5_STATE, S5_GROUP
    n_idx = jnp.arange(P, dtype=f32)
    return {
        'x': nrm((BATCH, SEQ, D), 1.0),
        'c': nrm((BATCH, D), 1.0),
        'ctx': nrm((BATCH, CTX_LEN, D), 1.0),
        'c_ctx': nrm((D,), 1.0),
        'w_mod': nrm((L, D, 6 * D), 0.5 * D ** -0.5),
        'b_mod': nrm((L, 6 * D), 0.02),
        'norm1_g': gain((L, D)),
        'norm2_g': gain((L, D)),
        'w_in': nrm((L, D, D_IN), D ** -0.5),
        'diff_q_norm_g': gain((L, DIFF_QK_DIM)),
        'diff_k_norm_g': gain((L, DIFF_QK_DIM)),
        'diff_lam_q1': nrm((L, DIFF_QK_DIM), 0.1),
        'diff_lam_k1': nrm((L, DIFF_QK_DIM), 0.1),
        'diff_lam_q2': nrm((L, DIFF_QK_DIM), 0.1),
        'diff_lam_k2': nrm((L, DIFF_QK_DIM), 0.1),
        'diff_out_norm_g': gain((L, DIFF_V_DIM)),
        's5_lambda_re': -0.5 + nrm((L, 2, G, P), 0.01),
        's5_lambda_im': math.pi * n_idx + nrm((L, 2, G, P), 0.01),
        's5_log_dt': jax.random.uniform(next(ks), (L, 2, G), f32, math.log(1e-3), math.log(1e-1)),
        's5_b_re': nrm((L, 2, G, P, HG), (2 * HG) ** -0.5),
        's5_b_im': nrm((L, 2, G, P, HG), (2 * HG) ** -0.5),
        's5_c_re': nrm((L, 2, G, HG, P), (2 * P) ** -0.5 * 4.0),
        's5_c_im': nrm((L, 2, G, HG, P), (2 * P) ** -0.5 * 4.0),
        's5_d': nrm((L, S5_WIDTH), 1.0),
        's5_w_glu': nrm((L, S5_WIDTH, S5_WIDTH), S5_WIDTH ** -0.5),
        'win_q_norm_g': gain((L, WIN_HEAD_DIM)),
        'win_k_norm_g': gain((L, WIN_HEAD_DIM)),
        'win_sink': nrm((L, WIN_Q_HEADS), 1.0),
        'w_proj_diff': nrm((L, DIFF_WIDTH, D), DIFF_WIDTH ** -0.5),
        'w_proj_s5': nrm((L, S5_WIDTH, D), S5_WIDTH ** -0.5),
        'w_proj_win': nrm((L, WIN_WIDTH, D), WIN_WIDTH ** -0.5),
        'w_out': nrm((L, D, D), D ** -0.5),
        'w_ff1': nrm((L, D, D_FF), D ** -0.5),
        'w_ff2': nrm((L, D_FF, D), D_FF ** -0.5),
    }


def reference(x, c, ctx, c_ctx, w_mod, b_mod, norm1_g, norm2_g, w_in,
              diff_q_norm_g, diff_k_norm_g, diff_lam_q1, diff_lam_k1, diff_lam_q2, diff_lam_k2, diff_out_norm_g,
              s5_lambda_re, s5_lambda_im, s5_log_dt, s5_b_re, s5_b_im, s5_c_re, s5_c_im, s5_d, s5_w_glu,
              win_q_norm_g, win_k_norm_g, win_sink,
              w_proj_diff, w_proj_s5, w_proj_win, w_out, w_ff1, w_ff2):
    n_lat = x.shape[1]
    ROWS = n_lat // GRID_W
    cos, sin = axial_rope(ROWS, ROPE_DIM)
    offs = _in_offsets()
    silu_c = jax.nn.silu(c)
    silu_cc = jax.nn.silu(c_ctx)
    h_lat, h_ctx = x, ctx
    for l in range(DEPTH):
        need_ctx = l < DEPTH - 1
        lam_init = 0.8 - 0.6 * math.exp(-0.3 * l)
        mod_l = (silu_c @ w_mod[l] + b_mod[l])[:, None, :]
        mod_c = (silu_cc @ w_mod[l] + b_mod[l])[None, None, :]
        sh1, sc1, g1, sh2, sc2, g2 = jnp.split(mod_l, 6, -1)
        csh1, csc1, cg1, csh2, csc2, cg2 = jnp.split(mod_c, 6, -1)

        a_lat = rms_norm(h_lat, norm1_g[l]) * (1.0 + sc1) + sh1
        a_ctx = rms_norm(h_ctx, norm1_g[l]) * (1.0 + csc1) + csh1
        dq_l, dk_l, dv_l, su_l, wq_l, wk_l, wv_l, gt_l = jnp.split(a_lat @ w_in[l], offs, -1)
        dq_c, dk_c, dv_c, su_c, wq_c, wk_c, wv_c, gt_c = jnp.split(a_ctx @ w_in[l], offs, -1)

        lam = (jnp.exp(jnp.sum(diff_lam_q1[l] * diff_lam_k1[l]).astype(jnp.float32))
               - jnp.exp(jnp.sum(diff_lam_q2[l] * diff_lam_k2[l]).astype(jnp.float32)) + lam_init)
        yd_l, yd_c = diff_attention(dq_l, dk_l, dv_l, dq_c, dk_c, dv_c, cos, sin, diff_q_norm_g[l],
                                    diff_k_norm_g[l], lam, lam_init, diff_out_norm_g[l], need_ctx)
        ys_l, ys_c = s5_mixer(su_l, su_c, s5_lambda_re[l], s5_lambda_im[l], s5_log_dt[l], s5_b_re[l],
                              s5_b_im[l], s5_c_re[l], s5_c_im[l], s5_d[l], s5_w_glu[l], need_ctx)
        yw_l, yw_c = window_gqa(wq_l, wk_l, wv_l, wq_c, wk_c, wv_c, cos, sin, win_q_norm_g[l],
                                win_k_norm_g[l], win_sink[l], need_ctx)

        def merge(yd, ys, yw, gates):
            ga, gb, gc = jnp.split(jax.nn.sigmoid(gates), N_BRANCHES, -1)
            m = ga * (yd @ w_proj_diff[l]) + gb * (ys @ w_proj_s5[l]) + gc * (yw @ w_proj_win[l])
            return m @ w_out[l]

        h_lat = h_lat + g1 * merge(yd_l, ys_l, yw_l, gt_l)
        f_lat = rms_norm(h_lat, norm2_g[l]) * (1.0 + sc2) + sh2
        h_lat = h_lat + g2 * (jnp.square(jax.nn.relu(f_lat @ w_ff1[l])) @ w_ff2[l])
        if need_ctx:
            h_ctx = h_ctx + cg1 * merge(yd_c, ys_c, yw_c, gt_c)
            f_ctx = rms_norm(h_ctx, norm2_g[l]) * (1.0 + csc2) + csh2
            h_ctx = h_ctx + cg2 * (jnp.square(jax.nn.relu(f_ctx @ w_ff1[l])) @ w_ff2[l])
    return h_lat
```

```python
import math
import types
import numpy as np
from contextlib import ExitStack
import concourse.bass as bass
import concourse.mybir as mybir
from concourse.bass_utils import run_bass_kernel_spmd

F32 = mybir.dt.float32
BF16 = mybir.dt.bfloat16
ALU = mybir.AluOpType
AF = mybir.ActivationFunctionType
AX = mybir.AxisListType

D = 1024
NCTX = 256
NLAT = 2048
NT = NCTX + NLAT
NTILE = NT // 128
GLAT = 4096
GNT = NCTX + GLAT
GTILE = GNT // 128
LT = NLAT // 128
DEPTH = 4
EPS = 1e-6
DIN = 5888
J = 128
PI = math.pi


class Buf:
    __slots__ = ("name", "w", "rc", "rd")

    def __init__(self, name=""):
        self.name = name
        self.w = None
        self.rc = {}
        self.rd = []


class Op:
    __slots__ = ("eng", "fn", "r", "w", "dma", "deps", "waits", "need_inc", "ev", "idx", "bar")

    def __init__(self, eng, fn, r, w, dma):
        self.eng = eng
        self.fn = fn
        self.r = r
        self.w = w
        self.dma = dma
        self.need_inc = False
        self.ev = None
        self.waits = None
        self.bar = False


class Prog:
    NDMA = 12
    LIMIT = 30000

    def __init__(self, nc):
        self.nc = nc
        self.ops = []
        self.cc_n = 0
        self.engs = {"pe": nc.tensor, "act": nc.scalar, "dve": nc.vector,
                     "pool": nc.gpsimd, "sp": nc.sync}

    @staticmethod
    def _freeze(fn):
        if fn is None or fn.__closure__ is None:
            return fn
        cells = []
        for c in fn.__closure__:
            try:
                cells.append(types.CellType(c.cell_contents))
            except ValueError:
                cells.append(c)
        return types.FunctionType(fn.__code__, fn.__globals__, fn.__name__, fn.__defaults__, tuple(cells))

    def add(self, eng, fn, r=(), w=(), dma=False):
        op = Op(eng, self._freeze(fn), tuple(r), tuple(w), dma)
        op.idx = len(self.ops)
        self.ops.append(op)
        return op

    def pe(self, fn, r=(), w=()): return self.add("pe", fn, r, w)
    def act(self, fn, r=(), w=()): return self.add("act", fn, r, w)
    def dve(self, fn, r=(), w=()): return self.add("dve", fn, r, w)
    def pool(self, fn, r=(), w=()): return self.add("pool", fn, r, w)

    def dma(self, q, out, in_, r=(), w=(), **kw):
        return self.add(q, lambda e: e.dma_start(out=out, in_=in_, **kw), r, w, dma=True)

    def allgather(self, in_dt, out_dt, r, w):
        self.cc_n += 1
        n = self.cc_n
        sem, flag, rg = self.cc_sem, self.cc_flag, self.cc_rg
        ia, oa = in_dt.ap, out_dt.ap

        def fn(E):
            E.collective_compute("AllGather", ALU.bypass, replica_groups=rg, ins=[ia.opt()], outs=[oa.opt()]).then_inc(sem)
            E.wait_ge(sem, n)
            return E.memset(flag, 0.0)
        return self.add("pool", fn, r, w)

    def barrier(self):
        for e in self.engs:
            op = self.add(e, None)
            op.bar = True

    def finalize(self, stack):
        nc = self.nc
        ops = self.ops
        dma_rr = {}
        dma_last = {}
        last_c = {}
        dma_since = []
        for i, op in enumerate(ops):
            deps = set()
            if op.bar:
                deps.update(last_c.values())
                deps.update(dma_since)
            for b in op.r:
                if b.w is not None:
                    deps.add(b.w)
            for b in op.w:
                if b.w is not None:
                    deps.add(b.w)
                deps.update(b.rc.values())
                deps.update(b.rd)
            for b in op.r:
                if op.dma:
                    b.rd.append(i)
                else:
                    b.rc[op.eng] = i
            for b in op.w:
                b.w = i
                b.rc = {}
                b.rd = []
            if op.dma:
                k = dma_rr.get(op.eng, 0)
                dma_rr[op.eng] = k + 1
                slot = (op.eng, k % self.NDMA)
                if slot in dma_last:
                    deps.add(dma_last[slot])
                dma_last[slot] = i
                op.ev = slot
                dma_since.append(i)
                if len(dma_since) > 3 * self.NDMA:
                    dma_since = dma_since[-3 * self.NDMA:]
            elif op.fn is not None:
                last_c[op.eng] = i
            deps.discard(i)
            op.deps = deps
        known_c = {e: {} for e in self.engs}
        known_d = {e: {} for e in self.engs}
        dma_cnt = {}
        for i, op in enumerate(ops):
            X = op.eng
            cand = {}
            dwaits = []
            for d in op.deps:
                od = ops[d]
                if od.dma:
                    slot, val = od.ev
                    if known_d[X].get(slot, 0) < val:
                        known_d[X][slot] = val
                        dwaits.append((slot, val))
                else:
                    if od.eng == "pe" and X == "pe" and not op.dma and not op.bar:
                        continue
                    if cand.get(od.eng, -1) < d:
                        cand[od.eng] = d
            cw = []
            for Y, d in cand.items():
                if known_c[X].get(Y, -1) < d:
                    known_c[X][Y] = d
                    ops[d].need_inc = True
                    cw.append(d)
            op.waits = (cw, dwaits)
            if op.dma:
                slot = op.ev
                v = dma_cnt.get(slot, 0) + 16
                dma_cnt[slot] = v
                op.ev = (slot, v)
        sem_c = {}
        cnt = {e: 0 for e in self.engs}
        epoch = {e: 0 for e in self.engs}

        def get_sem(key):
            if key not in sem_c:
                sem_c[key] = stack.enter_context(nc.semaphore("s_%s_%s" % key))
            return sem_c[key]

        for op in ops:
            if op.dma or op.fn is None:
                continue
            if op.need_inc:
                e = op.eng
                if cnt[e] >= self.LIMIT:
                    cnt[e] = 0
                    epoch[e] += 1
                cnt[e] += 1
                op.ev = ((e, "c%d" % epoch[e]), cnt[e])
        n_wait = 0
        for op in ops:
            E = self.engs[op.eng]
            cw, dwaits = op.waits
            for d in cw:
                key, val = ops[d].ev
                E.wait_ge(get_sem(key), val)
                n_wait += 1
            for slot, val in dwaits:
                E.wait_ge(get_sem((slot[0], "d%d" % slot[1])), val)
                n_wait += 1
            if op.fn is None:
                continue
            ins = op.fn(E)
            if op.dma:
                slot, val = op.ev
                ins.then_inc(get_sem((slot[0], "d%d" % slot[1])), 16)
            elif op.need_inc:
                key, val = op.ev
                ins.then_inc(get_sem(key), 1)
        self.stats = dict(n_ops=len(ops), n_wait=n_wait, n_sem=len(sem_c),
                          n_inc=sum(1 for o in ops if o.need_inc))
        return self.stats


class T:
    def __init__(self, nc, st, name, shape, dt, psum=False):
        if psum:
            self.t = st.enter_context(nc.psum_tensor(name, shape, dt))
        else:
            self.t = st.enter_context(nc.sbuf_tensor(name, shape, dt))
        self.b = Buf(name)
        self.sub = {}
        self.name = name

    def __getitem__(self, idx):
        return self.t[idx]

    def k(self, key):
        if key not in self.sub:
            self.sub[key] = Buf("%s.%s" % (self.name, key))
        return self.sub[key]


class DT:
    def __init__(self, ap, name):
        self.ap = ap
        self.name = name
        self.b = Buf(name)
        self.sub = {}

    def k(self, key):
        if key not in self.sub:
            self.sub[key] = Buf("%s.%s" % (self.name, key))
        return self.sub[key]


BLOCKS = [(0, NCTX, True)] + [(NCTX + 512 * i, 512, False) for i in range(NLAT // 512)]

WEIGHT_SPECS = [
    ("w_mod", [DEPTH, D, 6 * D]), ("b_mod", [DEPTH, 6 * D]), ("norm1_g", [DEPTH, D]), ("norm2_g", [DEPTH, D]),
    ("w_in", [DEPTH, D, DIN]),
    ("diff_q_norm_g", [DEPTH, 64]), ("diff_k_norm_g", [DEPTH, 64]),
    ("diff_lam_q1", [DEPTH, 64]), ("diff_lam_k1", [DEPTH, 64]), ("diff_lam_q2", [DEPTH, 64]), ("diff_lam_k2", [DEPTH, 64]),
    ("diff_out_norm_g", [DEPTH, 128]),
    ("s5_lambda_re", [DEPTH, 2, 16, 64]), ("s5_lambda_im", [DEPTH, 2, 16, 64]), ("s5_log_dt", [DEPTH, 2, 16]),
    ("s5_b_re", [DEPTH, 2, 16, 64, 16]), ("s5_b_im", [DEPTH, 2, 16, 64, 16]),
    ("s5_c_re", [DEPTH, 2, 16, 16, 64]), ("s5_c_im", [DEPTH, 2, 16, 16, 64]),
    ("s5_d", [DEPTH, 256]), ("s5_w_glu", [DEPTH, 512, 512]),
    ("win_q_norm_g", [DEPTH, 64]), ("win_k_norm_g", [DEPTH, 64]), ("win_sink", [DEPTH, 8]),
    ("w_proj_diff", [DEPTH, 512, D]), ("w_proj_s5", [DEPTH, 512, D]), ("w_proj_win", [DEPTH, 512, D]),
    ("w_out", [DEPTH, D, D]), ("w_ff1", [DEPTH, D, 4 * D]), ("w_ff2", [DEPTH, 4 * D, D]),
]


def host_consts():
    c = {}
    c["ident_f"] = np.eye(128, dtype=np.float32)
    n_freq = 16
    inv = (10000.0 ** (-np.arange(n_freq, dtype=np.float32) / n_freq)).astype(np.float32)
    r = np.repeat(np.arange(64, dtype=np.float32), 64)
    col = np.tile(np.arange(64, dtype=np.float32), 64)
    ang = np.concatenate([r[:, None] * inv, col[:, None] * inv], -1).astype(np.float32)
    cs, sn = np.cos(ang).astype(np.float32), np.sin(ang).astype(np.float32)
    c["rope_cos"] = np.concatenate([cs, cs], -1)
    c["rope_sin"] = np.concatenate([-sn, sn], -1)
    kk = np.arange(128)[:, None]
    qq = np.arange(128)[None, :]
    c["mask_prev"] = (kk >= qq).astype(np.float32)
    c["mask_next"] = (kk <= qq).astype(np.float32)
    tau = np.stack([np.arange(J), np.arange(J)[::-1]], 0).astype(np.float32)
    c["tau"] = np.broadcast_to(tau[None], (128, 2, J)).copy()
    gm = np.zeros((128, 8), np.float32)
    gm[np.arange(128), np.arange(128) // 16] = 1.0
    c["gmask"] = gm
    return c


CONST_SPECS = [("ident_f", [128, 128]), ("rope_cos", [NLAT, 64]), ("rope_sin", [NLAT, 64]),
               ("mask_prev", [128, 128]), ("mask_next", [128, 128]), ("tau", [128, 2, J]), ("gmask", [128, 8]), ("flags", [128, 2])]


def build(nl=DEPTH, dbg=(), stop_after=None, dlimit=None, final_out=True, n_cores=8):
    nc = bass.Bass("TRN2", target_bir_lowering=False)
    st = ExitStack()
    with st:
        P = Prog(nc)
        IN = {}

        def din(name, shape):
            IN[name] = nc.dram_tensor(name, list(shape), F32, kind="ExternalInput").ap()
            return IN[name]

        x_d = din("x", [NLAT, D])
        ctx_d = din("ctx", [NCTX, D])
        c_d = din("c", [D])
        cc_d = din("c_ctx", [D])
        for name, shape in WEIGHT_SPECS:
            din(name, [nl] + list(shape[1:]))
        for name, shape in CONST_SPECS:
            din(name, shape)
        out_d = DT(nc.dram_tensor("out", [NLAT, D], F32, kind="ExternalOutput").ap(), "out")

        def dscr(name, shape, dt):
            kind = "ExternalOutput" if name in dbg else "Internal"
            return DT(nc.dram_tensor(name, list(shape), dt, kind=kind).ap(), name)

        hT_d = dscr("hT", [8, 128, NT], F32)
        aT_d = dscr("aT", [8, 128, NT], BF16)
        qdT_d = dscr("qdT", [4, 128, NT], BF16)
        wqT_d = dscr("wqT", [4, 128, NT], BF16)
        kdC_d = dscr("kdC", [4, 128, NCTX], BF16)
        kdL_d = dscr("kdL", [512, NLAT], BF16)
        kdG_d = dscr("kdG", [1024, NLAT], BF16)
        vdC_d = dscr("vdC", [NCTX, 512], BF16)
        vdL_d = dscr("vdL", [NLAT, 512], BF16)
        vdG_d = dscr("vdG", [2 * NLAT, 512], BF16)
        suC_d = dscr("suC", [4, 128, NCTX], BF16)
        suL_d = dscr("suL", [512, NLAT], BF16)
        suG_d = dscr("suG", [1024, NLAT], BF16)
        wkC_d = dscr("wkC", [128, NCTX], BF16)
        wkL_d = dscr("wkL", [128, NLAT], BF16)
        wkG_d = dscr("wkG", [256, NLAT], BF16)
        wvC_d = dscr("wvC", [NCTX, 128], BF16)
        wvL_d = dscr("wvL", [NLAT, 128], BF16)
        wvG_d = dscr("wvG", [2 * NLAT, 128], BF16)
        ypLc_d = dscr("ypLc", [256, NCTX], BF16)
        ypLl_d = dscr("ypLl", [256, GLAT], BF16)
        ypGc_d = dscr("ypGc", [512, NCTX], BF16)
        ypGl_d = dscr("ypGl", [512, GLAT], BF16)
        ydT_d = dscr("ydT", [4, 128, NT], BF16)
        ysT_d = dscr("ysT", [4, 128, NT], BF16)
        ywT_d = dscr("ywT", [8, 64, NT], BF16)
        yf_d = dscr("yf", [2, 128, GNT], F32)
        modv_d = dscr("modv", [128, DEPTH, 48, 2], F32)

        uniq = [0]

        def sb(name, shape, dt, stack=st):
            uniq[0] += 1
            return T(nc, stack, "sb%d_%s" % (uniq[0], name), shape, dt)

        ident_f = sb("ident_f", [128, 128], F32)
        ident_b = sb("ident_b", [128, 128], BF16)
        ones_b = sb("ones_b", [128, 128], BF16)
        flg = sb("flags", [128, 2], F32)
        ccflag = sb("ccflag", [128, 4], F32)
        P.cc_sem = st.enter_context(nc.semaphore("cc_sem"))
        P.cc_flag = ccflag[:]
        P.cc_rg = [[2 * i, 2 * i + 1] for i in range(n_cores // 2)]
        modv = sb("modv_sb", [128, DEPTH, 48, 2], F32)
        gm1 = sb("gm1", [128, DEPTH, 8, 2], F32)
        gm2 = sb("gm2", [128, DEPTH, 8, 2], F32)
        psbig = st.enter_context(nc.psum_tensor("psbig", [128, 8, 512], F32))

        class PSV:
            def __init__(self, i):
                self.i = i
                self.b = Buf("ps%d" % i)

            def __getitem__(self, idx):
                return psbig[:, self.i, :][idx]

        ps = [PSV(i) for i in range(8)]

        def ps2(i):
            return psbig[:, i:i + 2, :].rearrange("p a b -> p (a b)")

        P.dma("sp", ident_f[:], IN["ident_f"], w=[ident_f.b])
        P.dma("sp", flg[:], IN["flags"], w=[flg.b])
        P.dma("pool", ident_b[:], IN["ident_f"], w=[ident_b.b])
        P.pool(lambda e: e.memset(ones_b[:], 1.0), w=[ones_b.b])

        with ExitStack() as ph:
            xt = [sb("xt%d" % i, [128, D], F32, ph) for i in range(2)]
            xT = [sb("xT%d" % i, [128, 8, 128], F32, ph) for i in range(2)]
            for t in range(NTILE):
                src = ctx_d[t * 128:(t + 1) * 128, :] if t < 2 else x_d[(t - 2) * 128:(t - 1) * 128, :]
                a = xt[t % 2]
                o = xT[t % 2]
                P.dma("sp", a[:], src, w=[a.b])
                for half in range(2):
                    pb = ps[(2 * t + half) % 4]
                    for q in range(4):
                        c = half * 4 + q
                        P.pe(lambda e, pb=pb, a=a, c=c, q=q: e.transpose(pb[:, q * 128:(q + 1) * 128], a[:, c * 128:(c + 1) * 128], ident_f[:]),
                             r=[a.b, ident_f.b], w=[pb.b])
                    if half == 0:
                        P.act(lambda e, pb=pb, o=o: e.activation(out=o[:, 0:4, :], in_=pb[:].rearrange("p (c t) -> p c t", c=4), func=AF.Copy),
                              r=[pb.b], w=[o.k(0)])
                    else:
                        P.dve(lambda e, pb=pb, o=o: e.tensor_copy(out=o[:, 4:8, :], in_=pb[:].rearrange("p (c t) -> p c t", c=4)),
                              r=[pb.b], w=[o.k(1)])
                P.dma("sp", hT_d.ap[:, :, t * 128:(t + 1) * 128].rearrange("c p t -> p c t"), o[:],
                      r=[o.k(0), o.k(1)], w=[hT_d.k(t)])

            cT = sb("cT", [128, 8, 2], F32, ph)
            scT = sb("scT", [128, 8, 2], F32, ph)
            P.dma("sp", cT[:, :, 0], c_d.rearrange("(c p) -> p c", p=128), w=[cT.b], allow_slow_non_contiguous=True)
            P.dma("sp", cT[:, :, 1], cc_d.rearrange("(c p) -> p c", p=128), w=[cT.b], allow_slow_non_contiguous=True)
            P.act(lambda e: e.activation(out=scT[:], in_=cT[:], func=AF.Silu), r=[cT.b], w=[scT.b])
            wm = [sb("wm%d" % i, [128, 8, 1024], F32, ph) for i in range(2)]
            bm = sb("bm", [128, DEPTH, 48], F32, ph)
            n1 = sb("n1g", [128, DEPTH, 8], F32, ph)
            n2 = sb("n2g", [128, DEPTH, 8], F32, ph)
            P.dma("sp", bm[:, :nl], IN["b_mod"].rearrange("l (j p) -> p l j", p=128), w=[bm.b], allow_slow_non_contiguous=True)
            P.dma("sp", n1[:, :nl], IN["norm1_g"].rearrange("l (j p) -> p l j", p=128), w=[n1.b], allow_slow_non_contiguous=True)
            P.dma("sp", n2[:, :nl], IN["norm2_g"].rearrange("l (j p) -> p l j", p=128), w=[n2.b], allow_slow_non_contiguous=True)
            pm = ps[4]
            it = 0
            for l in range(nl):
                for piece in range(6):
                    w_ = wm[it % 2]
                    it += 1
                    P.dma("sp", w_[:], IN["w_mod"][l, :, piece * 1024:(piece + 1) * 1024].rearrange("(c p) n -> p c n", p=128), w=[w_.b])
                    for jj in range(8):
                        j = piece * 8 + jj
                        for c in range(8):
                            P.pe(lambda e, w_=w_, jj=jj, c=c, j=j: e.matmul(pm[:, 2 * j:2 * j + 2], w_[:, c, jj * 128:(jj + 1) * 128], scT[:, c, :],
                                                                          start=(c == 0), stop=(c == 7)),
                                 r=[w_.b, scT.b], w=[pm.b])
                P.dve(lambda e, l=l: e.tensor_tensor(out=modv[:, l, :, :], in0=pm[:, 0:96].rearrange("p (j t) -> p j t", t=2),
                                                    in1=bm[:, l, :].unsqueeze(2).to_broadcast([128, 48, 2]), op=ALU.add),
                      r=[pm.b, bm.b], w=[modv.b])
                P.dve(lambda e, l=l: e.scalar_tensor_tensor(out=gm1[:, l, :, :], in0=modv[:, l, 8:16, :], scalar=1.0,
                                                           in1=n1[:, l, :].unsqueeze(2).to_broadcast([128, 8, 2]), op0=ALU.add, op1=ALU.mult),
                      r=[modv.b, n1.b], w=[gm1.b])
                P.dve(lambda e, l=l: e.scalar_tensor_tensor(out=gm2[:, l, :, :], in0=modv[:, l, 32:40, :], scalar=1.0,
                                                           in1=n2[:, l, :].unsqueeze(2).to_broadcast([128, 8, 2]), op0=ALU.add, op1=ALU.mult),
                      r=[modv.b, n2.b], w=[gm2.b])
            if "modv" in dbg:
                P.dma("sp", modv_d.ap, modv[:], r=[modv.b], w=[modv_d.b])
            P.barrier()

        def tiles_of(t0, N):
            return list(range(t0 // 128, (t0 + N) // 128))

        GROUPS = [("dq", 0, 512), ("dk", 512, 512), ("dv", 1024, 512), ("wq", 2048, 512), ("wkv", 2560, 256)]

        def phase_A(l):
            with ExitStack() as ph:
                win = sb("winA", [128, 8, 2816], BF16, ph)
                cos2 = sb("cos2", [128, LT, 64], F32, ph)
                sin2s = sb("sin2s", [128, LT, 64], F32, ph)
                P.dma("sp", cos2[:], IN["rope_cos"].rearrange("(t p) d -> p t d", p=128), w=[cos2.b])
                P.dma("sp", sin2s[:], IN["rope_sin"].rearrange("(t p) d -> p t d", p=128), w=[sin2s.b])
                for c in range(8):
                    rows = IN["w_in"][l, c * 128:(c + 1) * 128, :]
                    P.dma("pool", win[:, c, 0:2048], rows[:, 0:2048], w=[win.k(c)])
                    for kv in range(2):
                        P.dma("pool", win[:, c, 2048:2560].rearrange("p (g kv d) -> p kv g d", g=4, kv=2, d=64)[:, kv, :, :],
                              rows[:, 2048:2560].rearrange("p (kv g d) -> p kv g d", g=4, kv=2, d=64)[:, kv, :, :], w=[win.k(c)])
                    P.dma("pool", win[:, c, 2560:2816], rows[:, 2560:2816], w=[win.k(c)])
                graw = sb("graw", [128, 4, 64], F32, ph)
                gt = sb("gtab", [128, 4, 8, 64], F32, ph)
                for i, nm in enumerate(["diff_q_norm_g", "diff_k_norm_g", "win_q_norm_g", "win_k_norm_g"]):
                    P.dma("sp", graw[:, i, :], IN[nm][l].partition_broadcast(128), w=[graw.k(i)])
                    sc = 0.125 if i in (0, 2) else 1.0
                    P.dve(lambda e, i=i, sc=sc: e.tensor_scalar(out=gt[:, i, :, :], in0=graw[:, i, :].unsqueeze(1).to_broadcast([128, 8, 64]),
                                                              scalar1=sc, scalar2=None, op0=ALU.mult), r=[graw.k(i)], w=[gt.k(i)])
                hb = [sb("hb%d" % i, [128, 8, 512], F32, ph) for i in range(1)] * 2
                sq = sb("sq", [128, 8, 512], BF16, ph)
                rt = sb("rt", [128, 512], F32, ph)
                rstd = sb("rstd", [128, 512], F32, ph)
                tmp = [sb("tmp%d" % i, [128, 512], F32, ph) for i in range(2)]
                aT = [sb("aT%d" % i, [128, 8, 512], BF16, ph) for i in range(2)]
                NS = 4
                sqx = [sb("sqx%d" % i, [128, 512], F32, ph) for i in range(NS)]
                ss = [sb("ss%d" % i, [128, 8], F32, ph) for i in range(NS)]
                ss2 = [sb("ss2%d" % i, [128, 8], F32, ph) for i in range(NS)]
                rs = [sb("rs%d" % i, [128, 8], F32, ph) for i in range(NS)]
                xn = [sb("xn%d" % i, [128, 512], F32, ph) for i in range(NS)]
                xg = [sb("xg%d" % i, [128, 512], F32, ph) for i in range(NS)]
                r1 = [sb("r1%d" % i, [128, 512], F32, ph) for i in range(NS)]
                r2 = [sb("r2%d" % i, [128, 512], F32, ph) for i in range(NS)]
                qtm = [sb("qtm%d" % i, [128, 512], BF16, ph) for i in range(NS)]
                vtm = [sb("vtm%d" % i, [128, 512], BF16, ph) for i in range(2)]
                wvtm = [sb("wvtm%d" % i, [128, 128], BF16, ph) for i in range(2)]
                qdT_b = [sb("qdTb%d" % i, [128, 4, 512], BF16, ph) for i in range(2)]
                kdT_b = [sb("kdTb%d" % i, [128, 4, 512], BF16, ph) for i in range(2)]
                wqT_b = [sb("wqTb%d" % i, [128, 4, 512], BF16, ph) for i in range(2)]
                wkT_b = [sb("wkTb%d" % i, [128, 512], BF16, ph) for i in range(2)]
                suT_b = [sb("suTb%d" % i, [128, 4, 512], BF16, ph) for i in range(2)]
                pst = [ps[6], ps[7]]
                cnt = {"g": 0, "s": 0, "t": 0, "v": 0}

                def qk_post(pb, nh, gi, lat_tile):
                    W = nh * 64
                    si = cnt["s"] % NS
                    cnt["s"] += 1
                    v3 = lambda ap: ap.rearrange("p (h d) -> p h d", d=64)
                    P.act(lambda e: e.activation(out=sqx[si][:, :W], in_=pb[:, :W], func=AF.Square), r=[pb.b], w=[sqx[si].b])
                    P.dve(lambda e: e.tensor_reduce(out=ss[si][:, :nh], in_=v3(sqx[si][:, :W]), axis=AX.X, op=ALU.add), r=[sqx[si].b], w=[ss[si].b])
                    P.act(lambda e: e.activation(out=ss2[si][:, :nh], in_=ss[si][:, :nh], func=AF.Sqrt, scale=1.0 / 64, bias=EPS), r=[ss[si].b], w=[ss2[si].b])
                    P.dve(lambda e: e.reciprocal(out=rs[si][:, :nh], in_=ss2[si][:, :nh]), r=[ss2[si].b], w=[rs[si].b])
                    P.dve(lambda e: e.tensor_tensor(out=v3(xn[si][:, :W]), in0=v3(pb[:, :W]), in1=rs[si][:, :nh].unsqueeze(2).to_broadcast([128, nh, 64]), op=ALU.mult),
                          r=[pb.b, rs[si].b], w=[xn[si].b])
                    o = qtm[si]
                    if lat_tile is None:
                        P.dve(lambda e: e.tensor_tensor(out=v3(o[:, :W]), in0=v3(xn[si][:, :W]), in1=gt[:, gi, :nh, :], op=ALU.mult),
                               r=[xn[si].b, gt.k(gi)], w=[o.b])
                    else:
                        P.dve(lambda e: e.tensor_tensor(out=v3(xg[si][:, :W]), in0=v3(xn[si][:, :W]), in1=gt[:, gi, :nh, :], op=ALU.mult),
                               r=[xn[si].b, gt.k(gi)], w=[xg[si].b])
                        P.dve(lambda e: e.tensor_tensor(out=v3(r1[si][:, :W]), in0=v3(xg[si][:, :W]),
                                                        in1=cos2[:, lat_tile, :].unsqueeze(1).to_broadcast([128, nh, 64]), op=ALU.mult),
                              r=[xg[si].b, cos2.b], w=[r1[si].b])
                        v4 = lambda ap: ap.rearrange("p (h two d) -> p h two d", two=2, d=32)
                        P.dve(lambda e: e.tensor_tensor(out=v4(r2[si][:, :W]), in0=v4(xg[si][:, :W])[:, :, ::-1, :],
                                                         in1=sin2s[:, lat_tile, :].rearrange("p (two d) -> p two d", two=2).unsqueeze(1).to_broadcast([128, nh, 2, 32]),
                                                         op=ALU.mult),
                               r=[xg[si].b, sin2s.b], w=[r2[si].b])
                        P.dve(lambda e: e.tensor_tensor(out=o[:, :W], in0=r1[si][:, :W], in1=r2[si][:, :W], op=ALU.add),
                              r=[r1[si].b, r2[si].b], w=[o.b])
                    return o

                def transposes(src, nchunk, dst_ap, dst_tok):
                    pt = pst[cnt["t"] % 2]
                    use_act = cnt["t"] % 2 == 0
                    cnt["t"] += 1
                    ptv = pt[:].bitcast(BF16)
                    for k in range(nchunk):
                        P.pe(lambda e, k=k: e.transpose(ptv[:, k * 128:(k + 1) * 128], src[:, k * 128:(k + 1) * 128], ident_b[:]),
                             r=[src.b, ident_b.b], w=[pt.b])
                    inv = ptv[:, 0:nchunk * 128].rearrange("p (c t) -> p c t", c=nchunk)
                    if use_act:
                        P.act(lambda e: e.activation(out=dst_ap, in_=inv, func=AF.Copy), r=[pt.b], w=[dst_tok])
                    else:
                        P.dve(lambda e: e.tensor_copy(out=dst_ap, in_=inv), r=[pt.b], w=[dst_tok])

                for bi, (t0, N, isctx) in enumerate(BLOCKS):
                    h = hb[bi % 2]
                    a = aT[bi % 2]
                    mi = 1 if isctx else 0
                    P.dma("sp", h[:, :, :N], hT_d.ap[:, :, t0:t0 + N].rearrange("c p t -> p c t"),
                          r=[hT_d.k(t) for t in tiles_of(t0, N)], w=[h.b])
                    P.act(lambda e, h=h, N=N: e.activation(out=sq[:, :, :N], in_=h[:, :, :N], func=AF.Square), r=[h.b], w=[sq.b])
                    for c in range(8):
                        P.pe(lambda e, c=c, N=N: e.matmul(ps[0][:, :N], ones_b[:], sq[:, c, :N], start=(c == 0), stop=(c == 7)),
                             r=[sq.b, ones_b.b], w=[ps[0].b])
                    P.act(lambda e, N=N: e.activation(out=rt[:, :N], in_=ps[0][:, :N], func=AF.Sqrt, scale=1.0 / D, bias=EPS), r=[ps[0].b], w=[rt.b])
                    P.dve(lambda e, N=N: e.reciprocal(out=rstd[:, :N], in_=rt[:, :N]), r=[rt.b], w=[rstd.b])
                    for c in range(8):
                        tp = tmp[c % 2]
                        P.dve(lambda e, c=c, tp=tp, h=h, N=N, mi=mi: e.scalar_tensor_tensor(out=tp[:, :N], in0=h[:, c, :N], scalar=gm1[:, l, c, mi:mi + 1],
                                                                                       in1=rstd[:, :N], op0=ALU.mult, op1=ALU.mult),
                              r=[h.b, gm1.b, rstd.b], w=[tp.b])
                        P.act(lambda e, c=c, tp=tp, a=a, N=N, mi=mi: e.activation(out=a[:, c, :N], in_=tp[:, :N], func=AF.Identity,
                                                                             bias=modv[:, l, c, mi:mi + 1], scale=1.0),
                              r=[tp.b, modv.b], w=[a.k(c)])
                    P.dma("sp", aT_d.ap[:, :, t0:t0 + N].rearrange("c p t -> p c t"), a[:, :, :N],
                          r=[a.k(c) for c in range(8)], w=[aT_d.k(bi)])
                    qb, kb, wqb, wkb, sub = qdT_b[bi % 2], kdT_b[bi % 2], wqT_b[bi % 2], wkT_b[bi % 2], suT_b[bi % 2]
                    v3 = lambda ap: ap.rearrange("p (h d) -> p h d", d=64)
                    v4 = lambda ap: ap.rearrange("p (h two d) -> p h two d", two=2, d=32)
                    specs = [("dq", 8, 0), ("dk", 8, 1), ("wq", 8, 2), ("wkv", 2, 3)]
                    for j in range(N // 128):
                        tok = t0 // 128 + j
                        lat_tile = None if isctx else tok - 2
                        G = {}
                        for gi_, (kind, col0, W) in enumerate(GROUPS):
                            pb = ps[1 + gi_]
                            G[kind] = pb
                            for c in range(8):
                                P.pe(lambda e, pb=pb, c=c, j=j, a=a, col0=col0, W=W: e.matmul(pb[:, :W], a[:, c, j * 128:(j + 1) * 128], win[:, c, col0:col0 + W],
                                                                                         start=(c == 0), stop=(c == 7)),
                                     r=[a.k(c), win.k(c)], w=[pb.b])
                        vt = vtm[cnt["v"] % 2]
                        wt = wvtm[cnt["v"] % 2]
                        cnt["v"] += 1
                        pbv, pbw = G["dv"], G["wkv"]
                        P.act(lambda e: e.activation(out=vt[:], in_=pbv[:], func=AF.Copy), r=[pbv.b], w=[vt.b])
                        P.act(lambda e: e.activation(out=wt[:], in_=pbw[:, 128:256], func=AF.Copy), r=[pbw.b], w=[wt.b])
                        if isctx:
                            P.dma("sp", vdC_d.ap[tok * 128:(tok + 1) * 128, :], vt[:], r=[vt.b], w=[vdC_d.b])
                            P.dma("sp", wvC_d.ap[tok * 128:(tok + 1) * 128, :], wt[:], r=[wt.b], w=[wvC_d.b])
                        else:
                            P.dma("sp", vdL_d.ap[(tok - 2) * 128:(tok - 1) * 128, :], vt[:], r=[vt.b], w=[vdL_d.b])
                            P.dma("sp", wvL_d.ap[(tok - 2) * 128:(tok - 1) * 128, :], wt[:], r=[wt.b], w=[wvL_d.b])
                        for si, (kind, nh, gi) in enumerate(specs):
                            pb, W = G[kind], nh * 64
                            P.act(lambda e: e.activation(out=sqx[si][:, :W], in_=pb[:, :W], func=AF.Square), r=[pb.b], w=[sqx[si].b])
                        for si, (kind, nh, gi) in enumerate(specs):
                            W = nh * 64
                            P.dve(lambda e: e.tensor_reduce(out=ss[si][:, :nh], in_=v3(sqx[si][:, :W]), axis=AX.X, op=ALU.add), r=[sqx[si].b], w=[ss[si].b])
                        for si, (kind, nh, gi) in enumerate(specs):
                            P.act(lambda e: e.activation(out=ss2[si][:, :nh], in_=ss[si][:, :nh], func=AF.Sqrt, scale=1.0 / 64, bias=EPS), r=[ss[si].b], w=[ss2[si].b])
                        for si, (kind, nh, gi) in enumerate(specs):
                            P.dve(lambda e: e.reciprocal(out=rs[si][:, :nh], in_=ss2[si][:, :nh]), r=[ss2[si].b], w=[rs[si].b])
                        for si, (kind, nh, gi) in enumerate(specs):
                            pb, W = G[kind], nh * 64
                            P.dve(lambda e: e.tensor_tensor(out=v3(xn[si][:, :W]), in0=v3(pb[:, :W]), in1=rs[si][:, :nh].unsqueeze(2).to_broadcast([128, nh, 64]), op=ALU.mult),
                                  r=[pb.b, rs[si].b], w=[xn[si].b])
                        if lat_tile is None:
                            for si, (kind, nh, gi) in enumerate(specs):
                                W = nh * 64
                                P.dve(lambda e: e.tensor_tensor(out=v3(qtm[si][:, :W]), in0=v3(xn[si][:, :W]), in1=gt[:, gi, :nh, :], op=ALU.mult),
                                      r=[xn[si].b, gt.k(gi)], w=[qtm[si].b])
                        else:
                            for si, (kind, nh, gi) in enumerate(specs):
                                W = nh * 64
                                P.dve(lambda e: e.tensor_tensor(out=v3(xg[si][:, :W]), in0=v3(xn[si][:, :W]), in1=gt[:, gi, :nh, :], op=ALU.mult),
                                      r=[xn[si].b, gt.k(gi)], w=[xg[si].b])
                            for si, (kind, nh, gi) in enumerate(specs):
                                W = nh * 64
                                P.dve(lambda e: e.tensor_tensor(out=v3(r1[si][:, :W]), in0=v3(xg[si][:, :W]),
                                                                in1=cos2[:, lat_tile, :].unsqueeze(1).to_broadcast([128, nh, 64]), op=ALU.mult),
                                      r=[xg[si].b, cos2.b], w=[r1[si].b])
                            for si, (kind, nh, gi) in enumerate(specs):
                                W = nh * 64
                                P.dve(lambda e: e.tensor_tensor(out=v4(r2[si][:, :W]), in0=v4(xg[si][:, :W])[:, :, ::-1, :],
                                                                in1=sin2s[:, lat_tile, :].rearrange("p (two d) -> p two d", two=2).unsqueeze(1).to_broadcast([128, nh, 2, 32]),
                                                                op=ALU.mult),
                                      r=[xg[si].b, sin2s.b], w=[r2[si].b])
                            for si, (kind, nh, gi) in enumerate(specs):
                                W = nh * 64
                                P.dve(lambda e: e.tensor_tensor(out=qtm[si][:, :W], in0=r1[si][:, :W], in1=r2[si][:, :W], op=ALU.add),
                                      r=[r1[si].b, r2[si].b], w=[qtm[si].b])
                        pv6 = ps[6][:].bitcast(BF16)
                        pv7 = ps[7][:].bitcast(BF16)
                        for k in range(4):
                            P.pe(lambda e: e.transpose(pv6[:, k * 128:(k + 1) * 128], qtm[0][:, k * 128:(k + 1) * 128], ident_b[:]), r=[qtm[0].b, ident_b.b], w=[ps[6].b])
                        for k in range(4):
                            P.pe(lambda e: e.transpose(pv6[:, (4 + k) * 128:(5 + k) * 128], qtm[1][:, k * 128:(k + 1) * 128], ident_b[:]), r=[qtm[1].b, ident_b.b], w=[ps[6].b])
                        for k in range(4):
                            P.pe(lambda e: e.transpose(pv7[:, k * 128:(k + 1) * 128], qtm[2][:, k * 128:(k + 1) * 128], ident_b[:]), r=[qtm[2].b, ident_b.b], w=[ps[7].b])
                        P.pe(lambda e: e.transpose(pv7[:, 512:640], qtm[3][:, 0:128], ident_b[:]), r=[qtm[3].b, ident_b.b], w=[ps[7].b])
                        P.act(lambda e: e.activation(out=qb[:, :, j * 128:(j + 1) * 128], in_=pv6[:, 0:512].rearrange("p (c t) -> p c t", c=4), func=AF.Copy), r=[ps[6].b], w=[qb.k(j)])
                        P.act(lambda e: e.activation(out=kb[:, :, j * 128:(j + 1) * 128], in_=pv6[:, 512:1024].rearrange("p (c t) -> p c t", c=4), func=AF.Copy), r=[ps[6].b], w=[kb.k(j)])
                        P.dve(lambda e: e.tensor_copy(out=wqb[:, :, j * 128:(j + 1) * 128], in_=pv7[:, 0:512].rearrange("p (c t) -> p c t", c=4)), r=[ps[7].b], w=[wqb.k(j)])
                        P.dve(lambda e: e.tensor_copy(out=wkb[:, j * 128:(j + 1) * 128], in_=pv7[:, 512:640]), r=[ps[7].b], w=[wkb.k(j)])
                    for cc in range(4):
                        pb = ps[4 + cc % 2]
                        for c in range(8):
                            P.pe(lambda e, pb=pb, c=c, cc=cc, a=a, N=N: e.matmul(pb[:, :N], win[:, c, 1536 + cc * 128:1536 + (cc + 1) * 128], a[:, c, :N],
                                                                            start=(c == 0), stop=(c == 7)),
                                 r=[a.k(c), win.k(c)], w=[pb.b])
                        P.act(lambda e, pb=pb, cc=cc, sub=sub, N=N: e.activation(out=sub[:, cc, :N], in_=pb[:, :N], func=AF.Copy), r=[pb.b], w=[sub.k(cc)])
                    nj = N // 128
                    dview = lambda dt_: dt_.ap[:, :, t0:t0 + N].rearrange("c p t -> p c t")
                    P.dma("sp", dview(qdT_d), qb[:, :, :N], r=[qb.k(j) for j in range(nj)], w=[qdT_d.k(bi)])
                    l0 = t0 - NCTX
                    if isctx:
                        P.dma("sp", kdC_d.ap.rearrange("c p t -> p c t"), kb[:, :, :N], r=[kb.k(j) for j in range(nj)], w=[kdC_d.b])
                    else:
                        P.dma("sp", kdL_d.ap[:, l0:l0 + N].rearrange("(c p) t -> p c t", p=128), kb[:, :, :N], r=[kb.k(j) for j in range(nj)], w=[kdL_d.b])
                    P.dma("sp", dview(wqT_d), wqb[:, :, :N], r=[wqb.k(j) for j in range(nj)], w=[wqT_d.k(bi)])
                    if isctx:
                        P.dma("sp", wkC_d.ap, wkb[:, :N], r=[wkb.k(j) for j in range(nj)], w=[wkC_d.b])
                        P.dma("sp", suC_d.ap.rearrange("c p t -> p c t"), sub[:, :, :N], r=[sub.k(cc) for cc in range(4)], w=[suC_d.b])
                    else:
                        P.dma("sp", wkL_d.ap[:, l0:l0 + N], wkb[:, :N], r=[wkb.k(j) for j in range(nj)], w=[wkL_d.b])
                        P.dma("sp", suL_d.ap[:, l0:l0 + N].rearrange("(c p) t -> p c t", p=128), sub[:, :, :N], r=[sub.k(cc) for cc in range(4)], w=[suL_d.b])
                P.barrier()
                for a_, g_ in ((kdL_d, kdG_d), (vdL_d, vdG_d), (wkL_d, wkG_d), (wvL_d, wvG_d), (suL_d, suG_d)):
                    P.allgather(a_, g_, r=[a_.b], w=[g_.b])

        def phase_D(l):
            need_ctx = l < DEPTH - 1
            lam_init = 0.8 - 0.6 * math.exp(-0.3 * l)
            with ExitStack() as ph:
                kT = sb("kT", [128, 4, GNT], BF16, ph)
                V = sb("V", [128, GTILE, 512], BF16, ph)
                for c in range(4):
                    P.dma("sp", kT[:, c, 0:NCTX], kdC_d.ap[c], r=[kdC_d.b], w=[kT.k(c)])
                    for rk in range(2):
                        P.dma("sp", kT[:, c, NCTX + rk * NLAT:NCTX + (rk + 1) * NLAT], kdG_d.ap[rk * 512 + c * 128:rk * 512 + (c + 1) * 128, :],
                              r=[kdG_d.b], w=[kT.k(c)])
                P.dma("sp", V[:, 0:2, :], vdC_d.ap.rearrange("(t p) f -> p t f", p=128), r=[vdC_d.b], w=[V.k(0)])
                vv = vdG_d.ap.rearrange("(t p) f -> p t f", p=128)
                for q4 in range(0, 32, 8):
                    P.dma("sp", V[:, 2 + q4:2 + q4 + 8, :], vv[:, q4:q4 + 8, :], r=[vdG_d.b], w=[V.k(0)])
                Vtok = [V.k(0) for q4 in range(0, GTILE, 9)]
                lq = sb("lamv", [128, 4, 64], F32, ph)
                for i, nm in enumerate(["diff_lam_q1", "diff_lam_k1", "diff_lam_q2", "diff_lam_k2"]):
                    P.dma("sp", lq[:, i, :], IN[nm][l].partition_broadcast(128), w=[lq.b])
                lprod = sb("lprod", [128, 2, 64], F32, ph)
                lsum = sb("lsum", [128, 2], F32, ph)
                lexp = sb("lexp", [128, 2], F32, ph)
                nlam = sb("nlam", [128, 1], F32, ph)
                gout = sb("gout", [128, 1], F32, ph)
                P.dve(lambda e: e.tensor_tensor(out=lprod[:], in0=lq[:, 0::2, :], in1=lq[:, 1::2, :], op=ALU.mult), r=[lq.b], w=[lprod.b])
                P.dve(lambda e: e.tensor_reduce(out=lsum[:], in_=lprod[:], axis=AX.X, op=ALU.add), r=[lprod.b], w=[lsum.b])
                P.act(lambda e: e.activation(out=lexp[:], in_=lsum[:], func=AF.Exp), r=[lsum.b], w=[lexp.b])
                P.dve(lambda e: e.tensor_tensor(out=nlam[:], in0=lexp[:, 1:2], in1=lexp[:, 0:1], op=ALU.subtract), r=[lexp.b], w=[nlam.b])
                P.dve(lambda e: e.tensor_scalar(out=nlam[:], in0=nlam[:], scalar1=-lam_init, scalar2=None, op0=ALU.add), r=[nlam.b], w=[nlam.b])
                P.dma("sp", gout[:], IN["diff_out_norm_g"][l].rearrange("(p o) -> p o", o=1), w=[gout.b])
                P.dve(lambda e: e.tensor_scalar(out=gout[:], in0=gout[:], scalar1=1.0 - lam_init, scalar2=None, op0=ALU.mult), r=[gout.b], w=[gout.b])
                qTb = [sb("qTb%d" % i, [128, 4, 512], BF16, ph) for i in range(2)]
                ydb_ = [sb("ydb%d" % i, [128, 4, 512], BF16, ph) for i in range(2)]
                pT = [sb("pT%d" % i, [128, 2, 512], BF16, ph) for i in range(2)]
                rz = [sb("rz%d" % i, [128, 512], F32, ph) for i in range(2)]
                oz = sb("oz", [128, 4, 512], F32, ph)
                t1 = sb("t1", [128, 512], F32, ph)
                t2 = sb("t2", [128, 512], F32, ph)
                o_ = sb("o_", [128, 512], F32, ph)
                osq = sb("osq", [128, 512], BF16, ph)
                ort = sb("ort", [128, 512], F32, ph)
                orst = sb("orst", [128, 512], F32, ph)
                gi = [0]
                for bi, (t0, N, isctx) in enumerate(BLOCKS):
                    if isctx and not need_ctx:
                        continue
                    if dlimit is not None and bi not in dlimit:
                        continue
                    ktiles = [0, 1] if isctx else list(range(GTILE))
                    q = qTb[bi % 2]
                    ydb = ydb_[bi % 2]
                    P.dma("sp", q[:, :, :N], qdT_d.ap[:, :, t0:t0 + N].rearrange("c p t -> p c t"), r=[qdT_d.k(bi)], w=[q.b])
                    for h in range(4):
                        half = h % 2
                        items = [(kt, m) for kt in ktiles for m in range(2)]
                        base = gi[0]
                        gi[0] += len(items)

                        SB = [(ps[0], ps[1]), (ps[2], ps[3])]
                        npair = len(ktiles)
                        pbase = gi[0]
                        gi[0] += npair

                        def qk(i):
                            kt = ktiles[i]
                            for m in range(2):
                                c = 2 * m + h // 2
                                sbank = SB[(pbase + i) % 2][m]
                                P.pe(lambda e: e.matmul(sbank[:, :N], kT[half * 64:(half + 1) * 64, c, kt * 128:(kt + 1) * 128],
                                                        q[half * 64:(half + 1) * 64, c, :N], start=True, stop=True),
                                     r=[kT.k(c), q.b], w=[SB[(pbase + i) % 2][0].b])

                        def pv(i):
                            kt = ktiles[i]
                            sb2 = SB[(pbase + i) % 2]
                            p_ = pT[(pbase + i) % 2]
                            bk = 2 * ((pbase + i) % 2)
                            P.act(lambda e: e.activation(out=p_[:, :, :N], in_=psbig[:, bk:bk + 2, :N], func=AF.Exp), r=[sb2[0].b], w=[p_.b])
                            first = (i == 0)
                            last = (i == npair - 1)
                            for m in range(2):
                                P.pe(lambda e: e.matmul(ps[4 + m][:, :N], V[:, kt, h * 128:(h + 1) * 128], p_[:, m, :N], start=first, stop=last),
                                     r=[Vtok[kt // 9], p_.b], w=[ps[4].b])
                                P.pe(lambda e: e.matmul(ps[6 + m][:, :N], ones_b[:], p_[:, m, :N], start=first, stop=last),
                                     r=[ones_b.b, p_.b], w=[ps[4].b])

                        qk(0)
                        for i in range(npair):
                            if i + 1 < npair:
                                qk(i + 1)
                            pv(i)
                        P.dve(lambda e: e.tensor_copy(out=oz[:, :, :N], in_=psbig[:, 4:8, :N]), r=[ps[4].b], w=[oz.b])
                        P.dve(lambda e: e.reciprocal(out=rz[0][:, :N], in_=oz[:, 2, :N]), r=[oz.b], w=[rz[0].b])
                        P.dve(lambda e: e.reciprocal(out=rz[1][:, :N], in_=oz[:, 3, :N]), r=[oz.b], w=[rz[1].b])
                        P.dve(lambda e: e.tensor_tensor(out=t1[:, :N], in0=oz[:, 0, :N], in1=rz[0][:, :N], op=ALU.mult), r=[oz.b, rz[0].b], w=[t1.b])
                        P.dve(lambda e: e.tensor_tensor(out=t2[:, :N], in0=oz[:, 1, :N], in1=rz[1][:, :N], op=ALU.mult), r=[oz.b, rz[1].b], w=[t2.b])
                        P.dve(lambda e: e.scalar_tensor_tensor(out=o_[:, :N], in0=t2[:, :N], scalar=nlam[:, 0:1], in1=t1[:, :N], op0=ALU.mult, op1=ALU.add),
                              r=[t1.b, t2.b, nlam.b], w=[o_.b])
                        P.act(lambda e: e.activation(out=osq[:, :N], in_=o_[:, :N], func=AF.Square), r=[o_.b], w=[osq.b])
                        nb = SB[gi[0] % 2]
                        P.pe(lambda e: e.matmul(nb[1][:, :N], ones_b[:], osq[:, :N], start=True, stop=True), r=[ones_b.b, osq.b], w=[nb[0].b])
                        P.act(lambda e: e.activation(out=ort[:, :N], in_=nb[1][:, :N], func=AF.Sqrt, scale=1.0 / 128, bias=EPS), r=[nb[0].b], w=[ort.b])
                        gi[0] += 1
                        P.dve(lambda e: e.reciprocal(out=orst[:, :N], in_=ort[:, :N]), r=[ort.b], w=[orst.b])
                        P.dve(lambda e, h=h: e.scalar_tensor_tensor(out=ydb[:, h, :N], in0=o_[:, :N], scalar=gout[:, 0:1], in1=orst[:, :N], op0=ALU.mult, op1=ALU.mult),
                              r=[o_.b, gout.b, orst.b], w=[ydb.k(h)])
                    P.dma("sp", ydT_d.ap[:, :, t0:t0 + N].rearrange("c p t -> p c t"), ydb[:, :, :N], r=[ydb.k(h) for h in range(4)], w=[ydT_d.k(bi)])
                P.barrier()

        def phase_W(l):
            need_ctx = l < DEPTH - 1
            with ExitStack() as ph:
                ET = LT + 4
                wkT = sb("wkT", [128, ET * 128], BF16, ph)
                wv = sb("wv", [128, ET, 128], BF16, ph)
                P.dma("sp", wkT[:, 0:NCTX], wkC_d.ap, r=[wkC_d.b], w=[wkT.b])
                P.dma("sp", wkT[:, 2 * 128:3 * 128], wkG_d.ap[0:128, NLAT - 128:NLAT], r=[wkG_d.b], w=[wkT.b])
                P.dma("sp", wkT[:, 3 * 128:(3 + LT) * 128], wkL_d.ap, r=[wkL_d.b], w=[wkT.b])
                P.dma("sp", wkT[:, (3 + LT) * 128:(4 + LT) * 128], wkG_d.ap[128:256, 0:128], r=[wkG_d.b], w=[wkT.b])
                P.dma("sp", wv[:, 0:2, :], wvC_d.ap.rearrange("(t p) f -> p t f", p=128), r=[wvC_d.b], w=[wv.b])
                P.dma("sp", wv[:, 2, :], wvG_d.ap[NLAT - 128:NLAT, :], r=[wvG_d.b], w=[wv.b])
                P.dma("sp", wv[:, 3:3 + LT, :], wvL_d.ap.rearrange("(t p) f -> p t f", p=128), r=[wvL_d.b], w=[wv.b])
                P.dma("sp", wv[:, 3 + LT, :], wvG_d.ap[NLAT:NLAT + 128, :], r=[wvG_d.b], w=[wv.b])
                mk = sb("wmask", [128, 4, 128], BF16, ph)
                mkf = sb("wmaskf", [128, 2, 128], F32, ph)
                P.dma("pool", mk[:, 0, :], IN["mask_prev"], w=[mk.b])
                P.dma("pool", mk[:, 1, :], IN["mask_next"], w=[mk.b])
                P.dma("sp", mkf[:, 0, :], IN["mask_prev"], w=[mkf.b])
                P.dma("sp", mkf[:, 1, :], IN["mask_next"], w=[mkf.b])
                P.dve(lambda e: e.tensor_scalar(out=mk[:, 2, :], in0=mkf[:, 0, :], scalar1=flg[:, 1:2], scalar2=None, op0=ALU.mult), r=[mkf.b, flg.b, mk.b], w=[mk.b])
                P.dve(lambda e: e.tensor_scalar(out=mk[:, 3, :], in0=mkf[:, 1, :], scalar1=flg[:, 0:1], scalar2=None, op0=ALU.mult), r=[mkf.b, flg.b, mk.b], w=[mk.b])
                esink = sb("esink", [64, 8], F32, ph)
                P.dma("sp", esink[:], IN["win_sink"][l].partition_broadcast(64), w=[esink.b])
                P.act(lambda e: e.activation(out=esink[:], in_=esink[:], func=AF.Exp), r=[esink.b], w=[esink.b])
                wqb_ = [sb("wqb%d" % i, [128, 4, 512], BF16, ph) for i in range(2)]
                ywb_ = [sb("ywb%d" % i, [64, 8, 512], BF16, ph) for i in range(2)]
                pw = [sb("pw%d" % i, [128, 5, 512], BF16, ph) for i in range(2)]
                zs = sb("zs", [64, 512], F32, ph)
                rzw = sb("rzw", [64, 512], F32, ph)
                u = [0]
                for bi, (t0, N, isctx) in enumerate(BLOCKS):
                    if isctx and not need_ctx:
                        continue
                    wqb = wqb_[bi % 2]
                    ywb = ywb_[bi % 2]
                    P.dma("sp", wqb[:, :, :N], wqT_d.ap[:, :, t0:t0 + N].rearrange("g p t -> p g t"), r=[wqT_d.k(bi)], w=[wqb.b])
                    for j in range(N // 128):
                        Tt = t0 // 128 + j
                        if isctx:
                            keys = [(0, None), (1, None)]
                        else:
                            lt = Tt - 2
                            keys = [(0, None), (1, None), (2 + lt, 2 if lt == 0 else 0), (3 + lt, None), (4 + lt, 3 if lt == LT - 1 else 1)]
                        nk = len(keys)
                        for kv in range(2):
                            p_ = pw[u[0] % 2]
                            u[0] += 1
                            for idx, (kt, mm) in enumerate(keys):
                                P.pe(lambda e: e.matmul(ps[idx][:, :], wkT[kv * 64:(kv + 1) * 64, kt * 128:(kt + 1) * 128],
                                                        wqb[kv * 64:(kv + 1) * 64, :, j * 128:(j + 1) * 128], start=True, stop=True),
                                     r=[wkT.b, wqb.b], w=[ps[0].b])
                            for idx, (kt, mm) in enumerate(keys):
                                P.act(lambda e: e.activation(out=p_[:, idx, :], in_=ps[idx][:, :], func=AF.Exp), r=[ps[0].b], w=[p_.b])
                            for idx, (kt, mm) in enumerate(keys):
                                if mm is not None:
                                    P.pool(lambda e: e.tensor_tensor(out=p_[:, idx, :].rearrange("p (g q) -> p g q", g=4),
                                                                     in0=p_[:, idx, :].rearrange("p (g q) -> p g q", g=4),
                                                                     in1=mk[:, mm, :].unsqueeze(1).to_broadcast([128, 4, 128]), op=ALU.mult),
                                           r=[p_.b, mk.b], w=[p_.b])
                            for idx, (kt, mm) in enumerate(keys):
                                P.pe(lambda e: e.matmul(ps[5][0:64, :], wv[:, kt, kv * 64:(kv + 1) * 64], p_[:, idx, :], start=(idx == 0), stop=(idx == nk - 1)),
                                     r=[wv.b, p_.b], w=[ps[5].b])
                                P.pe(lambda e: e.matmul(ps[6][0:64, :], ones_b[:, 0:64], p_[:, idx, :], start=(idx == 0), stop=(idx == nk - 1)),
                                     r=[ones_b.b, p_.b], w=[ps[5].b])
                            P.dve(lambda e: e.tensor_tensor(out=zs[:].rearrange("p (g q) -> p g q", g=4), in0=ps[6][0:64, :].rearrange("p (g q) -> p g q", g=4),
                                                            in1=esink[:, kv * 4:(kv + 1) * 4].unsqueeze(2).to_broadcast([64, 4, 128]), op=ALU.add),
                                  r=[ps[5].b, esink.b], w=[zs.b])
                            P.dve(lambda e: e.reciprocal(out=rzw[:], in_=zs[:]), r=[zs.b], w=[rzw.b])
                            P.dve(lambda e: e.tensor_tensor(out=ywb[:, kv * 4:(kv + 1) * 4, j * 128:(j + 1) * 128],
                                                            in0=ps[5][0:64, :].rearrange("p (g q) -> p g q", g=4),
                                                            in1=rzw[:].rearrange("p (g q) -> p g q", g=4), op=ALU.mult),
                                  r=[ps[5].b, rzw.b], w=[ywb.k((j, kv))])
                    P.dma("sp", ywT_d.ap[:, :, t0:t0 + N].rearrange("h d t -> d h t"), ywb[:, :, :N],
                          r=[ywb.k((j, kv)) for j in range(N // 128) for kv in range(2)], w=[ywT_d.k(bi)])
                P.barrier()

        I32 = mybir.dt.int32
        C1 = 6.28125
        C2 = 2 * PI - C1

        NSC = 8

        def phase_S(l):
            need_ctx = l < DEPTH - 1
            with ExitStack() as ph:
                tok = Buf("s5prm")
                def sm(name, shape=(128, 2, NSC), dt=F32):
                    return sb(name, list(shape), dt, ph)
                lre, lim, ldt, dtt, th, rl, rr, are, aim = [sm(n) for n in ("lre", "lim", "ldt", "dtt", "th", "rl", "rr", "are", "aim")]
                den, nre, fre, fim, kre, kim, u1, u2, thj = [sm(n) for n in ("den", "nre", "fre", "fim", "kre", "kim", "u1", "u2", "thj")]
                P.dma("sp", lre[:], IN["s5_lambda_re"][l].rearrange("d (sc g2) p -> (g2 p) d sc", g2=2), w=[tok], allow_slow_non_contiguous=True)
                P.dma("sp", lim[:], IN["s5_lambda_im"][l].rearrange("d (sc g2) p -> (g2 p) d sc", g2=2), w=[tok], allow_slow_non_contiguous=True)
                ldv = IN["s5_log_dt"][l].rearrange("d (sc g2) -> g2 d sc", g2=2)
                for g2 in range(2):
                    P.dma("sp", ldt[g2 * 64:(g2 + 1) * 64, :, :], ldv[g2].partition_broadcast(64), w=[tok], allow_slow_non_contiguous=True)
                tau = sm("tau", (128, 2, J))
                gmk = sm("gmk", (128, 2, 8))
                P.dma("sp", tau[:], IN["tau"], w=[tok])
                P.dma("sp", gmk[:, 0, :], IN["gmask"], w=[tok])
                P.dve(lambda e: e.tensor_scalar(out=gmk[:, 1, :], in0=gmk[:, 0, :], scalar1=-1.0, scalar2=None, op0=ALU.mult), r=[tok], w=[tok])
                NE = NSC * J
                NP = 2 * NSC
                cosT = sb("cosT", [128, 2, NSC, J], F32, ph)
                sinT = sb("sinT", [128, 2, NSC, J], F32, ph)
                Rm = sb("Rm", [128, 2, NSC, J], F32, ph)
                CS = sb("CS", [128, 2, 2, NSC * J], BF16, ph)
                CS2 = sb("CS2", [128, 2, 2, NSC * J], BF16, ph)
                BF = [sb("BblkF%d" % i, [128, 4, NSC, 128], BF16, ph) for i in range(2)]
                Cblk = sb("Cblk", [128, 4, NSC, 128], BF16, ph)
                dsk = sb("dsk", [128, 2], F32, ph)
                diagF = [sb("diagF%d" % i, [128, 2, 128], BF16, ph) for i in range(2)]
                pp = ExitStack()
                Bblk = sb("Bblk", [128, 4, NSC, 128], BF16, pp)
                diagD = sb("diagD", [128, 2, 128], F32, pp)
                rv = sb("rv", [128, NE], F32, pp)
                rki = sb("rki", [128, NE], I32, pp)
                rkf = sb("rkf", [128, NE], F32, pp)
                rm = sb("rm", [128, NE], F32, pp)
                ang = sb("ang", [128, NE], F32, pp)

                def sin_of(dst, src, n, add, rt, wt):
                    V_ = rv[:, :n]; KI = rki[:, :n]; KF = rkf[:, :n]; M_ = rm[:, :n]
                    tk = rv.b
                    P.dve(lambda e: e.tensor_scalar(out=V_, in0=src, scalar1=add, scalar2=1.0 / (2 * PI), op0=ALU.add, op1=ALU.mult), r=rt, w=[tk])
                    P.dve(lambda e: e.tensor_copy(out=KI, in_=V_), r=[tk], w=[tk])
                    P.dve(lambda e: e.tensor_copy(out=KF, in_=KI), r=[tk], w=[tk])
                    P.dve(lambda e: e.scalar_tensor_tensor(out=V_, in0=KF, scalar=-C1, in1=src, op0=ALU.mult, op1=ALU.add), r=[tk] + rt, w=[tk])
                    P.dve(lambda e: e.scalar_tensor_tensor(out=V_, in0=KF, scalar=-C2, in1=V_, op0=ALU.mult, op1=ALU.add), r=[tk], w=[tk])
                    if add != 0.0:
                        P.dve(lambda e: e.tensor_scalar(out=V_, in0=V_, scalar1=add, scalar2=None, op0=ALU.add), r=[tk], w=[tk])
                    P.dve(lambda e: e.tensor_scalar(out=M_, in0=V_, scalar1=PI, scalar2=-2 * PI, op0=ALU.is_gt, op1=ALU.mult), r=[tk], w=[tk])
                    P.dve(lambda e: e.tensor_tensor(out=V_, in0=V_, in1=M_, op=ALU.add), r=[tk], w=[tk])
                    P.dve(lambda e: e.tensor_scalar(out=M_, in0=V_, scalar1=-PI, scalar2=2 * PI, op0=ALU.is_lt, op1=ALU.mult), r=[tk], w=[tk])
                    P.dve(lambda e: e.tensor_tensor(out=V_, in0=V_, in1=M_, op=ALU.add), r=[tk], w=[tk])
                    P.act(lambda e: e.activation(out=dst, in_=V_, func=AF.Sin), r=[tk], w=wt)

                fl = lambda t_: t_[:].rearrange("p d s -> p (d s)")
                TT = lambda o_, a_, b_, op: P.dve(lambda e: e.tensor_tensor(out=fl(o_), in0=fl(a_), in1=fl(b_), op=op), r=[tok], w=[tok])
                P.act(lambda e: e.activation(out=fl(dtt), in_=fl(ldt), func=AF.Exp), r=[tok], w=[tok])
                TT(th, lim, dtt, ALU.mult)
                TT(rl, lre, dtt, ALU.mult)
                P.act(lambda e: e.activation(out=fl(rr), in_=fl(rl), func=AF.Exp), r=[tok], w=[tok])
                sin_of(fl(u1), fl(th), NP, 0.0, [tok], [tok])
                sin_of(fl(u2), fl(th), NP, PI / 2, [tok], [tok])
                TT(aim, rr, u1, ALU.mult)
                TT(are, rr, u2, ALU.mult)
                P.dve(lambda e: e.tensor_scalar(out=fl(thj), in0=fl(th), scalar1=float(J), scalar2=None, op0=ALU.mult), r=[tok], w=[tok])
                sin_of(fl(u1), fl(thj), NP, 0.0, [tok], [tok])
                sin_of(fl(u2), fl(thj), NP, PI / 2, [tok], [tok])
                TT(kim, rr, u1, ALU.mult)
                TT(kre, rr, u2, ALU.mult)
                TT(den, lre, lre, ALU.mult)
                TT(u1, lim, lim, ALU.mult)
                TT(den, den, u1, ALU.add)
                P.dve(lambda e: e.reciprocal(out=fl(den), in_=fl(den)), r=[tok], w=[tok])
                P.dve(lambda e: e.tensor_scalar(out=fl(nre), in0=fl(are), scalar1=-1.0, scalar2=None, op0=ALU.add), r=[tok], w=[tok])
                TT(u1, nre, lre, ALU.mult)
                TT(u2, aim, lim, ALU.mult)
                TT(fre, u1, u2, ALU.add)
                TT(fre, fre, den, ALU.mult)
                TT(u1, aim, lre, ALU.mult)
                TT(u2, nre, lim, ALU.mult)
                TT(fim, u1, u2, ALU.subtract)
                TT(fim, fim, den, ALU.mult)
                for d in range(2):
                    P.dve(lambda e: e.tensor_tensor(out=ang[:].rearrange("p (s t) -> p s t", s=NSC), in0=th[:, d, :].unsqueeze(2).to_broadcast([128, NSC, J]),
                                                    in1=tau[:, d, :].unsqueeze(1).to_broadcast([128, NSC, J]), op=ALU.mult), r=[tok, rv.b], w=[ang.b])
                    sin_of(sinT[:, d, :, :].rearrange("p s t -> p (s t)"), ang[:], NE, 0.0, [ang.b], [sinT.k(d)])
                    sin_of(cosT[:, d, :, :].rearrange("p s t -> p (s t)"), ang[:], NE, PI / 2, [ang.b], [cosT.k(d)])
                    P.dve(lambda e: e.tensor_copy(out=Rm[:, d, :, :], in_=rr[:, d, :].unsqueeze(2).to_broadcast([128, NSC, J])), r=[tok], w=[Rm.k(d)])
                    cflat = cosT[:, d, :, :].rearrange("p s t -> p (s t)")
                    sflat = sinT[:, d, :, :].rearrange("p s t -> p (s t)")
                    P.act(lambda e: e.activation(out=CS[:, d, 0, :], in_=cflat, func=AF.Copy), r=[cosT.k(d)], w=[CS.k(d)])
                    P.act(lambda e: e.activation(out=CS[:, d, 1, :], in_=sflat, func=AF.Copy), r=[sinT.k(d)], w=[CS.k(d)])
                    P.act(lambda e: e.activation(out=CS2[:, d, 0, :], in_=cflat, func=AF.Copy), r=[cosT.k(d)], w=[CS2.k(d)])
                    P.act(lambda e: e.activation(out=CS2[:, d, 1, :], in_=sflat, func=AF.Identity, scale=-1.0), r=[sinT.k(d)], w=[CS2.k(d)])
                    pos = 0 if d == 0 else J - 1
                    P.dve(lambda e: e.memset(Rm[:, d, :, pos:pos + 1], 0.0), r=[], w=[Rm.k(d)])
                XY = sb("XY", [128, 4, NSC, 128], BF16, pp)
                bnat = [sb("bnat%d" % i, [128, 2, NSC, 16], F32, pp) for i in range(2)]
                bb = [sb("bb%d" % i, [128, 2, NSC, 16], F32, pp) for i in range(2)]
                bt = [sb("bt%d" % i, [128, 2, NSC, 16], F32, pp) for i in range(2)]
                P.dma("sp", bnat[0][:], IN["s5_b_re"][l].rearrange("d (sc g2) p h -> (g2 p) d sc h", g2=2), w=[bnat[0].b])
                P.dma("sp", bnat[1][:], IN["s5_b_im"][l].rearrange("d (sc g2) p h -> (g2 p) d sc h", g2=2), w=[bnat[1].b])
                bc = lambda f_: f_[:].unsqueeze(3).to_broadcast([128, 2, NSC, 16])
                P.dve(lambda e: e.tensor_tensor(out=bt[0][:], in0=bnat[0][:], in1=bc(fre), op=ALU.mult), r=[bnat[0].b, tok], w=[bt[0].b])
                P.dve(lambda e: e.tensor_tensor(out=bt[1][:], in0=bnat[1][:], in1=bc(fim), op=ALU.mult), r=[bnat[1].b, tok], w=[bt[1].b])
                P.dve(lambda e: e.tensor_tensor(out=bb[0][:], in0=bt[0][:], in1=bt[1][:], op=ALU.subtract), r=[bt[0].b, bt[1].b], w=[bb[0].b])
                P.dve(lambda e: e.tensor_tensor(out=bt[0][:], in0=bnat[1][:], in1=bc(fre), op=ALU.mult), r=[bnat[1].b, tok, bb[0].b], w=[bt[0].b])
                P.dve(lambda e: e.tensor_tensor(out=bt[1][:], in0=bnat[0][:], in1=bc(fim), op=ALU.mult), r=[bnat[0].b, tok, bb[0].b], w=[bt[1].b])
                P.dve(lambda e: e.tensor_tensor(out=bb[1][:], in0=bt[0][:], in1=bt[1][:], op=ALU.add), r=[bt[0].b, bt[1].b], w=[bb[1].b])
                P.pool(lambda e: e.memset(XY[:], 0.0), w=[XY.b])
                for ri in range(2):
                    for j in range(4):
                        for g2 in range(2):
                            P.dve(lambda e: e.tensor_copy(out=XY[g2 * 64:(g2 + 1) * 64, ri::2, j::4, (2 * j + g2) * 16:(2 * j + g2 + 1) * 16],
                                                          in_=bb[ri][g2 * 64:(g2 + 1) * 64, :, j::4, :]), r=[bb[ri].b, XY.b], w=[XY.b])

                def xpose_all(dst):
                    for k in range(4):
                        pt = ps[6 + k % 2]
                        ptv = pt[:].bitcast(BF16)
                        for sc in range(NSC):
                            P.pe(lambda e: e.transpose(ptv[:, sc * 128:(sc + 1) * 128], XY[:, k, sc, :], ident_b[:]), r=[XY.b, ident_b.b], w=[pt.b])
                        P.act(lambda e: e.activation(out=dst[:, k, :, :], in_=ptv.rearrange("p (s c) -> p s c", s=NSC), func=AF.Copy),
                              r=[pt.b], w=[dst.b])

                xpose_all(Bblk)
                for w_ in range(2):
                    P.dve(lambda e: e.tensor_scalar(out=BF[w_][:].rearrange("p k s c -> p (k s c)"), in0=Bblk[:].rearrange("p k s c -> p (k s c)"),
                                                    scalar1=flg[:, w_:w_ + 1], scalar2=None, op0=ALU.mult), r=[Bblk.b, flg.b], w=[BF[w_].b])
                cnat = [sb("cnat%d" % i, [128, 2, 2, 64], F32, pp) for i in range(2)]
                P.dma("sp", cnat[0][:], IN["s5_c_re"][l].rearrange("d (cc gl) h p -> (gl h) d cc p", gl=8), w=[cnat[0].b])
                P.dma("sp", cnat[1][:], IN["s5_c_im"][l].rearrange("d (cc gl) h p -> (gl h) d cc p", gl=8), w=[cnat[1].b])
                for ri in range(2):
                    for j in range(4):
                        for g2 in range(2):
                            P.dve(lambda e: e.tensor_scalar(out=XY[:, ri::2, j::4, g2 * 64:(g2 + 1) * 64], in0=cnat[ri][:],
                                                            scalar1=gmk[:, ri, 2 * j + g2:2 * j + g2 + 1], scalar2=None, op0=ALU.mult),
                                  r=[cnat[ri].b, tok, XY.b, Bblk.b], w=[XY.b])
                xpose_all(Cblk)
                P.dma("sp", dsk[:], IN["s5_d"][l].rearrange("(cc p) -> p cc", p=128), w=[dsk.b], allow_slow_non_contiguous=True)
                for cc in range(2):
                    P.dve(lambda e: e.tensor_scalar(out=diagD[:, cc, :], in0=ident_f[:], scalar1=dsk[:, cc:cc + 1], scalar2=None, op0=ALU.mult),
                          r=[dsk.b, ident_f.b], w=[diagD.b])
                for w_ in range(2):
                    P.dve(lambda e: e.tensor_scalar(out=diagF[w_][:], in0=diagD[:], scalar1=flg[:, w_:w_ + 1], scalar2=None, op0=ALU.mult),
                          r=[diagD.b, flg.b], w=[diagF[w_].b])
                P.barrier()
                pp.close()
                uT_ = [sb("uT%d" % i, [128, 4, J], BF16, ph) for i in range(3)]
                NE2 = NSC * J
                car = sb("car", [128, 2, NSC], F32, ph)
                NB2 = 2
                W_ = [sb("W%d" % i, [128, 2, NSC, J], F32, ph) for i in range(NB2)]
                Z_ = [sb("Z%d" % i, [128, 2, NSC, J], F32, ph) for i in range(NB2)]
                TA_ = [sb("TA%d" % i, [128, 4, NE2], BF16, ph) for i in range(NB2)]
                bub_ = [sb("bub%d" % i, [128, 2, NE2], BF16, ph) for i in range(NB2)]
                zb_ = [sb("zb%d" % i, [128, 2, NE2], BF16, ph) for i in range(NB2)]
                SRI_ = [sb("SRI%d" % i, [128, 2, NSC, J], BF16, ph) for i in range(NB2)]
                c4_ = [[sb("c4%d_%d" % (i, k), [128, NSC], F32, ph) for k in range(4)] for i in range(NB2)]
                yfs = [sb("yfs%d" % i, [128, 2, J], F32, ph) for i in range(2)]
                yfl = [sb("yfl%d" % i, [128, 2, J], F32, ph) for i in range(2)]
                ypo = [sb("ypo%d" % i, [128, 2, J], BF16, ph) for i in range(2)]
                F2 = lambda t_: t_[:].rearrange("p s t -> p (s t)")
                BUre, BUim = ps2(0), ps2(2)
                BUt = ps[0].b
                Yb = ps[4]
                jobs = []
                n_it = 0
                for d in range(2):
                    order = list(range(GTILE)) if d == 0 else [1, 0] + list(range(GTILE - 1, 1, -1))
                    for oi, Tt in enumerate(order):
                        jobs.append(dict(d=d, Tt=Tt, n_it=n_it, first=(oi == 0)))
                        n_it += 1

                F3 = lambda t_: t_[:].rearrange("p a s t -> p a (s t)")

                def stageA(jb, q):
                    d, Tt, n_it = jb["d"], jb["Tt"], jb["n_it"]
                    uT = uT_[n_it % 3]
                    W, TA, bub = W_[q], TA_[q], bub_[q]
                    pos_first = 0 if d == 0 else J - 1
                    if jb["first"]:
                        P.dve(lambda e: e.memset(car[:], 0.0), w=[car.b])
                    if Tt < 2:
                        P.dma("sp", uT[:], suC_d.ap[:, :, Tt * J:(Tt + 1) * J].rearrange("c p t -> p c t"), r=[suC_d.b], w=[uT.b])
                    else:
                        gt = Tt - 2
                        rk, lt = gt // LT, gt % LT
                        P.dma("sp", uT[:], suG_d.ap[rk * 512:(rk + 1) * 512, lt * J:(lt + 1) * J].rearrange("(c p) t -> p c t", p=128), r=[suG_d.b], w=[uT.b])
                    for scl in range(NSC):
                        cc = scl // 4
                        for ri in range(2):
                            BU = BUre if ri == 0 else BUim
                            P.pe(lambda e: e.matmul(BU[:, scl * J:(scl + 1) * J], BF[0][:, d * 2 + ri, scl, :], uT[:, cc, :], start=True, stop=False),
                                 r=[BF[0].b, uT.b], w=[BUt])
                            P.pe(lambda e: e.matmul(BU[:, scl * J:(scl + 1) * J], BF[1][:, d * 2 + ri, scl, :], uT[:, 2 + cc, :], start=False, stop=True),
                                 r=[BF[1].b, uT.b], w=[BUt])
                    P.act(lambda e: e.activation(out=bub[:].rearrange("p a n -> p (a n)"), in_=psbig[:, 0:4, :].rearrange("p a b -> p (a b)"), func=AF.Copy),
                          r=[BUt], w=[bub.b])
                    P.dve(lambda e: e.tensor_tensor(out=TA[:, 0:2, :], in0=bub[:, :, :], in1=CS[:, d, :, :], op=ALU.mult), r=[bub.b, CS.k(d)], w=[TA.k(0)])
                    P.dve(lambda e: e.tensor_tensor(out=TA[:, 2:4, :], in0=bub[:, ::-1, :], in1=CS2[:, d, :, :], op=ALU.mult), r=[bub.b, CS2.k(d)], w=[TA.k(1)])
                    P.dve(lambda e: e.tensor_tensor(out=F3(W), in0=TA[:, 0::2, :], in1=TA[:, 1::2, :], op=ALU.add), r=[TA.k(0), TA.k(1)], w=[W.b])
                    P.pool(lambda e: e.tensor_tensor(out=W[:, :, :, pos_first:pos_first + 1], in0=W[:, :, :, pos_first:pos_first + 1],
                                                     in1=car[:, :, :].unsqueeze(3), op=ALU.add), r=[W.b, car.b], w=[W.b])

                def stageC(jb, q):
                    d = jb["d"]
                    W, Z, c4 = W_[q], Z_[q], c4_[q]
                    Rr = Rm[:, d, :, :].rearrange("p s t -> p (s t)")
                    pos_last = J - 1 if d == 0 else 0
                    for ri in range(2):
                        wv_ = W[:, ri, :, :].rearrange("p s t -> p (s t)")
                        zv_ = Z[:, ri, :, :].rearrange("p s t -> p (s t)")
                        if d == 0:
                            P.dve(lambda e: e.tensor_tensor_scan(out=zv_, data0=Rr, data1=wv_, initial=0.0, op0=ALU.mult, op1=ALU.add),
                                  r=[W.b, Rm.k(d)], w=[Z.k(ri)])
                        else:
                            P.dve(lambda e: e.tensor_tensor_scan(out=zv_[:, ::-1], data0=Rr[:, ::-1], data1=wv_[:, ::-1], initial=0.0,
                                                                 op0=ALU.mult, op1=ALU.add), r=[W.b, Rm.k(d)], w=[Z.k(ri)])
                    zlr = Z[:, 0, :, pos_last:pos_last + 1].rearrange("p s o -> p (s o)")
                    zli = Z[:, 1, :, pos_last:pos_last + 1].rearrange("p s o -> p (s o)")
                    Kr = kre[:, d, :]
                    Ki = kim[:, d, :]
                    P.pool(lambda e: e.tensor_tensor(out=c4[0][:], in0=zlr, in1=Kr, op=ALU.mult), r=[Z.k(0), tok], w=[c4[0].b])
                    P.pool(lambda e: e.tensor_tensor(out=c4[1][:], in0=zli, in1=Ki, op=ALU.mult), r=[Z.k(1), tok], w=[c4[1].b])
                    P.pool(lambda e: e.tensor_tensor(out=c4[2][:], in0=zli, in1=Kr, op=ALU.mult), r=[Z.k(1), tok], w=[c4[2].b])
                    P.pool(lambda e: e.tensor_tensor(out=c4[3][:], in0=zlr, in1=Ki, op=ALU.mult), r=[Z.k(0), tok], w=[c4[3].b])
                    P.pool(lambda e: e.tensor_tensor(out=car[:, 0, :], in0=c4[0][:], in1=c4[1][:], op=ALU.subtract), r=[c4[0].b, c4[1].b], w=[car.b])
                    P.pool(lambda e: e.tensor_tensor(out=car[:, 1, :], in0=c4[2][:], in1=c4[3][:], op=ALU.add), r=[c4[2].b, c4[3].b, car.b], w=[car.b])

                def stageB(jb, q):
                    d, Tt, n_it = jb["d"], jb["Tt"], jb["n_it"]
                    uT = uT_[n_it % 3]
                    Z, TA, zb, SRI = Z_[q], TA_[q], zb_[q], SRI_[q]
                    P.act(lambda e: e.activation(out=zb[:, :, :], in_=F3(Z), func=AF.Copy), r=[Z.k(0), Z.k(1)], w=[zb.b])
                    P.dve(lambda e: e.tensor_tensor(out=TA[:, 0:2, :], in0=zb[:, :, :], in1=CS2[:, d, :, :], op=ALU.mult), r=[zb.b, CS2.k(d)], w=[TA.k(0)])
                    P.dve(lambda e: e.tensor_tensor(out=TA[:, 2:4, :], in0=zb[:, ::-1, :], in1=CS[:, d, :, :], op=ALU.mult), r=[zb.b, CS.k(d)], w=[TA.k(1)])
                    P.dve(lambda e: e.tensor_tensor(out=F3(SRI), in0=TA[:, 0::2, :], in1=TA[:, 1::2, :], op=ALU.add), r=[TA.k(0), TA.k(1)], w=[SRI.b])
                    for cc in range(2):
                        first = True
                        if d == 1:
                            P.pe(lambda e: e.matmul(Yb[:, cc * J:(cc + 1) * J], diagF[0][:, cc, :], uT[:, cc, :], start=True, stop=False),
                                 r=[diagF[0].b, uT.b], w=[Yb.b])
                            P.pe(lambda e: e.matmul(Yb[:, cc * J:(cc + 1) * J], diagF[1][:, cc, :], uT[:, 2 + cc, :], start=False, stop=False),
                                 r=[diagF[1].b, uT.b], w=[Yb.b])
                            first = False
                        for q4 in range(4):
                            scl = cc * 4 + q4
                            P.pe(lambda e: e.matmul(Yb[:, cc * J:(cc + 1) * J], Cblk[:, d * 2 + 0, scl, :], SRI[:, 0, scl, :], start=first, stop=False),
                                 r=[Cblk.b, SRI.b], w=[Yb.b])
                            first = False
                            P.pe(lambda e: e.matmul(Yb[:, cc * J:(cc + 1) * J], Cblk[:, d * 2 + 1, scl, :], SRI[:, 1, scl, :], start=False, stop=(q4 == 3)),
                                 r=[Cblk.b, SRI.b], w=[Yb.b])
                    Yv = Yb[:, 0:2 * J].rearrange("p (c t) -> p c t", c=2)
                    if d == 0:
                        ys_ = yfs[n_it % 2]
                        P.act(lambda e: e.activation(out=ys_[:], in_=Yv, func=AF.Copy), r=[Yb.b], w=[ys_.b])
                        P.dma("sp", yf_d.ap[:, :, Tt * J:(Tt + 1) * J].rearrange("c p t -> p c t"), ys_[:], r=[ys_.b], w=[yf_d.k(Tt)])
                    else:
                        yl = yfl[n_it % 2]
                        yo = ypo[n_it % 2]
                        P.dma("sp", yl[:], yf_d.ap[:, :, Tt * J:(Tt + 1) * J].rearrange("c p t -> p c t"), r=[yf_d.k(Tt)], w=[yl.b])
                        P.dve(lambda e: e.tensor_tensor(out=yo[:], in0=Yv, in1=yl[:], op=ALU.add), r=[Yb.b, yl.b], w=[yo.b])
                        if Tt < 2:
                            P.dma("sp", ypLc_d.ap[:, Tt * J:(Tt + 1) * J].rearrange("(c p) t -> p c t", p=128), yo[:], r=[yo.b], w=[ypLc_d.b])
                        else:
                            P.dma("sp", ypLl_d.ap[:, (Tt - 2) * J:(Tt - 1) * J].rearrange("(c p) t -> p c t", p=128), yo[:], r=[yo.b], w=[ypLl_d.b])

                for i, jb in enumerate(jobs):
                    stageA(jb, i % 2)
                    if i > 0:
                        stageB(jobs[i - 1], (i - 1) % 2)
                    stageC(jb, i % 2)
                stageB(jobs[-1], (len(jobs) - 1) % 2)
                P.barrier()
            P.allgather(ypLc_d, ypGc_d, r=[ypLc_d.b], w=[ypGc_d.b])
            P.allgather(ypLl_d, ypGl_d, r=[ypLl_d.b], w=[ypGl_d.b])
            with ExitStack() as ph:
                wglu = sb("wglu", [128, 4, 512], BF16, ph)
                P.dma("pool", wglu[:], IN["s5_w_glu"][l].rearrange("(c p) n -> p c n", p=128), w=[wglu.b])
                yA_ = [sb("yA%d" % i, [128, 4, 512], BF16, ph) for i in range(2)]
                yB_ = [sb("yB%d" % i, [128, 4, 512], BF16, ph) for i in range(2)]
                ysel = sb("ysel", [128, 4, 512], F32, ph)
                g_ = [sb("gg%d" % i, [128, 4, 512], BF16, ph) for i in range(2)]
                sg = [sb("sgg%d" % i, [128, 512], F32, ph) for i in range(2)]
                yo_ = [sb("yoo%d" % i, [128, 4, 512], BF16, ph) for i in range(2)]
                for bi, (t0, N, isctx) in enumerate(BLOCKS):
                    if isctx and not need_ctx:
                        continue
                    yA, yB, g, yo = yA_[bi % 2], yB_[bi % 2], g_[bi % 2], yo_[bi % 2]
                    if isctx:
                        P.dma("sp", yA[:, :, :N], ypGc_d.ap[:, 0:N].rearrange("(c p) t -> p c t", p=128), r=[ypGc_d.b], w=[yA.b])
                        P.act(lambda e: e.activation(out=g[:, :, :N], in_=yA[:, :, :N], func=AF.Gelu), r=[yA.b], w=[g.b])
                    else:
                        l0 = t0 - NCTX
                        P.dma("sp", yA[:, :, :N], ypGl_d.ap[:, l0:l0 + N].rearrange("(c p) t -> p c t", p=128), r=[ypGl_d.b], w=[yA.b])
                        P.dma("sp", yB[:, :, :N], ypGl_d.ap[:, NLAT + l0:NLAT + l0 + N].rearrange("(c p) t -> p c t", p=128), r=[ypGl_d.b], w=[yB.b])
                        P.dve(lambda e: e.tensor_scalar(out=ysel[:, :, :N], in0=yA[:, :, :N], scalar1=flg[:, 0:1], scalar2=None, op0=ALU.mult),
                              r=[yA.b, flg.b], w=[ysel.b])
                        P.dve(lambda e: e.scalar_tensor_tensor(out=ysel[:, :, :N], in0=yB[:, :, :N], scalar=flg[:, 1:2], in1=ysel[:, :, :N],
                                                               op0=ALU.mult, op1=ALU.add), r=[yB.b, flg.b, ysel.b], w=[ysel.b])
                        P.act(lambda e: e.activation(out=g[:, :, :N], in_=ysel[:, :, :N], func=AF.Gelu), r=[ysel.b], w=[g.b])
                    for oc in range(4):
                        Gb = ps[oc % 2]
                        for kc in range(4):
                            P.pe(lambda e: e.matmul(Gb[:, :N], wglu[:, kc, oc * 128:(oc + 1) * 128], g[:, kc, :N], start=(kc == 0), stop=(kc == 3)),
                                 r=[wglu.b, g.b], w=[Gb.b])
                        sg_ = sg[oc % 2]
                        P.act(lambda e: e.activation(out=sg_[:, :N], in_=Gb[:, :N], func=AF.Sigmoid), r=[Gb.b], w=[sg_.b])
                        P.dve(lambda e: e.tensor_tensor(out=yo[:, oc, :N], in0=g[:, oc, :N], in1=sg_[:, :N], op=ALU.mult), r=[g.b, sg_.b], w=[yo.k(oc)])
                    P.dma("sp", ysT_d.ap[:, :, t0:t0 + N].rearrange("c p t -> p c t"), yo[:, :, :N], r=[yo.k(oc) for oc in range(4)], w=[ysT_d.k(bi)])
                P.barrier()

        def phase_M(l):
            need_ctx = l < DEPTH - 1
            with ExitStack() as ph:
                wg = sb("wg", [128, 8, 3072], BF16, ph)
                wpd = sb("wpd", [128, 4, 1024], BF16, ph)
                wps = sb("wps", [128, 4, 1024], BF16, ph)
                wpw = sb("wpw", [64, 8, 1024], BF16, ph)
                wout = sb("wout", [128, 8, 1024], BF16, ph)
                for c in range(8):
                    P.dma("pool", wg[:, c, :], IN["w_in"][l, c * 128:(c + 1) * 128, 2816:5888], w=[wg.k(c)])
                P.dma("pool", wpd[:], IN["w_proj_diff"][l].rearrange("(c p) n -> p c n", p=128), w=[wpd.b])
                P.dma("pool", wps[:], IN["w_proj_s5"][l].rearrange("(c p) n -> p c n", p=128), w=[wps.b])
                P.dma("pool", wpw[:], IN["w_proj_win"][l].rearrange("(h d) n -> d h n", d=64), w=[wpw.b])
                for c in range(0, 8, 2):
                    P.dma("pool", wout[:, c:c + 2, :], IN["w_out"][l, c * 128:(c + 2) * 128, :].rearrange("(c p) n -> p c n", p=128), w=[wout.k(c // 2)])
                a_ = [sb("am%d" % i, [128, 8, 512], BF16, ph) for i in range(2)]
                yd_ = [sb("ydm%d" % i, [128, 4, 512], BF16, ph) for i in range(2)]
                ys_ = [sb("ysm%d" % i, [128, 4, 512], BF16, ph) for i in range(2)]
                yw_ = [sb("ywm%d" % i, [64, 8, 512], BF16, ph) for i in range(2)]
                h_ = [sb("hm%d" % i, [128, 8, 512], F32, ph) for i in range(2)]
                mT = sb("mT", [128, 8, 512], BF16, ph)
                sig = [sb("sig%d" % i, [128, 512], F32, ph) for i in range(3)]
                mt = [sb("mt%d" % i, [128, 512], F32, ph) for i in range(3)]
                gcnt = [0]
                for bi, (t0, N, isctx) in enumerate(BLOCKS):
                    if isctx and not need_ctx:
                        continue
                    mi = 1 if isctx else 0
                    a, yd, ys, yw, h = a_[bi % 2], yd_[bi % 2], ys_[bi % 2], yw_[bi % 2], h_[bi % 2]
                    dv = lambda dt_: dt_.ap[:, :, t0:t0 + N].rearrange("c p t -> p c t")
                    tl = tiles_of(t0, N)
                    P.dma("sp", a[:, :, :N], dv(aT_d), r=[aT_d.k(bi)], w=[a.b])
                    P.dma("sp", yd[:, :, :N], dv(ydT_d), r=[ydT_d.k(bi)], w=[yd.b])
                    P.dma("sp", ys[:, :, :N], dv(ysT_d), r=[ysT_d.k(bi)], w=[ys.b])
                    P.dma("sp", yw[:, :, :N], ywT_d.ap[:, :, t0:t0 + N].rearrange("h d t -> d h t"), r=[ywT_d.k(bi)], w=[yw.b])
                    P.dma("sp", h[:, :, :N], dv(hT_d), r=[hT_d.k(t) for t in tl], w=[h.b])
                    for oc in range(8):
                        for br in range(3):
                            G = ps[gcnt[0] % 2]
                            gcnt[0] += 1
                            Pj = ps[2 + br]
                            for c in range(8):
                                P.pe(lambda e: e.matmul(G[:, :N], wg[:, c, br * 1024 + oc * 128:br * 1024 + (oc + 1) * 128], a[:, c, :N], start=(c == 0), stop=(c == 7)),
                                     r=[wg.k(c), a.b], w=[G.b])
                            P.act(lambda e: e.activation(out=sig[br][:, :N], in_=G[:, :N], func=AF.Sigmoid), r=[G.b], w=[sig[br].b])
                            if br == 0:
                                for k in range(4):
                                    P.pe(lambda e: e.matmul(Pj[:, :N], wpd[:, k, oc * 128:(oc + 1) * 128], yd[:, k, :N], start=(k == 0), stop=(k == 3)),
                                         r=[wpd.b, yd.b], w=[Pj.b])
                            elif br == 1:
                                for k in range(4):
                                    P.pe(lambda e: e.matmul(Pj[:, :N], wps[:, k, oc * 128:(oc + 1) * 128], ys[:, k, :N], start=(k == 0), stop=(k == 3)),
                                         r=[wps.b, ys.b], w=[Pj.b])
                            else:
                                for k in range(8):
                                    P.pe(lambda e: e.matmul(Pj[:, :N], wpw[:, k, oc * 128:(oc + 1) * 128], yw[:, k, :N], start=(k == 0), stop=(k == 7)),
                                         r=[wpw.b, yw.b], w=[Pj.b])
                            P.dve(lambda e: e.tensor_tensor(out=mt[br][:, :N], in0=Pj[:, :N], in1=sig[br][:, :N], op=ALU.mult), r=[Pj.b, sig[br].b], w=[mt[br].b])
                        P.dve(lambda e: e.tensor_tensor(out=mt[0][:, :N], in0=mt[0][:, :N], in1=mt[1][:, :N], op=ALU.add), r=[mt[0].b, mt[1].b], w=[mt[0].b])
                        P.dve(lambda e: e.tensor_tensor(out=mT[:, oc, :N], in0=mt[0][:, :N], in1=mt[2][:, :N], op=ALU.add), r=[mt[0].b, mt[2].b], w=[mT.k(oc)])
                    for oc in range(8):
                        O = ps[5 + oc % 2]
                        for c in range(8):
                            P.pe(lambda e: e.matmul(O[:, :N], wout[:, c, oc * 128:(oc + 1) * 128], mT[:, c, :N], start=(c == 0), stop=(c == 7)),
                                 r=[wout.k(c // 2), mT.k(c)], w=[O.b])
                        P.dve(lambda e: e.scalar_tensor_tensor(out=h[:, oc, :N], in0=O[:, :N], scalar=modv[:, l, 16 + oc, mi:mi + 1], in1=h[:, oc, :N],
                                                               op0=ALU.mult, op1=ALU.add), r=[O.b, modv.b, h.b], w=[h.b])
                    P.dma("sp", dv(hT_d), h[:, :, :N], r=[h.b], w=[hT_d.k(t) for t in tl])
                P.barrier()

        def phase_F(l, last):
            need_ctx = l < DEPTH - 1
            with ExitStack() as ph:
                w1 = sb("w1", [128, 8, 4096], BF16, ph)
                w2 = sb("w2", [128, 32, 1024], BF16, ph)
                for c in range(8):
                    P.dma("pool", w1[:, c, :], IN["w_ff1"][l, c * 128:(c + 1) * 128, :], w=[w1.k(c)])
                for c in range(0, 32, 4):
                    P.dma("pool", w2[:, c:c + 4, :], IN["w_ff2"][l, c * 128:(c + 4) * 128, :].rearrange("(c p) n -> p c n", p=128), w=[w2.k(c // 4)])
                FN = 512
                h = sb("hf", [128, 8, FN], F32, ph)
                sq = sb("sqf", [128, 8, FN], BF16, ph)
                rt = sb("rtf", [128, FN], F32, ph)
                rstd = sb("rstdf", [128, FN], F32, ph)
                tmp = [sb("tmpf%d" % i, [128, FN], F32, ph) for i in range(2)]
                fT = sb("fT", [128, 8, FN], BF16, ph)
                rl_ = [sb("rlf%d" % i, [128, FN], BF16, ph) for i in range(2)]
                hid = sb("hid", [128, 32, FN], BF16, ph)
                class OTV:
                    def __init__(self, i):
                        self.i = i
                        self.b = sq.b

                    def __getitem__(self, idx):
                        return sq[:].rearrange("p c t -> p (c t)").bitcast(F32)[:, self.i * 1024:(self.i + 1) * 1024][idx]

                    def k(self, key):
                        return sq.b
                ot = [OTV(i) for i in range(2)]
                cnt = [0, 0]
                for bi, (t0, N, isctx) in enumerate(BLOCKS):
                    if isctx and not need_ctx:
                        continue
                    mi = 1 if isctx else 0
                    tl = tiles_of(t0, N)
                    dv = lambda dt_: dt_.ap[:, :, t0:t0 + N].rearrange("c p t -> p c t")
                    P.dma("sp", h[:, :, :N], dv(hT_d), r=[hT_d.k(t) for t in tl], w=[h.b] + [h.k(oc) for oc in range(8)])
                    P.act(lambda e: e.activation(out=sq[:, :, :N], in_=h[:, :, :N], func=AF.Square), r=[h.b], w=[sq.b])
                    for c in range(8):
                        P.pe(lambda e: e.matmul(ps[0][:, :N], ones_b[:], sq[:, c, :N], start=(c == 0), stop=(c == 7)), r=[sq.b, ones_b.b], w=[ps[0].b])
                    P.act(lambda e: e.activation(out=rt[:, :N], in_=ps[0][:, :N], func=AF.Sqrt, scale=1.0 / D, bias=EPS), r=[ps[0].b], w=[rt.b])
                    P.dve(lambda e: e.reciprocal(out=rstd[:, :N], in_=rt[:, :N]), r=[rt.b], w=[rstd.b])
                    for c in range(8):
                        tp = tmp[c % 2]
                        P.dve(lambda e: e.scalar_tensor_tensor(out=tp[:, :N], in0=h[:, c, :N], scalar=gm2[:, l, c, mi:mi + 1], in1=rstd[:, :N], op0=ALU.mult, op1=ALU.mult),
                              r=[h.b, gm2.b, rstd.b], w=[tp.b])
                        P.act(lambda e: e.activation(out=fT[:, c, :N], in_=tp[:, :N], func=AF.Identity, bias=modv[:, l, 24 + c, mi:mi + 1], scale=1.0),
                              r=[tp.b, modv.b], w=[fT.k(c)])
                    for fc in range(32):
                        pb = ps[1 + fc % 3]
                        for c in range(8):
                            P.pe(lambda e: e.matmul(pb[:, :N], w1[:, c, fc * 128:(fc + 1) * 128], fT[:, c, :N], start=(c == 0), stop=(c == 7)),
                                 r=[w1.k(c), fT.k(c)], w=[pb.b])
                        rl = rl_[fc % 2]
                        P.act(lambda e: e.activation(out=rl[:, :N], in_=pb[:, :N], func=AF.Relu), r=[pb.b], w=[rl.b])
                        P.dve(lambda e: e.tensor_tensor(out=hid[:, fc, :N], in0=rl[:, :N], in1=rl[:, :N], op=ALU.mult), r=[rl.b], w=[hid.k(fc // 4)])
                    for oc in range(8):
                        O = ps[4 + oc % 2]
                        for fc in range(32):
                            P.pe(lambda e: e.matmul(O[:, :N], w2[:, fc, oc * 128:(oc + 1) * 128], hid[:, fc, :N], start=(fc == 0), stop=(fc == 31)),
                                 r=[w2.k(fc // 4), hid.k(fc // 4)], w=[O.b])
                        P.dve(lambda e: e.scalar_tensor_tensor(out=h[:, oc, :N], in0=O[:, :N], scalar=modv[:, l, 40 + oc, mi:mi + 1], in1=h[:, oc, :N],
                                                               op0=ALU.mult, op1=ALU.add), r=[O.b, modv.b, h.b], w=[h.k(oc)])
                    if not last:
                        P.dma("sp", dv(hT_d), h[:, :, :N], r=[h.k(oc) for oc in range(8)] + [h.b], w=[hT_d.k(t) for t in tl])
                    else:
                        for j in range(N // 128):
                            o = ot[cnt[0] % 2]
                            cnt[0] += 1
                            for half in range(2):
                                pt = ps[6 + half]
                                for q in range(4):
                                    c = half * 4 + q
                                    P.pe(lambda e: e.transpose(pt[:, q * 128:(q + 1) * 128], h[:, c, j * 128:(j + 1) * 128], ident_f[:]),
                                         r=[h.k(c), h.b, ident_f.b], w=[pt.b])
                                if half == 0:
                                    P.act(lambda e: e.activation(out=o[:, 0:512], in_=pt[:], func=AF.Copy), r=[pt.b], w=[o.k(0)])
                                else:
                                    P.dve(lambda e: e.tensor_copy(out=o[:, 512:1024], in_=pt[:]), r=[pt.b], w=[o.k(1)])
                            row = t0 - NCTX + j * 128
                            P.dma("sp", out_d.ap[row:row + 128, :], o[:], r=[o.k(0), o.k(1)], w=[out_d.k(row)])
                P.barrier()

        for l in range(nl):
            phase_A(l)
            if stop_after == "A":
                break
            if stop_after not in ("W", "S"):
                phase_D(l)
            if stop_after == "D":
                break
            if stop_after != "S":
                phase_W(l)
            if stop_after == "W":
                break
            phase_S(l)
            if stop_after == "S":
                break
            phase_M(l)
            if stop_after == "M":
                break
            phase_F(l, last=(l == nl - 1) and final_out)
        P.barrier()
        stats = P.finalize(st)
        print("build stats", stats)
    return nc


S5_GROUP_AXIS = {"s5_lambda_re": 2, "s5_lambda_im": 2, "s5_log_dt": 2, "s5_b_re": 2, "s5_b_im": 2, "s5_c_re": 2, "s5_c_im": 2}


def make_in_maps(inputs, n_cores, nl=DEPTH):
    consts = host_consts()
    maps = []
    for core in range(n_cores):
        b, h = core // 2, core % 2
        m = {"x": np.ascontiguousarray(inputs["x"][b][h * NLAT:(h + 1) * NLAT]), "ctx": np.ascontiguousarray(inputs["ctx"][b]),
             "c": np.ascontiguousarray(inputs["c"][b]), "c_ctx": np.ascontiguousarray(inputs["c_ctx"])}
        for name, _ in WEIGHT_SPECS:
            w = inputs[name][:nl]
            if name in S5_GROUP_AXIS:
                w = np.take(w, np.arange(h * 16, (h + 1) * 16), axis=S5_GROUP_AXIS[name])
            elif name == "s5_d":
                w = w[:, h * 256:(h + 1) * 256]
            m[name] = np.ascontiguousarray(w)
        for name, _ in CONST_SPECS:
            v = consts.get(name)
            if name in ("rope_cos", "rope_sin"):
                v = np.ascontiguousarray(v[h * NLAT:(h + 1) * NLAT])
            elif name == "flags":
                v = np.zeros((128, 2), np.float32)
                v[:, h] = 1.0
            m[name] = v
        maps.append(m)
    return maps


def kernel(**inputs):
    n = 8
    nc = build(n_cores=n)
    res = run_bass_kernel_spmd(nc, make_in_maps(inputs, n), core_ids=list(range(n)))
    return np.stack([np.concatenate([res.results[2 * b]["out"], res.results[2 * b + 1]["out"]], 0) for b in range(4)], 0).astype(np.float32)
```

```python
import math
import types
import numpy as np
from contextlib import ExitStack
import concourse.bass as bass
import concourse.mybir as mybir
from concourse.bass_utils import run_bass_kernel_spmd

F32 = mybir.dt.float32
BF16 = mybir.dt.bfloat16
ALU = mybir.AluOpType
AF = mybir.ActivationFunctionType
AX = mybir.AxisListType

D = 1024
NCTX = 256
NLAT = 2048
NT = NCTX + NLAT
NTILE = NT // 128
GLAT = 4096
GNT = NCTX + GLAT
GTILE = GNT // 128
LT = NLAT // 128
DEPTH = 4
EPS = 1e-6
DIN = 5888
J = 128
PI = math.pi


class Buf:
    __slots__ = ("name", "w", "rc", "rd")

    def __init__(self, name=""):
        self.name = name
        self.w = None
        self.rc = {}
        self.rd = []


class Op:
    __slots__ = ("eng", "fn", "r", "w", "dma", "deps", "waits", "need_inc", "ev", "idx", "bar")

    def __init__(self, eng, fn, r, w, dma):
        self.eng = eng
        self.fn = fn
        self.r = r
        self.w = w
        self.dma = dma
        self.need_inc = False
        self.ev = None
        self.waits = None
        self.bar = False


class Prog:
    NDMA = 12
    LIMIT = 30000

    def __init__(self, nc):
        self.nc = nc
        self.ops = []
        self.cc_n = 0
        self.engs = {"pe": nc.tensor, "act": nc.scalar, "dve": nc.vector,
                     "pool": nc.gpsimd, "sp": nc.sync}

    @staticmethod
    def _freeze(fn):
        if fn is None or fn.__closure__ is None:
            return fn
        cells = []
        for c in fn.__closure__:
            try:
                cells.append(types.CellType(c.cell_contents))
            except ValueError:
                cells.append(c)
        return types.FunctionType(fn.__code__, fn.__globals__, fn.__name__, fn.__defaults__, tuple(cells))

    def add(self, eng, fn, r=(), w=(), dma=False):
        op = Op(eng, self._freeze(fn), tuple(r), tuple(w), dma)
        op.idx = len(self.ops)
        self.ops.append(op)
        return op

    def pe(self, fn, r=(), w=()): return self.add("pe", fn, r, w)
    def act(self, fn, r=(), w=()): return self.add("act", fn, r, w)
    def dve(self, fn, r=(), w=()): return self.add("dve", fn, r, w)
    def pool(self, fn, r=(), w=()): return self.add("pool", fn, r, w)

    def dma(self, q, out, in_, r=(), w=(), **kw):
        return self.add(q, lambda e: e.dma_start(out=out, in_=in_, **kw), r, w, dma=True)

    def allgather(self, in_dt, out_dt, r, w):
        self.cc_n += 1
        n = self.cc_n
        sem, flag, rg = self.cc_sem, self.cc_flag, self.cc_rg
        ia, oa = in_dt.ap, out_dt.ap

        def fn(E):
            E.collective_compute("AllGather", ALU.bypass, replica_groups=rg, ins=[ia.opt()], outs=[oa.opt()]).then_inc(sem)
            E.wait_ge(sem, n)
            return E.memset(flag, 0.0)
        return self.add("pool", fn, r, w)

    def barrier(self):
        for e in self.engs:
            op = self.add(e, None)
            op.bar = True

    def finalize(self, stack):
        nc = self.nc
        ops = self.ops
        dma_rr = {}
        dma_last = {}
        last_c = {}
        dma_since = []
        for i, op in enumerate(ops):
            deps = set()
            if op.bar:
                deps.update(last_c.values())
                deps.update(dma_since)
            for b in op.r:
                if b.w is not None:
                    deps.add(b.w)
            for b in op.w:
                if b.w is not None:
                    deps.add(b.w)
                deps.update(b.rc.values())
                deps.update(b.rd)
            for b in op.r:
                if op.dma:
                    b.rd.append(i)
                else:
                    b.rc[op.eng] = i
            for b in op.w:
                b.w = i
                b.rc = {}
                b.rd = []
            if op.dma:
                k = dma_rr.get(op.eng, 0)
                dma_rr[op.eng] = k + 1
                slot = (op.eng, k % self.NDMA)
                if slot in dma_last:
                    deps.add(dma_last[slot])
                dma_last[slot] = i
                op.ev = slot
                dma_since.append(i)
                if len(dma_since) > 3 * self.NDMA:
                    dma_since = dma_since[-3 * self.NDMA:]
            elif op.fn is not None:
                last_c[op.eng] = i
            deps.discard(i)
            op.deps = deps
        known_c = {e: {} for e in self.engs}
        known_d = {e: {} for e in self.engs}
        dma_cnt = {}
        for i, op in enumerate(ops):
            X = op.eng
            cand = {}
            dwaits = []
            for d in op.deps:
                od = ops[d]
                if od.dma:
                    slot, val = od.ev
                    if known_d[X].get(slot, 0) < val:
                        known_d[X][slot] = val
                        dwaits.append((slot, val))
                else:
                    if od.eng == "pe" and X == "pe" and not op.dma and not op.bar:
                        continue
                    if cand.get(od.eng, -1) < d:
                        cand[od.eng] = d
            cw = []
            for Y, d in cand.items():
                if known_c[X].get(Y, -1) < d:
                    known_c[X][Y] = d
                    ops[d].need_inc = True
                    cw.append(d)
            op.waits = (cw, dwaits)
            if op.dma:
                slot = op.ev
                v = dma_cnt.get(slot, 0) + 16
                dma_cnt[slot] = v
                op.ev = (slot, v)
        sem_c = {}
        cnt = {e: 0 for e in self.engs}
        epoch = {e: 0 for e in self.engs}

        def get_sem(key):
            if key not in sem_c:
                sem_c[key] = stack.enter_context(nc.semaphore("s_%s_%s" % key))
            return sem_c[key]

        for op in ops:
            if op.dma or op.fn is None:
                continue
            if op.need_inc:
                e = op.eng
                if cnt[e] >= self.LIMIT:
                    cnt[e] = 0
                    epoch[e] += 1
                cnt[e] += 1
                op.ev = ((e, "c%d" % epoch[e]), cnt[e])
        n_wait = 0
        for op in ops:
            E = self.engs[op.eng]
            cw, dwaits = op.waits
            for d in cw:
                key, val = ops[d].ev
                E.wait_ge(get_sem(key), val)
                n_wait += 1
            for slot, val in dwaits:
                E.wait_ge(get_sem((slot[0], "d%d" % slot[1])), val)
                n_wait += 1
            if op.fn is None:
                continue
            ins = op.fn(E)
            if op.dma:
                slot, val = op.ev
                ins.then_inc(get_sem((slot[0], "d%d" % slot[1])), 16)
            elif op.need_inc:
                key, val = op.ev
                ins.then_inc(get_sem(key), 1)
        self.stats = dict(n_ops=len(ops), n_wait=n_wait, n_sem=len(sem_c),
                          n_inc=sum(1 for o in ops if o.need_inc))
        return self.stats


class T:
    def __init__(self, nc, st, name, shape, dt, psum=False):
        if psum:
            self.t = st.enter_context(nc.psum_tensor(name, shape, dt))
        else:
            self.t = st.enter_context(nc.sbuf_tensor(name, shape, dt))
        self.b = Buf(name)
        self.sub = {}
        self.name = name

    def __getitem__(self, idx):
        return self.t[idx]

    def k(self, key):
        if key not in self.sub:
            self.sub[key] = Buf("%s.%s" % (self.name, key))
        return self.sub[key]


class DT:
    def __init__(self, ap, name):
        self.ap = ap
        self.name = name
        self.b = Buf(name)
        self.sub = {}

    def k(self, key):
        if key not in self.sub:
            self.sub[key] = Buf("%s.%s" % (self.name, key))
        return self.sub[key]


BLOCKS = [(0, NCTX, True)] + [(NCTX + 512 * i, 512, False) for i in range(NLAT // 512)]

WEIGHT_SPECS = [
    ("w_mod", [DEPTH, D, 6 * D]), ("b_mod", [DEPTH, 6 * D]), ("norm1_g", [DEPTH, D]), ("norm2_g", [DEPTH, D]),
    ("w_in", [DEPTH, D, DIN]),
    ("diff_q_norm_g", [DEPTH, 64]), ("diff_k_norm_g", [DEPTH, 64]),
    ("diff_lam_q1", [DEPTH, 64]), ("diff_lam_k1", [DEPTH, 64]), ("diff_lam_q2", [DEPTH, 64]), ("diff_lam_k2", [DEPTH, 64]),
    ("diff_out_norm_g", [DEPTH, 128]),
    ("s5_lambda_re", [DEPTH, 2, 16, 64]), ("s5_lambda_im", [DEPTH, 2, 16, 64]), ("s5_log_dt", [DEPTH, 2, 16]),
    ("s5_b_re", [DEPTH, 2, 16, 64, 16]), ("s5_b_im", [DEPTH, 2, 16, 64, 16]),
    ("s5_c_re", [DEPTH, 2, 16, 16, 64]), ("s5_c_im", [DEPTH, 2, 16, 16, 64]),
    ("s5_d", [DEPTH, 256]), ("s5_w_glu", [DEPTH, 512, 512]),
    ("win_q_norm_g", [DEPTH, 64]), ("win_k_norm_g", [DEPTH, 64]), ("win_sink", [DEPTH, 8]),
    ("w_proj_diff", [DEPTH, 512, D]), ("w_proj_s5", [DEPTH, 512, D]), ("w_proj_win", [DEPTH, 512, D]),
    ("w_out", [DEPTH, D, D]), ("w_ff1", [DEPTH, D, 4 * D]), ("w_ff2", [DEPTH, 4 * D, D]),
]


def host_consts():
    c = {}
    c["ident_f"] = np.eye(128, dtype=np.float32)
    n_freq = 16
    inv = (10000.0 ** (-np.arange(n_freq, dtype=np.float32) / n_freq)).astype(np.float32)
    r = np.repeat(np.arange(64, dtype=np.float32), 64)
    col = np.tile(np.arange(64, dtype=np.float32), 64)
    ang = np.concatenate([r[:, None] * inv, col[:, None] * inv], -1).astype(np.float32)
    cs, sn = np.cos(ang).astype(np.float32), np.sin(ang).astype(np.float32)
    c["rope_cos"] = np.concatenate([cs, cs], -1)
    c["rope_sin"] = np.concatenate([-sn, sn], -1)
    kk = np.arange(128)[:, None]
    qq = np.arange(128)[None, :]
    c["mask_prev"] = (kk >= qq).astype(np.float32)
    c["mask_next"] = (kk <= qq).astype(np.float32)
    tau = np.stack([np.arange(J), np.arange(J)[::-1]], 0).astype(np.float32)
    c["tau"] = np.broadcast_to(tau[None], (128, 2, J)).copy()
    gm = np.zeros((128, 8), np.float32)
    gm[np.arange(128), np.arange(128) // 16] = 1.0
    c["gmask"] = gm
    return c


CONST_SPECS = [("ident_f", [128, 128]), ("rope_cos", [NLAT, 64]), ("rope_sin", [NLAT, 64]),
               ("mask_prev", [128, 128]), ("mask_next", [128, 128]), ("tau", [128, 2, J]), ("gmask", [128, 8]), ("flags", [128, 2])]


def build(nl=DEPTH, dbg=(), stop_after=None, dlimit=None, final_out=True, n_cores=8):
    nc = bass.Bass("TRN2", target_bir_lowering=False)
    st = ExitStack()
    with st:
        P = Prog(nc)
        IN = {}

        def din(name, shape):
            IN[name] = nc.dram_tensor(name, list(shape), F32, kind="ExternalInput").ap()
            return IN[name]

        x_d = din("x", [NLAT, D])
        ctx_d = din("ctx", [NCTX, D])
        c_d = din("c", [D])
        cc_d = din("c_ctx", [D])
        for name, shape in WEIGHT_SPECS:
            din(name, [nl] + list(shape[1:]))
        for name, shape in CONST_SPECS:
            din(name, shape)
        out_d = DT(nc.dram_tensor("out", [NLAT, D], F32, kind="ExternalOutput").ap(), "out")

        def dscr(name, shape, dt):
            kind = "ExternalOutput" if name in dbg else "Internal"
            return DT(nc.dram_tensor(name, list(shape), dt, kind=kind).ap(), name)

        hT_d = dscr("hT", [8, 128, NT], F32)
        aT_d = dscr("aT", [8, 128, NT], BF16)
        qdT_d = dscr("qdT", [4, 128, NT], BF16)
        wqT_d = dscr("wqT", [4, 128, NT], BF16)
        kdC_d = dscr("kdC", [4, 128, NCTX], BF16)
        kdL_d = dscr("kdL", [512, NLAT], BF16)
        kdG_d = dscr("kdG", [1024, NLAT], BF16)
        vdC_d = dscr("vdC", [NCTX, 512], BF16)
        vdL_d = dscr("vdL", [NLAT, 512], BF16)
        vdG_d = dscr("vdG", [2 * NLAT, 512], BF16)
        suC_d = dscr("suC", [4, 128, NCTX], BF16)
        suL_d = dscr("suL", [512, NLAT], BF16)
        suG_d = dscr("suG", [1024, NLAT], BF16)
        wkC_d = dscr("wkC", [128, NCTX], BF16)
        wkL_d = dscr("wkL", [128, NLAT], BF16)
        wkG_d = dscr("wkG", [256, NLAT], BF16)
        wvC_d = dscr("wvC", [NCTX, 128], BF16)
        wvL_d = dscr("wvL", [NLAT, 128], BF16)
        wvG_d = dscr("wvG", [2 * NLAT, 128], BF16)
        ypLc_d = dscr("ypLc", [256, NCTX], BF16)
        ypLl_d = dscr("ypLl", [256, GLAT], BF16)
        ypGc_d = dscr("ypGc", [512, NCTX], BF16)
        ypGl_d = dscr("ypGl", [512, GLAT], BF16)
        ydT_d = dscr("ydT", [4, 128, NT], BF16)
        ysT_d = dscr("ysT", [4, 128, NT], BF16)
        ywT_d = dscr("ywT", [8, 64, NT], BF16)
        yf_d = dscr("yf", [2, 128, GNT], F32)
        modv_d = dscr("modv", [128, DEPTH, 48, 2], F32)

        uniq = [0]

        def sb(name, shape, dt, stack=st):
            uniq[0] += 1
            return T(nc, stack, "sb%d_%s" % (uniq[0], name), shape, dt)

        ident_f = sb("ident_f", [128, 128], F32)
        ident_b = sb("ident_b", [128, 128], BF16)
        ones_b = sb("ones_b", [128, 128], BF16)
        flg = sb("flags", [128, 2], F32)
        ccflag = sb("ccflag", [128, 4], F32)
        P.cc_sem = st.enter_context(nc.semaphore("cc_sem"))
        P.cc_flag = ccflag[:]
        P.cc_rg = [[2 * i, 2 * i + 1] for i in range(n_cores // 2)]
        modv = sb("modv_sb", [128, DEPTH, 48, 2], F32)
        gm1 = sb("gm1", [128, DEPTH, 8, 2], F32)
        gm2 = sb("gm2", [128, DEPTH, 8, 2], F32)
        psbig = st.enter_context(nc.psum_tensor("psbig", [128, 8, 512], F32))

        class PSV:
            def __init__(self, i):
                self.i = i
                self.b = Buf("ps%d" % i)

            def __getitem__(self, idx):
                return psbig[:, self.i, :][idx]

        ps = [PSV(i) for i in range(8)]

        def ps2(i):
            return psbig[:, i:i + 2, :].rearrange("p a b -> p (a b)")

        P.dma("sp", ident_f[:], IN["ident_f"], w=[ident_f.b])
        P.dma("sp", flg[:], IN["flags"], w=[flg.b])
        P.dma("pool", ident_b[:], IN["ident_f"], w=[ident_b.b])
        P.pool(lambda e: e.memset(ones_b[:], 1.0), w=[ones_b.b])

        with ExitStack() as ph:
            xt = [sb("xt%d" % i, [128, D], F32, ph) for i in range(2)]
            xT = [sb("xT%d" % i, [128, 8, 128], F32, ph) for i in range(2)]
            for t in range(NTILE):
                src = ctx_d[t * 128:(t + 1) * 128, :] if t < 2 else x_d[(t - 2) * 128:(t - 1) * 128, :]
                a = xt[t % 2]
                o = xT[t % 2]
                P.dma("sp", a[:], src, w=[a.b])
                for half in range(2):
                    pb = ps[(2 * t + half) % 4]
                    for q in range(4):
                        c = half * 4 + q
                        P.pe(lambda e, pb=pb, a=a, c=c, q=q: e.transpose(pb[:, q * 128:(q + 1) * 128], a[:, c * 128:(c + 1) * 128], ident_f[:]),
                             r=[a.b, ident_f.b], w=[pb.b])
                    if half == 0:
                        P.act(lambda e, pb=pb, o=o: e.activation(out=o[:, 0:4, :], in_=pb[:].rearrange("p (c t) -> p c t", c=4), func=AF.Copy),
                              r=[pb.b], w=[o.k(0)])
                    else:
                        P.dve(lambda e, pb=pb, o=o: e.tensor_copy(out=o[:, 4:8, :], in_=pb[:].rearrange("p (c t) -> p c t", c=4)),
                              r=[pb.b], w=[o.k(1)])
                P.dma("sp", hT_d.ap[:, :, t * 128:(t + 1) * 128].rearrange("c p t -> p c t"), o[:],
                      r=[o.k(0), o.k(1)], w=[hT_d.k(t)])

            cT = sb("cT", [128, 8, 2], F32, ph)
            scT = sb("scT", [128, 8, 2], F32, ph)
            P.dma("sp", cT[:, :, 0], c_d.rearrange("(c p) -> p c", p=128), w=[cT.b], allow_slow_non_contiguous=True)
            P.dma("sp", cT[:, :, 1], cc_d.rearrange("(c p) -> p c", p=128), w=[cT.b], allow_slow_non_contiguous=True)
            P.act(lambda e: e.activation(out=scT[:], in_=cT[:], func=AF.Silu), r=[cT.b], w=[scT.b])
            wm = [sb("wm%d" % i, [128, 8, 1024], F32, ph) for i in range(2)]
            bm = sb("bm", [128, DEPTH, 48], F32, ph)
            n1 = sb("n1g", [128, DEPTH, 8], F32, ph)
            n2 = sb("n2g", [128, DEPTH, 8], F32, ph)
            P.dma("sp", bm[:, :nl], IN["b_mod"].rearrange("l (j p) -> p l j", p=128), w=[bm.b], allow_slow_non_contiguous=True)
            P.dma("sp", n1[:, :nl], IN["norm1_g"].rearrange("l (j p) -> p l j", p=128), w=[n1.b], allow_slow_non_contiguous=True)
            P.dma("sp", n2[:, :nl], IN["norm2_g"].rearrange("l (j p) -> p l j", p=128), w=[n2.b], allow_slow_non_contiguous=True)
            pm = ps[4]
            it = 0
            for l in range(nl):
                for piece in range(6):
                    w_ = wm[it % 2]
                    it += 1
                    P.dma("sp", w_[:], IN["w_mod"][l, :, piece * 1024:(piece + 1) * 1024].rearrange("(c p) n -> p c n", p=128), w=[w_.b])
                    for jj in range(8):
                        j = piece * 8 + jj
                        for c in range(8):
                            P.pe(lambda e, w_=w_, jj=jj, c=c, j=j: e.matmul(pm[:, 2 * j:2 * j + 2], w_[:, c, jj * 128:(jj + 1) * 128], scT[:, c, :],
                                                                          start=(c == 0), stop=(c == 7)),
                                 r=[w_.b, scT.b], w=[pm.b])
                P.dve(lambda e, l=l: e.tensor_tensor(out=modv[:, l, :, :], in0=pm[:, 0:96].rearrange("p (j t) -> p j t", t=2),
                                                    in1=bm[:, l, :].unsqueeze(2).to_broadcast([128, 48, 2]), op=ALU.add),
                      r=[pm.b, bm.b], w=[modv.b])
                P.dve(lambda e, l=l: e.scalar_tensor_tensor(out=gm1[:, l, :, :], in0=modv[:, l, 8:16, :], scalar=1.0,
                                                           in1=n1[:, l, :].unsqueeze(2).to_broadcast([128, 8, 2]), op0=ALU.add, op1=ALU.mult),
                      r=[modv.b, n1.b], w=[gm1.b])
                P.dve(lambda e, l=l: e.scalar_tensor_tensor(out=gm2[:, l, :, :], in0=modv[:, l, 32:40, :], scalar=1.0,
                                                           in1=n2[:, l, :].unsqueeze(2).to_broadcast([128, 8, 2]), op0=ALU.add, op1=ALU.mult),
                      r=[modv.b, n2.b], w=[gm2.b])
            if "modv" in dbg:
                P.dma("sp", modv_d.ap, modv[:], r=[modv.b], w=[modv_d.b])
            P.barrier()

        def tiles_of(t0, N):
            return list(range(t0 // 128, (t0 + N) // 128))

        GROUPS = [("dq", 0, 512), ("dk", 512, 512), ("dv", 1024, 512), ("wq", 2048, 512), ("wkv", 2560, 256)]

        def phase_A(l):
            with ExitStack() as ph:
                win = sb("winA", [128, 8, 2816], BF16, ph)
                cos2 = sb("cos2", [128, LT, 64], F32, ph)
                sin2s = sb("sin2s", [128, LT, 64], F32, ph)
                P.dma("sp", cos2[:], IN["rope_cos"].rearrange("(t p) d -> p t d", p=128), w=[cos2.b])
                P.dma("sp", sin2s[:], IN["rope_sin"].rearrange("(t p) d -> p t d", p=128), w=[sin2s.b])
                for c in range(8):
                    rows = IN["w_in"][l, c * 128:(c + 1) * 128, :]
                    P.dma("pool", win[:, c, 0:2048], rows[:, 0:2048], w=[win.k(c)])
                    for kv in range(2):
                        P.dma("pool", win[:, c, 2048:2560].rearrange("p (g kv d) -> p kv g d", g=4, kv=2, d=64)[:, kv, :, :],
                              rows[:, 2048:2560].rearrange("p (kv g d) -> p kv g d", g=4, kv=2, d=64)[:, kv, :, :], w=[win.k(c)])
                    P.dma("pool", win[:, c, 2560:2816], rows[:, 2560:2816], w=[win.k(c)])
                graw = sb("graw", [128, 4, 64], F32, ph)
                gt = sb("gtab", [128, 4, 8, 64], F32, ph)
                for i, nm in enumerate(["diff_q_norm_g", "diff_k_norm_g", "win_q_norm_g", "win_k_norm_g"]):
                    P.dma("sp", graw[:, i, :], IN[nm][l].partition_broadcast(128), w=[graw.k(i)])
                    sc = 0.125 if i in (0, 2) else 1.0
                    P.dve(lambda e, i=i, sc=sc: e.tensor_scalar(out=gt[:, i, :, :], in0=graw[:, i, :].unsqueeze(1).to_broadcast([128, 8, 64]),
                                                              scalar1=sc, scalar2=None, op0=ALU.mult), r=[graw.k(i)], w=[gt.k(i)])
                hb = [sb("hb%d" % i, [128, 8, 512], F32, ph) for i in range(1)] * 2
                sq = sb("sq", [128, 8, 512], BF16, ph)
                rt = sb("rt", [128, 512], F32, ph)
                rstd = sb("rstd", [128, 512], F32, ph)
                tmp = [sb("tmp%d" % i, [128, 512], F32, ph) for i in range(2)]
                aT = [sb("aT%d" % i, [128, 8, 512], BF16, ph) for i in range(2)]
                NS = 4
                sqx = [sb("sqx%d" % i, [128, 512], F32, ph) for i in range(NS)]
                ss = [sb("ss%d" % i, [128, 8], F32, ph) for i in range(NS)]
                ss2 = [sb("ss2%d" % i, [128, 8], F32, ph) for i in range(NS)]
                rs = [sb("rs%d" % i, [128, 8], F32, ph) for i in range(NS)]
                xn = [sb("xn%d" % i, [128, 512], F32, ph) for i in range(NS)]
                xg = [sb("xg%d" % i, [128, 512], F32, ph) for i in range(NS)]
                r1 = [sb("r1%d" % i, [128, 512], F32, ph) for i in range(NS)]
                r2 = [sb("r2%d" % i, [128, 512], F32, ph) for i in range(NS)]
                qtm = [sb("qtm%d" % i, [128, 512], BF16, ph) for i in range(NS)]
                vtm = [sb("vtm%d" % i, [128, 512], BF16, ph) for i in range(2)]
                wvtm = [sb("wvtm%d" % i, [128, 128], BF16, ph) for i in range(2)]
                qdT_b = [sb("qdTb%d" % i, [128, 4, 512], BF16, ph) for i in range(2)]
                kdT_b = [sb("kdTb%d" % i, [128, 4, 512], BF16, ph) for i in range(2)]
                wqT_b = [sb("wqTb%d" % i, [128, 4, 512], BF16, ph) for i in range(2)]
                wkT_b = [sb("wkTb%d" % i, [128, 512], BF16, ph) for i in range(2)]
                suT_b = [sb("suTb%d" % i, [128, 4, 512], BF16, ph) for i in range(2)]
                pst = [ps[6], ps[7]]
                cnt = {"g": 0, "s": 0, "t": 0, "v": 0}

                def qk_post(pb, nh, gi, lat_tile):
                    W = nh * 64
                    si = cnt["s"] % NS
                    cnt["s"] += 1
                    v3 = lambda ap: ap.rearrange("p (h d) -> p h d", d=64)
                    P.act(lambda e: e.activation(out=sqx[si][:, :W], in_=pb[:, :W], func=AF.Square), r=[pb.b], w=[sqx[si].b])
                    P.dve(lambda e: e.tensor_reduce(out=ss[si][:, :nh], in_=v3(sqx[si][:, :W]), axis=AX.X, op=ALU.add), r=[sqx[si].b], w=[ss[si].b])
                    P.act(lambda e: e.activation(out=ss2[si][:, :nh], in_=ss[si][:, :nh], func=AF.Sqrt, scale=1.0 / 64, bias=EPS), r=[ss[si].b], w=[ss2[si].b])
                    P.dve(lambda e: e.reciprocal(out=rs[si][:, :nh], in_=ss2[si][:, :nh]), r=[ss2[si].b], w=[rs[si].b])
                    P.dve(lambda e: e.tensor_tensor(out=v3(xn[si][:, :W]), in0=v3(pb[:, :W]), in1=rs[si][:, :nh].unsqueeze(2).to_broadcast([128, nh, 64]), op=ALU.mult),
                          r=[pb.b, rs[si].b], w=[xn[si].b])
                    o = qtm[si]
                    if lat_tile is None:
                        P.dve(lambda e: e.tensor_tensor(out=v3(o[:, :W]), in0=v3(xn[si][:, :W]), in1=gt[:, gi, :nh, :], op=ALU.mult),
                               r=[xn[si].b, gt.k(gi)], w=[o.b])
                    else:
                        P.dve(lambda e: e.tensor_tensor(out=v3(xg[si][:, :W]), in0=v3(xn[si][:, :W]), in1=gt[:, gi, :nh, :], op=ALU.mult),
                               r=[xn[si].b, gt.k(gi)], w=[xg[si].b])
                        P.dve(lambda e: e.tensor_tensor(out=v3(r1[si][:, :W]), in0=v3(xg[si][:, :W]),
                                                        in1=cos2[:, lat_tile, :].unsqueeze(1).to_broadcast([128, nh, 64]), op=ALU.mult),
                              r=[xg[si].b, cos2.b], w=[r1[si].b])
                        v4 = lambda ap: ap.rearrange("p (h two d) -> p h two d", two=2, d=32)
                        P.dve(lambda e: e.tensor_tensor(out=v4(r2[si][:, :W]), in0=v4(xg[si][:, :W])[:, :, ::-1, :],
                                                         in1=sin2s[:, lat_tile, :].rearrange("p (two d) -> p two d", two=2).unsqueeze(1).to_broadcast([128, nh, 2, 32]),
                                                         op=ALU.mult),
                               r=[xg[si].b, sin2s.b], w=[r2[si].b])
                        P.dve(lambda e: e.tensor_tensor(out=o[:, :W], in0=r1[si][:, :W], in1=r2[si][:, :W], op=ALU.add),
                              r=[r1[si].b, r2[si].b], w=[o.b])
                    return o

                def transposes(src, nchunk, dst_ap, dst_tok):
                    pt = pst[cnt["t"] % 2]
                    use_act = cnt["t"] % 2 == 0
                    cnt["t"] += 1
                    ptv = pt[:].bitcast(BF16)
                    for k in range(nchunk):
                        P.pe(lambda e, k=k: e.transpose(ptv[:, k * 128:(k + 1) * 128], src[:, k * 128:(k + 1) * 128], ident_b[:]),
                             r=[src.b, ident_b.b], w=[pt.b])
                    inv = ptv[:, 0:nchunk * 128].rearrange("p (c t) -> p c t", c=nchunk)
                    if use_act:
                        P.act(lambda e: e.activation(out=dst_ap, in_=inv, func=AF.Copy), r=[pt.b], w=[dst_tok])
                    else:
                        P.dve(lambda e: e.tensor_copy(out=dst_ap, in_=inv), r=[pt.b], w=[dst_tok])

                for bi, (t0, N, isctx) in enumerate(BLOCKS):
                    h = hb[bi % 2]
                    a = aT[bi % 2]
                    mi = 1 if isctx else 0
                    P.dma("sp", h[:, :, :N], hT_d.ap[:, :, t0:t0 + N].rearrange("c p t -> p c t"),
                          r=[hT_d.k(t) for t in tiles_of(t0, N)], w=[h.b])
                    P.act(lambda e, h=h, N=N: e.activation(out=sq[:, :, :N], in_=h[:, :, :N], func=AF.Square), r=[h.b], w=[sq.b])
                    for c in range(8):
                        P.pe(lambda e, c=c, N=N: e.matmul(ps[0][:, :N], ones_b[:], sq[:, c, :N], start=(c == 0), stop=(c == 7)),
                             r=[sq.b, ones_b.b], w=[ps[0].b])
                    P.act(lambda e, N=N: e.activation(out=rt[:, :N], in_=ps[0][:, :N], func=AF.Sqrt, scale=1.0 / D, bias=EPS), r=[ps[0].b], w=[rt.b])
                    P.dve(lambda e, N=N: e.reciprocal(out=rstd[:, :N], in_=rt[:, :N]), r=[rt.b], w=[rstd.b])
                    for c in range(8):
                        tp = tmp[c % 2]
                        P.dve(lambda e, c=c, tp=tp, h=h, N=N, mi=mi: e.scalar_tensor_tensor(out=tp[:, :N], in0=h[:, c, :N], scalar=gm1[:, l, c, mi:mi + 1],
                                                                                       in1=rstd[:, :N], op0=ALU.mult, op1=ALU.mult),
                              r=[h.b, gm1.b, rstd.b], w=[tp.b])
                        P.act(lambda e, c=c, tp=tp, a=a, N=N, mi=mi: e.activation(out=a[:, c, :N], in_=tp[:, :N], func=AF.Identity,
                                                                             bias=modv[:, l, c, mi:mi + 1], scale=1.0),
                              r=[tp.b, modv.b], w=[a.k(c)])
                    P.dma("sp", aT_d.ap[:, :, t0:t0 + N].rearrange("c p t -> p c t"), a[:, :, :N],
                          r=[a.k(c) for c in range(8)], w=[aT_d.k(bi)])
                    qb, kb, wqb, wkb, sub = qdT_b[bi % 2], kdT_b[bi % 2], wqT_b[bi % 2], wkT_b[bi % 2], suT_b[bi % 2]
                    v3 = lambda ap: ap.rearrange("p (h d) -> p h d", d=64)
                    v4 = lambda ap: ap.rearrange("p (h two d) -> p h two d", two=2, d=32)
                    specs = [("dq", 8, 0), ("dk", 8, 1), ("wq", 8, 2), ("wkv", 2, 3)]
                    for j in range(N // 128):
                        tok = t0 // 128 + j
                        lat_tile = None if isctx else tok - 2
                        G = {}
                        for gi_, (kind, col0, W) in enumerate(GROUPS):
                            pb = ps[1 + gi_]
                            G[kind] = pb
                            for c in range(8):
                                P.pe(lambda e, pb=pb, c=c, j=j, a=a, col0=col0, W=W: e.matmul(pb[:, :W], a[:, c, j * 128:(j + 1) * 128], win[:, c, col0:col0 + W],
                                                                                         start=(c == 0), stop=(c == 7)),
                                     r=[a.k(c), win.k(c)], w=[pb.b])
                        vt = vtm[cnt["v"] % 2]
                        wt = wvtm[cnt["v"] % 2]
                        cnt["v"] += 1
                        pbv, pbw = G["dv"], G["wkv"]
                        P.act(lambda e: e.activation(out=vt[:], in_=pbv[:], func=AF.Copy), r=[pbv.b], w=[vt.b])
                        P.act(lambda e: e.activation(out=wt[:], in_=pbw[:, 128:256], func=AF.Copy), r=[pbw.b], w=[wt.b])
                        if isctx:
                            P.dma("sp", vdC_d.ap[tok * 128:(tok + 1) * 128, :], vt[:], r=[vt.b], w=[vdC_d.b])
                            P.dma("sp", wvC_d.ap[tok * 128:(tok + 1) * 128, :], wt[:], r=[wt.b], w=[wvC_d.b])
                        else:
                            P.dma("sp", vdL_d.ap[(tok - 2) * 128:(tok - 1) * 128, :], vt[:], r=[vt.b], w=[vdL_d.b])
                            P.dma("sp", wvL_d.ap[(tok - 2) * 128:(tok - 1) * 128, :], wt[:], r=[wt.b], w=[wvL_d.b])
                        for si, (kind, nh, gi) in enumerate(specs):
                            pb, W = G[kind], nh * 64
                            P.act(lambda e: e.activation(out=sqx[si][:, :W], in_=pb[:, :W], func=AF.Square), r=[pb.b], w=[sqx[si].b])
                        for si, (kind, nh, gi) in enumerate(specs):
                            W = nh * 64
                            P.dve(lambda e: e.tensor_reduce(out=ss[si][:, :nh], in_=v3(sqx[si][:, :W]), axis=AX.X, op=ALU.add), r=[sqx[si].b], w=[ss[si].b])
                        for si, (kind, nh, gi) in enumerate(specs):
                            P.act(lambda e: e.activation(out=ss2[si][:, :nh], in_=ss[si][:, :nh], func=AF.Sqrt, scale=1.0 / 64, bias=EPS), r=[ss[si].b], w=[ss2[si].b])
                        for si, (kind, nh, gi) in enumerate(specs):
                            P.dve(lambda e: e.reciprocal(out=rs[si][:, :nh], in_=ss2[si][:, :nh]), r=[ss2[si].b], w=[rs[si].b])
                        for si, (kind, nh, gi) in enumerate(specs):
                            pb, W = G[kind], nh * 64
                            P.dve(lambda e: e.tensor_tensor(out=v3(xn[si][:, :W]), in0=v3(pb[:, :W]), in1=rs[si][:, :nh].unsqueeze(2).to_broadcast([128, nh, 64]), op=ALU.mult),
                                  r=[pb.b, rs[si].b], w=[xn[si].b])
                        if lat_tile is None:
                            for si, (kind, nh, gi) in enumerate(specs):
                                W = nh * 64
                                P.dve(lambda e: e.tensor_tensor(out=v3(qtm[si][:, :W]), in0=v3(xn[si][:, :W]), in1=gt[:, gi, :nh, :], op=ALU.mult),
                                      r=[xn[si].b, gt.k(gi)], w=[qtm[si].b])
                        else:
                            for si, (kind, nh, gi) in enumerate(specs):
                                W = nh * 64
                                P.dve(lambda e: e.tensor_tensor(out=v3(xg[si][:, :W]), in0=v3(xn[si][:, :W]), in1=gt[:, gi, :nh, :], op=ALU.mult),
                                      r=[xn[si].b, gt.k(gi)], w=[xg[si].b])
                            for si, (kind, nh, gi) in enumerate(specs):
                                W = nh * 64
                                P.dve(lambda e: e.tensor_tensor(out=v3(r1[si][:, :W]), in0=v3(xg[si][:, :W]),
                                                                in1=cos2[:, lat_tile, :].unsqueeze(1).to_broadcast([128, nh, 64]), op=ALU.mult),
                                      r=[xg[si].b, cos2.b], w=[r1[si].b])
                            for si, (kind, nh, gi) in enumerate(specs):
                                W = nh * 64
                                P.dve(lambda e: e.tensor_tensor(out=v4(r2[si][:, :W]), in0=v4(xg[si][:, :W])[:, :, ::-1, :],
                                                                in1=sin2s[:, lat_tile, :].rearrange("p (two d) -> p two d", two=2).unsqueeze(1).to_broadcast([128, nh, 2, 32]),
                                                                op=ALU.mult),
                                      r=[xg[si].b, sin2s.b], w=[r2[si].b])
                            for si, (kind, nh, gi) in enumerate(specs):
                                W = nh * 64
                                P.dve(lambda e: e.tensor_tensor(out=qtm[si][:, :W], in0=r1[si][:, :W], in1=r2[si][:, :W], op=ALU.add),
                                      r=[r1[si].b, r2[si].b], w=[qtm[si].b])
                        pv6 = ps[6][:].bitcast(BF16)
                        pv7 = ps[7][:].bitcast(BF16)
                        for k in range(4):
                            P.pe(lambda e: e.transpose(pv6[:, k * 128:(k + 1) * 128], qtm[0][:, k * 128:(k + 1) * 128], ident_b[:]), r=[qtm[0].b, ident_b.b], w=[ps[6].b])
                        for k in range(4):
                            P.pe(lambda e: e.transpose(pv6[:, (4 + k) * 128:(5 + k) * 128], qtm[1][:, k * 128:(k + 1) * 128], ident_b[:]), r=[qtm[1].b, ident_b.b], w=[ps[6].b])
                        for k in range(4):
                            P.pe(lambda e: e.transpose(pv7[:, k * 128:(k + 1) * 128], qtm[2][:, k * 128:(k + 1) * 128], ident_b[:]), r=[qtm[2].b, ident_b.b], w=[ps[7].b])
                        P.pe(lambda e: e.transpose(pv7[:, 512:640], qtm[3][:, 0:128], ident_b[:]), r=[qtm[3].b, ident_b.b], w=[ps[7].b])
                        P.act(lambda e: e.activation(out=qb[:, :, j * 128:(j + 1) * 128], in_=pv6[:, 0:512].rearrange("p (c t) -> p c t", c=4), func=AF.Copy), r=[ps[6].b], w=[qb.k(j)])
                        P.act(lambda e: e.activation(out=kb[:, :, j * 128:(j + 1) * 128], in_=pv6[:, 512:1024].rearrange("p (c t) -> p c t", c=4), func=AF.Copy), r=[ps[6].b], w=[kb.k(j)])
                        P.dve(lambda e: e.tensor_copy(out=wqb[:, :, j * 128:(j + 1) * 128], in_=pv7[:, 0:512].rearrange("p (c t) -> p c t", c=4)), r=[ps[7].b], w=[wqb.k(j)])
                        P.dve(lambda e: e.tensor_copy(out=wkb[:, j * 128:(j + 1) * 128], in_=pv7[:, 512:640]), r=[ps[7].b], w=[wkb.k(j)])
                    for cc in range(4):
                        pb = ps[4 + cc % 2]
                        for c in range(8):
                            P.pe(lambda e, pb=pb, c=c, cc=cc, a=a, N=N: e.matmul(pb[:, :N], win[:, c, 1536 + cc * 128:1536 + (cc + 1) * 128], a[:, c, :N],
                                                                            start=(c == 0), stop=(c == 7)),
                                 r=[a.k(c), win.k(c)], w=[pb.b])
                        P.act(lambda e, pb=pb, cc=cc, sub=sub, N=N: e.activation(out=sub[:, cc, :N], in_=pb[:, :N], func=AF.Copy), r=[pb.b], w=[sub.k(cc)])
                    nj = N // 128
                    dview = lambda dt_: dt_.ap[:, :, t0:t0 + N].rearrange("c p t -> p c t")
                    P.dma("sp", dview(qdT_d), qb[:, :, :N], r=[qb.k(j) for j in range(nj)], w=[qdT_d.k(bi)])
                    l0 = t0 - NCTX
                    if isctx:
                        P.dma("sp", kdC_d.ap.rearrange("c p t -> p c t"), kb[:, :, :N], r=[kb.k(j) for j in range(nj)], w=[kdC_d.b])
                    else:
                        P.dma("sp", kdL_d.ap[:, l0:l0 + N].rearrange("(c p) t -> p c t", p=128), kb[:, :, :N], r=[kb.k(j) for j in range(nj)], w=[kdL_d.b])
                    P.dma("sp", dview(wqT_d), wqb[:, :, :N], r=[wqb.k(j) for j in range(nj)], w=[wqT_d.k(bi)])
                    if isctx:
                        P.dma("sp", wkC_d.ap, wkb[:, :N], r=[wkb.k(j) for j in range(nj)], w=[wkC_d.b])
                        P.dma("sp", suC_d.ap.rearrange("c p t -> p c t"), sub[:, :, :N], r=[sub.k(cc) for cc in range(4)], w=[suC_d.b])
                    else:
                        P.dma("sp", wkL_d.ap[:, l0:l0 + N], wkb[:, :N], r=[wkb.k(j) for j in range(nj)], w=[wkL_d.b])
                        P.dma("sp", suL_d.ap[:, l0:l0 + N].rearrange("(c p) t -> p c t", p=128), sub[:, :, :N], r=[sub.k(cc) for cc in range(4)], w=[suL_d.b])
                P.barrier()
                for a_, g_ in ((kdL_d, kdG_d), (vdL_d, vdG_d), (wkL_d, wkG_d), (wvL_d, wvG_d), (suL_d, suG_d)):
                    P.allgather(a_, g_, r=[a_.b], w=[g_.b])

        def phase_D(l):
            need_ctx = l < DEPTH - 1
            lam_init = 0.8 - 0.6 * math.exp(-0.3 * l)
            with ExitStack() as ph:
                kT = sb("kT", [128, 4, GNT], BF16, ph)
                V = sb("V", [128, GTILE, 512], BF16, ph)
                for c in range(4):
                    P.dma("sp", kT[:, c, 0:NCTX], kdC_d.ap[c], r=[kdC_d.b], w=[kT.k(c)])
                    for rk in range(2):
                        P.dma("sp", kT[:, c, NCTX + rk * NLAT:NCTX + (rk + 1) * NLAT], kdG_d.ap[rk * 512 + c * 128:rk * 512 + (c + 1) * 128, :],
                              r=[kdG_d.b], w=[kT.k(c)])
                P.dma("sp", V[:, 0:2, :], vdC_d.ap.rearrange("(t p) f -> p t f", p=128), r=[vdC_d.b], w=[V.k(0)])
                vv = vdG_d.ap.rearrange("(t p) f -> p t f", p=128)
                for q4 in range(0, 32, 8):
                    P.dma("sp", V[:, 2 + q4:2 + q4 + 8, :], vv[:, q4:q4 + 8, :], r=[vdG_d.b], w=[V.k(0)])
                Vtok = [V.k(0) for q4 in range(0, GTILE, 9)]
                lq = sb("lamv", [128, 4, 64], F32, ph)
                for i, nm in enumerate(["diff_lam_q1", "diff_lam_k1", "diff_lam_q2", "diff_lam_k2"]):
                    P.dma("sp", lq[:, i, :], IN[nm][l].partition_broadcast(128), w=[lq.b])
                lprod = sb("lprod", [128, 2, 64], F32, ph)
                lsum = sb("lsum", [128, 2], F32, ph)
                lexp = sb("lexp", [128, 2], F32, ph)
                nlam = sb("nlam", [128, 1], F32, ph)
                gout = sb("gout", [128, 1], F32, ph)
                P.dve(lambda e: e.tensor_tensor(out=lprod[:], in0=lq[:, 0::2, :], in1=lq[:, 1::2, :], op=ALU.mult), r=[lq.b], w=[lprod.b])
                P.dve(lambda e: e.tensor_reduce(out=lsum[:], in_=lprod[:], axis=AX.X, op=ALU.add), r=[lprod.b], w=[lsum.b])
                P.act(lambda e: e.activation(out=lexp[:], in_=lsum[:], func=AF.Exp), r=[lsum.b], w=[lexp.b])
                P.dve(lambda e: e.tensor_tensor(out=nlam[:], in0=lexp[:, 1:2], in1=lexp[:, 0:1], op=ALU.subtract), r=[lexp.b], w=[nlam.b])
                P.dve(lambda e: e.tensor_scalar(out=nlam[:], in0=nlam[:], scalar1=-lam_init, scalar2=None, op0=ALU.add), r=[nlam.b], w=[nlam.b])
                P.dma("sp", gout[:], IN["diff_out_norm_g"][l].rearrange("(p o) -> p o", o=1), w=[gout.b])
                P.dve(lambda e: e.tensor_scalar(out=gout[:], in0=gout[:], scalar1=1.0 - lam_init, scalar2=None, op0=ALU.mult), r=[gout.b], w=[gout.b])
                qTb = [sb("qTb%d" % i, [128, 4, 512], BF16, ph) for i in range(2)]
                ydb_ = [sb("ydb%d" % i, [128, 4, 512], BF16, ph) for i in range(2)]
                pT = [sb("pT%d" % i, [128, 2, 512], BF16, ph) for i in range(2)]
                rz = [sb("rz%d" % i, [128, 512], F32, ph) for i in range(2)]
                oz = sb("oz", [128, 4, 512], F32, ph)
                t1 = sb("t1", [128, 512], F32, ph)
                t2 = sb("t2", [128, 512], F32, ph)
                o_ = sb("o_", [128, 512], F32, ph)
                osq = sb("osq", [128, 512], BF16, ph)
                ort = sb("ort", [128, 512], F32, ph)
                orst = sb("orst", [128, 512], F32, ph)
                gi = [0]
                for bi, (t0, N, isctx) in enumerate(BLOCKS):
                    if isctx and not need_ctx:
                        continue
                    if dlimit is not None and bi not in dlimit:
                        continue
                    ktiles = [0, 1] if isctx else list(range(GTILE))
                    q = qTb[bi % 2]
                    ydb = ydb_[bi % 2]
                    P.dma("sp", q[:, :, :N], qdT_d.ap[:, :, t0:t0 + N].rearrange("c p t -> p c t"), r=[qdT_d.k(bi)], w=[q.b])
                    for h in range(4):
                        half = h % 2
                        items = [(kt, m) for kt in ktiles for m in range(2)]
                        base = gi[0]
                        gi[0] += len(items)

                        SB = [(ps[0], ps[1]), (ps[2], ps[3])]
                        npair = len(ktiles)
                        pbase = gi[0]
                        gi[0] += npair

                        def qk(i):
                            kt = ktiles[i]
                            for m in range(2):
                                c = 2 * m + h // 2
                                sbank = SB[(pbase + i) % 2][m]
                                P.pe(lambda e: e.matmul(sbank[:, :N], kT[half * 64:(half + 1) * 64, c, kt * 128:(kt + 1) * 128],
                                                        q[half * 64:(half + 1) * 64, c, :N], start=True, stop=True),
                                     r=[kT.k(c), q.b], w=[SB[(pbase + i) % 2][0].b])

                        def pv(i):
                            kt = ktiles[i]
                            sb2 = SB[(pbase + i) % 2]
                            p_ = pT[(pbase + i) % 2]
                            bk = 2 * ((pbase + i) % 2)
                            P.act(lambda e: e.activation(out=p_[:, :, :N], in_=psbig[:, bk:bk + 2, :N], func=AF.Exp), r=[sb2[0].b], w=[p_.b])
                            first = (i == 0)
                            last = (i == npair - 1)
                            for m in range(2):
                                P.pe(lambda e: e.matmul(ps[4 + m][:, :N], V[:, kt, h * 128:(h + 1) * 128], p_[:, m, :N], start=first, stop=last),
                                     r=[Vtok[kt // 9], p_.b], w=[ps[4].b])
                                P.pe(lambda e: e.matmul(ps[6 + m][:, :N], ones_b[:], p_[:, m, :N], start=first, stop=last),
                                     r=[ones_b.b, p_.b], w=[ps[4].b])

                        qk(0)
                        for i in range(npair):
                            if i + 1 < npair:
                                qk(i + 1)
                            pv(i)
                        P.dve(lambda e: e.tensor_copy(out=oz[:, :, :N], in_=psbig[:, 4:8, :N]), r=[ps[4].b], w=[oz.b])
                        P.dve(lambda e: e.reciprocal(out=rz[0][:, :N], in_=oz[:, 2, :N]), r=[oz.b], w=[rz[0].b])
                        P.dve(lambda e: e.reciprocal(out=rz[1][:, :N], in_=oz[:, 3, :N]), r=[oz.b], w=[rz[1].b])
                        P.dve(lambda e: e.tensor_tensor(out=t1[:, :N], in0=oz[:, 0, :N], in1=rz[0][:, :N], op=ALU.mult), r=[oz.b, rz[0].b], w=[t1.b])
                        P.dve(lambda e: e.tensor_tensor(out=t2[:, :N], in0=oz[:, 1, :N], in1=rz[1][:, :N], op=ALU.mult), r=[oz.b, rz[1].b], w=[t2.b])
                        P.dve(lambda e: e.scalar_tensor_tensor(out=o_[:, :N], in0=t2[:, :N], scalar=nlam[:, 0:1], in1=t1[:, :N], op0=ALU.mult, op1=ALU.add),
                              r=[t1.b, t2.b, nlam.b], w=[o_.b])
                        P.act(lambda e: e.activation(out=osq[:, :N], in_=o_[:, :N], func=AF.Square), r=[o_.b], w=[osq.b])
                        nb = SB[gi[0] % 2]
                        P.pe(lambda e: e.matmul(nb[1][:, :N], ones_b[:], osq[:, :N], start=True, stop=True), r=[ones_b.b, osq.b], w=[nb[0].b])
                        P.act(lambda e: e.activation(out=ort[:, :N], in_=nb[1][:, :N], func=AF.Sqrt, scale=1.0 / 128, bias=EPS), r=[nb[0].b], w=[ort.b])
                        gi[0] += 1
                        P.dve(lambda e: e.reciprocal(out=orst[:, :N], in_=ort[:, :N]), r=[ort.b], w=[orst.b])
                        P.dve(lambda e, h=h: e.scalar_tensor_tensor(out=ydb[:, h, :N], in0=o_[:, :N], scalar=gout[:, 0:1], in1=orst[:, :N], op0=ALU.mult, op1=ALU.mult),
                              r=[o_.b, gout.b, orst.b], w=[ydb.k(h)])
                    P.dma("sp", ydT_d.ap[:, :, t0:t0 + N].rearrange("c p t -> p c t"), ydb[:, :, :N], r=[ydb.k(h) for h in range(4)], w=[ydT_d.k(bi)])
                P.barrier()

        def phase_W(l):
            need_ctx = l < DEPTH - 1
            with ExitStack() as ph:
                ET = LT + 4
                wkT = sb("wkT", [128, ET * 128], BF16, ph)
                wv = sb("wv", [128, ET, 128], BF16, ph)
                P.dma("sp", wkT[:, 0:NCTX], wkC_d.ap, r=[wkC_d.b], w=[wkT.b])
                P.dma("sp", wkT[:, 2 * 128:3 * 128], wkG_d.ap[0:128, NLAT - 128:NLAT], r=[wkG_d.b], w=[wkT.b])
                P.dma("sp", wkT[:, 3 * 128:(3 + LT) * 128], wkL_d.ap, r=[wkL_d.b], w=[wkT.b])
                P.dma("sp", wkT[:, (3 + LT) * 128:(4 + LT) * 128], wkG_d.ap[128:256, 0:128], r=[wkG_d.b], w=[wkT.b])
                P.dma("sp", wv[:, 0:2, :], wvC_d.ap.rearrange("(t p) f -> p t f", p=128), r=[wvC_d.b], w=[wv.b])
                P.dma("sp", wv[:, 2, :], wvG_d.ap[NLAT - 128:NLAT, :], r=[wvG_d.b], w=[wv.b])
                P.dma("sp", wv[:, 3:3 + LT, :], wvL_d.ap.rearrange("(t p) f -> p t f", p=128), r=[wvL_d.b], w=[wv.b])
                P.dma("sp", wv[:, 3 + LT, :], wvG_d.ap[NLAT:NLAT + 128, :], r=[wvG_d.b], w=[wv.b])
                mk = sb("wmask", [128, 4, 128], BF16, ph)
                mkf = sb("wmaskf", [128, 2, 128], F32, ph)
                P.dma("pool", mk[:, 0, :], IN["mask_prev"], w=[mk.b])
                P.dma("pool", mk[:, 1, :], IN["mask_next"], w=[mk.b])
                P.dma("sp", mkf[:, 0, :], IN["mask_prev"], w=[mkf.b])
                P.dma("sp", mkf[:, 1, :], IN["mask_next"], w=[mkf.b])
                P.dve(lambda e: e.tensor_scalar(out=mk[:, 2, :], in0=mkf[:, 0, :], scalar1=flg[:, 1:2], scalar2=None, op0=ALU.mult), r=[mkf.b, flg.b, mk.b], w=[mk.b])
                P.dve(lambda e: e.tensor_scalar(out=mk[:, 3, :], in0=mkf[:, 1, :], scalar1=flg[:, 0:1], scalar2=None, op0=ALU.mult), r=[mkf.b, flg.b, mk.b], w=[mk.b])
                esink = sb("esink", [64, 8], F32, ph)
                P.dma("sp", esink[:], IN["win_sink"][l].partition_broadcast(64), w=[esink.b])
                P.act(lambda e: e.activation(out=esink[:], in_=esink[:], func=AF.Exp), r=[esink.b], w=[esink.b])
                wqb_ = [sb("wqb%d" % i, [128, 4, 512], BF16, ph) for i in range(2)]
                ywb_ = [sb("ywb%d" % i, [64, 8, 512], BF16, ph) for i in range(2)]
                pw = [sb("pw%d" % i, [128, 5, 512], BF16, ph) for i in range(2)]
                zs = sb("zs", [64, 512], F32, ph)
                rzw = sb("rzw", [64, 512], F32, ph)
                u = [0]
                for bi, (t0, N, isctx) in enumerate(BLOCKS):
                    if isctx and not need_ctx:
                        continue
                    wqb = wqb_[bi % 2]
                    ywb = ywb_[bi % 2]
                    P.dma("sp", wqb[:, :, :N], wqT_d.ap[:, :, t0:t0 + N].rearrange("g p t -> p g t"), r=[wqT_d.k(bi)], w=[wqb.b])
                    for j in range(N // 128):
                        Tt = t0 // 128 + j
                        if isctx:
                            keys = [(0, None), (1, None)]
                        else:
                            lt = Tt - 2
                            keys = [(0, None), (1, None), (2 + lt, 2 if lt == 0 else 0), (3 + lt, None), (4 + lt, 3 if lt == LT - 1 else 1)]
                        nk = len(keys)
                        for kv in range(2):
                            p_ = pw[u[0] % 2]
                            u[0] += 1
                            for idx, (kt, mm) in enumerate(keys):
                                P.pe(lambda e: e.matmul(ps[idx][:, :], wkT[kv * 64:(kv + 1) * 64, kt * 128:(kt + 1) * 128],
                                                        wqb[kv * 64:(kv + 1) * 64, :, j * 128:(j + 1) * 128], start=True, stop=True),
                                     r=[wkT.b, wqb.b], w=[ps[0].b])
                            for idx, (kt, mm) in enumerate(keys):
                                P.act(lambda e: e.activation(out=p_[:, idx, :], in_=ps[idx][:, :], func=AF.Exp), r=[ps[0].b], w=[p_.b])
                            for idx, (kt, mm) in enumerate(keys):
                                if mm is not None:
                                    P.pool(lambda e: e.tensor_tensor(out=p_[:, idx, :].rearrange("p (g q) -> p g q", g=4),
                                                                     in0=p_[:, idx, :].rearrange("p (g q) -> p g q", g=4),
                                                                     in1=mk[:, mm, :].unsqueeze(1).to_broadcast([128, 4, 128]), op=ALU.mult),
                                           r=[p_.b, mk.b], w=[p_.b])
                            for idx, (kt, mm) in enumerate(keys):
                                P.pe(lambda e: e.matmul(ps[5][0:64, :], wv[:, kt, kv * 64:(kv + 1) * 64], p_[:, idx, :], start=(idx == 0), stop=(idx == nk - 1)),
                                     r=[wv.b, p_.b], w=[ps[5].b])
                                P.pe(lambda e: e.matmul(ps[6][0:64, :], ones_b[:, 0:64], p_[:, idx, :], start=(idx == 0), stop=(idx == nk - 1)),
                                     r=[ones_b.b, p_.b], w=[ps[5].b])
                            P.dve(lambda e: e.tensor_tensor(out=zs[:].rearrange("p (g q) -> p g q", g=4), in0=ps[6][0:64, :].rearrange("p (g q) -> p g q", g=4),
                                                            in1=esink[:, kv * 4:(kv + 1) * 4].unsqueeze(2).to_broadcast([64, 4, 128]), op=ALU.add),
                                  r=[ps[5].b, esink.b], w=[zs.b])
                            P.dve(lambda e: e.reciprocal(out=rzw[:], in_=zs[:]), r=[zs.b], w=[rzw.b])
                            P.dve(lambda e: e.tensor_tensor(out=ywb[:, kv * 4:(kv + 1) * 4, j * 128:(j + 1) * 128],
                                                            in0=ps[5][0:64, :].rearrange("p (g q) -> p g q", g=4),
                                                            in1=rzw[:].rearrange("p (g q) -> p g q", g=4), op=ALU.mult),
                                  r=[ps[5].b, rzw.b], w=[ywb.k((j, kv))])
                    P.dma("sp", ywT_d.ap[:, :, t0:t0 + N].rearrange("h d t -> d h t"), ywb[:, :, :N],
                          r=[ywb.k((j, kv)) for j in range(N // 128) for kv in range(2)], w=[ywT_d.k(bi)])
                P.barrier()

        I32 = mybir.dt.int32
        C1 = 6.28125
        C2 = 2 * PI - C1

        NSC = 8

        def phase_S(l):
            need_ctx = l < DEPTH - 1
            with ExitStack() as ph:
                tok = Buf("s5prm")
                def sm(name, shape=(128, 2, NSC), dt=F32):
                    return sb(name, list(shape), dt, ph)
                lre, lim, ldt, dtt, th, rl, rr, are, aim = [sm(n) for n in ("lre", "lim", "ldt", "dtt", "th", "rl", "rr", "are", "aim")]
                den, nre, fre, fim, kre, kim, u1, u2, thj = [sm(n) for n in ("den", "nre", "fre", "fim", "kre", "kim", "u1", "u2", "thj")]
                P.dma("sp", lre[:], IN["s5_lambda_re"][l].rearrange("d (sc g2) p -> (g2 p) d sc", g2=2), w=[tok], allow_slow_non_contiguous=True)
                P.dma("sp", lim[:], IN["s5_lambda_im"][l].rearrange("d (sc g2) p -> (g2 p) d sc", g2=2), w=[tok], allow_slow_non_contiguous=True)
                ldv = IN["s5_log_dt"][l].rearrange("d (sc g2) -> g2 d sc", g2=2)
                for g2 in range(2):
                    P.dma("sp", ldt[g2 * 64:(g2 + 1) * 64, :, :], ldv[g2].partition_broadcast(64), w=[tok], allow_slow_non_contiguous=True)
                tau = sm("tau", (128, 2, J))
                gmk = sm("gmk", (128, 2, 8))
                P.dma("sp", tau[:], IN["tau"], w=[tok])
                P.dma("sp", gmk[:, 0, :], IN["gmask"], w=[tok])
                P.dve(lambda e: e.tensor_scalar(out=gmk[:, 1, :], in0=gmk[:, 0, :], scalar1=-1.0, scalar2=None, op0=ALU.mult), r=[tok], w=[tok])
                NE = NSC * J
                NP = 2 * NSC
                cosT = sb("cosT", [128, 2, NSC, J], F32, ph)
                sinT = sb("sinT", [128, 2, NSC, J], F32, ph)
                Rm = sb("Rm", [128, 2, NSC, J], F32, ph)
                cosB = sb("cosB", [128, 2, NSC, J], BF16, ph)
                sinB = sb("sinB", [128, 2, NSC, J], BF16, ph)
                BF = [sb("BblkF%d" % i, [128, 4, NSC, 128], BF16, ph) for i in range(2)]
                Cblk = sb("Cblk", [128, 4, NSC, 128], BF16, ph)
                dsk = sb("dsk", [128, 2], F32, ph)
                diagF = [sb("diagF%d" % i, [128, 2, 128], BF16, ph) for i in range(2)]
                pp = ExitStack()
                Bblk = sb("Bblk", [128, 4, NSC, 128], BF16, pp)
                diagD = sb("diagD", [128, 2, 128], F32, pp)
                rv = sb("rv", [128, NE], F32, pp)
                rki = sb("rki", [128, NE], I32, pp)
                rkf = sb("rkf", [128, NE], F32, pp)
                rm = sb("rm", [128, NE], F32, pp)
                ang = sb("ang", [128, NE], F32, pp)

                def sin_of(dst, src, n, add, rt, wt):
                    V_ = rv[:, :n]; KI = rki[:, :n]; KF = rkf[:, :n]; M_ = rm[:, :n]
                    tk = rv.b
                    P.dve(lambda e: e.tensor_scalar(out=V_, in0=src, scalar1=add, scalar2=1.0 / (2 * PI), op0=ALU.add, op1=ALU.mult), r=rt, w=[tk])
                    P.dve(lambda e: e.tensor_copy(out=KI, in_=V_), r=[tk], w=[tk])
                    P.dve(lambda e: e.tensor_copy(out=KF, in_=KI), r=[tk], w=[tk])
                    P.dve(lambda e: e.scalar_tensor_tensor(out=V_, in0=KF, scalar=-C1, in1=src, op0=ALU.mult, op1=ALU.add), r=[tk] + rt, w=[tk])
                    P.dve(lambda e: e.scalar_tensor_tensor(out=V_, in0=KF, scalar=-C2, in1=V_, op0=ALU.mult, op1=ALU.add), r=[tk], w=[tk])
                    if add != 0.0:
                        P.dve(lambda e: e.tensor_scalar(out=V_, in0=V_, scalar1=add, scalar2=None, op0=ALU.add), r=[tk], w=[tk])
                    P.dve(lambda e: e.tensor_scalar(out=M_, in0=V_, scalar1=PI, scalar2=-2 * PI, op0=ALU.is_gt, op1=ALU.mult), r=[tk], w=[tk])
                    P.dve(lambda e: e.tensor_tensor(out=V_, in0=V_, in1=M_, op=ALU.add), r=[tk], w=[tk])
                    P.dve(lambda e: e.tensor_scalar(out=M_, in0=V_, scalar1=-PI, scalar2=2 * PI, op0=ALU.is_lt, op1=ALU.mult), r=[tk], w=[tk])
                    P.dve(lambda e: e.tensor_tensor(out=V_, in0=V_, in1=M_, op=ALU.add), r=[tk], w=[tk])
                    P.act(lambda e: e.activation(out=dst, in_=V_, func=AF.Sin), r=[tk], w=wt)

                fl = lambda t_: t_[:].rearrange("p d s -> p (d s)")
                TT = lambda o_, a_, b_, op: P.dve(lambda e: e.tensor_tensor(out=fl(o_), in0=fl(a_), in1=fl(b_), op=op), r=[tok], w=[tok])
                P.act(lambda e: e.activation(out=fl(dtt), in_=fl(ldt), func=AF.Exp), r=[tok], w=[tok])
                TT(th, lim, dtt, ALU.mult)
                TT(rl, lre, dtt, ALU.mult)
                P.act(lambda e: e.activation(out=fl(rr), in_=fl(rl), func=AF.Exp), r=[tok], w=[tok])
                sin_of(fl(u1), fl(th), NP, 0.0, [tok], [tok])
                sin_of(fl(u2), fl(th), NP, PI / 2, [tok], [tok])
                TT(aim, rr, u1, ALU.mult)
                TT(are, rr, u2, ALU.mult)
                P.dve(lambda e: e.tensor_scalar(out=fl(thj), in0=fl(th), scalar1=float(J), scalar2=None, op0=ALU.mult), r=[tok], w=[tok])
                sin_of(fl(u1), fl(thj), NP, 0.0, [tok], [tok])
                sin_of(fl(u2), fl(thj), NP, PI / 2, [tok], [tok])
                TT(kim, rr, u1, ALU.mult)
                TT(kre, rr, u2, ALU.mult)
                TT(den, lre, lre, ALU.mult)
                TT(u1, lim, lim, ALU.mult)
                TT(den, den, u1, ALU.add)
                P.dve(lambda e: e.reciprocal(out=fl(den), in_=fl(den)), r=[tok], w=[tok])
                P.dve(lambda e: e.tensor_scalar(out=fl(nre), in0=fl(are), scalar1=-1.0, scalar2=None, op0=ALU.add), r=[tok], w=[tok])
                TT(u1, nre, lre, ALU.mult)
                TT(u2, aim, lim, ALU.mult)
                TT(fre, u1, u2, ALU.add)
                TT(fre, fre, den, ALU.mult)
                TT(u1, aim, lre, ALU.mult)
                TT(u2, nre, lim, ALU.mult)
                TT(fim, u1, u2, ALU.subtract)
                TT(fim, fim, den, ALU.mult)
                for d in range(2):
                    P.dve(lambda e: e.tensor_tensor(out=ang[:].rearrange("p (s t) -> p s t", s=NSC), in0=th[:, d, :].unsqueeze(2).to_broadcast([128, NSC, J]),
                                                    in1=tau[:, d, :].unsqueeze(1).to_broadcast([128, NSC, J]), op=ALU.mult), r=[tok, rv.b], w=[ang.b])
                    sin_of(sinT[:, d, :, :].rearrange("p s t -> p (s t)"), ang[:], NE, 0.0, [ang.b], [sinT.k(d)])
                    sin_of(cosT[:, d, :, :].rearrange("p s t -> p (s t)"), ang[:], NE, PI / 2, [ang.b], [cosT.k(d)])
                    P.dve(lambda e: e.tensor_copy(out=Rm[:, d, :, :], in_=rr[:, d, :].unsqueeze(2).to_broadcast([128, NSC, J])), r=[tok], w=[Rm.k(d)])
                    P.act(lambda e: e.activation(out=cosB[:, d, :, :], in_=cosT[:, d, :, :], func=AF.Copy), r=[cosT.k(d)], w=[cosB.k(d)])
                    P.act(lambda e: e.activation(out=sinB[:, d, :, :], in_=sinT[:, d, :, :], func=AF.Copy), r=[sinT.k(d)], w=[sinB.k(d)])
                    pos = 0 if d == 0 else J - 1
                    P.dve(lambda e: e.memset(Rm[:, d, :, pos:pos + 1], 0.0), r=[], w=[Rm.k(d)])
                XY = sb("XY", [128, 4, NSC, 128], BF16, pp)
                bnat = [sb("bnat%d" % i, [128, 2, NSC, 16], F32, pp) for i in range(2)]
                bb = [sb("bb%d" % i, [128, 2, NSC, 16], F32, pp) for i in range(2)]
                bt = [sb("bt%d" % i, [128, 2, NSC, 16], F32, pp) for i in range(2)]
                P.dma("sp", bnat[0][:], IN["s5_b_re"][l].rearrange("d (sc g2) p h -> (g2 p) d sc h", g2=2), w=[bnat[0].b])
                P.dma("sp", bnat[1][:], IN["s5_b_im"][l].rearrange("d (sc g2) p h -> (g2 p) d sc h", g2=2), w=[bnat[1].b])
                bc = lambda f_: f_[:].unsqueeze(3).to_broadcast([128, 2, NSC, 16])
                P.dve(lambda e: e.tensor_tensor(out=bt[0][:], in0=bnat[0][:], in1=bc(fre), op=ALU.mult), r=[bnat[0].b, tok], w=[bt[0].b])
                P.dve(lambda e: e.tensor_tensor(out=bt[1][:], in0=bnat[1][:], in1=bc(fim), op=ALU.mult), r=[bnat[1].b, tok], w=[bt[1].b])
                P.dve(lambda e: e.tensor_tensor(out=bb[0][:], in0=bt[0][:], in1=bt[1][:], op=ALU.subtract), r=[bt[0].b, bt[1].b], w=[bb[0].b])
                P.dve(lambda e: e.tensor_tensor(out=bt[0][:], in0=bnat[1][:], in1=bc(fre), op=ALU.mult), r=[bnat[1].b, tok, bb[0].b], w=[bt[0].b])
                P.dve(lambda e: e.tensor_tensor(out=bt[1][:], in0=bnat[0][:], in1=bc(fim), op=ALU.mult), r=[bnat[0].b, tok, bb[0].b], w=[bt[1].b])
                P.dve(lambda e: e.tensor_tensor(out=bb[1][:], in0=bt[0][:], in1=bt[1][:], op=ALU.add), r=[bt[0].b, bt[1].b], w=[bb[1].b])
                P.pool(lambda e: e.memset(XY[:], 0.0), w=[XY.b])
                for ri in range(2):
                    for j in range(4):
                        for g2 in range(2):
                            P.dve(lambda e: e.tensor_copy(out=XY[g2 * 64:(g2 + 1) * 64, ri::2, j::4, (2 * j + g2) * 16:(2 * j + g2 + 1) * 16],
                                                          in_=bb[ri][g2 * 64:(g2 + 1) * 64, :, j::4, :]), r=[bb[ri].b, XY.b], w=[XY.b])

                def xpose_all(dst):
                    for k in range(4):
                        pt = ps[6 + k % 2]
                        ptv = pt[:].bitcast(BF16)
                        for sc in range(NSC):
                            P.pe(lambda e: e.transpose(ptv[:, sc * 128:(sc + 1) * 128], XY[:, k, sc, :], ident_b[:]), r=[XY.b, ident_b.b], w=[pt.b])
                        P.act(lambda e: e.activation(out=dst[:, k, :, :], in_=ptv.rearrange("p (s c) -> p s c", s=NSC), func=AF.Copy),
                              r=[pt.b], w=[dst.b])

                xpose_all(Bblk)
                for w_ in range(2):
                    P.dve(lambda e: e.tensor_scalar(out=BF[w_][:].rearrange("p k s c -> p (k s c)"), in0=Bblk[:].rearrange("p k s c -> p (k s c)"),
                                                    scalar1=flg[:, w_:w_ + 1], scalar2=None, op0=ALU.mult), r=[Bblk.b, flg.b], w=[BF[w_].b])
                cnat = [sb("cnat%d" % i, [128, 2, 2, 64], F32, pp) for i in range(2)]
                P.dma("sp", cnat[0][:], IN["s5_c_re"][l].rearrange("d (cc gl) h p -> (gl h) d cc p", gl=8), w=[cnat[0].b])
                P.dma("sp", cnat[1][:], IN["s5_c_im"][l].rearrange("d (cc gl) h p -> (gl h) d cc p", gl=8), w=[cnat[1].b])
                for ri in range(2):
                    for j in range(4):
                        for g2 in range(2):
                            P.dve(lambda e: e.tensor_scalar(out=XY[:, ri::2, j::4, g2 * 64:(g2 + 1) * 64], in0=cnat[ri][:],
                                                            scalar1=gmk[:, ri, 2 * j + g2:2 * j + g2 + 1], scalar2=None, op0=ALU.mult),
                                  r=[cnat[ri].b, tok, XY.b, Bblk.b], w=[XY.b])
                xpose_all(Cblk)
                P.dma("sp", dsk[:], IN["s5_d"][l].rearrange("(cc p) -> p cc", p=128), w=[dsk.b], allow_slow_non_contiguous=True)
                for cc in range(2):
                    P.dve(lambda e: e.tensor_scalar(out=diagD[:, cc, :], in0=ident_f[:], scalar1=dsk[:, cc:cc + 1], scalar2=None, op0=ALU.mult),
                          r=[dsk.b, ident_f.b], w=[diagD.b])
                for w_ in range(2):
                    P.dve(lambda e: e.tensor_scalar(out=diagF[w_][:], in0=diagD[:], scalar1=flg[:, w_:w_ + 1], scalar2=None, op0=ALU.mult),
                          r=[diagD.b, flg.b], w=[diagF[w_].b])
                P.barrier()
                pp.close()
                uT_ = [sb("uT%d" % i, [128, 4, J], BF16, ph) for i in range(3)]
                car = [sb("car%d" % i, [128, NSC], F32, ph) for i in range(2)]
                NB2 = 2
                wre_ = [sb("wre%d" % i, [128, NSC, J], F32, ph) for i in range(NB2)]
                wim_ = [sb("wim%d" % i, [128, NSC, J], F32, ph) for i in range(NB2)]
                zre_ = [sb("zre%d" % i, [128, NSC, J], F32, ph) for i in range(NB2)]
                zim_ = [sb("zim%d" % i, [128, NSC, J], F32, ph) for i in range(NB2)]
                ta_ = [[sb("ta%d_%d" % (i, k), [128, NSC, J], BF16, ph) for k in range(4)] for i in range(NB2)]
                bub_ = [[sb("bub%d_%d" % (i, k), [128, NSC, J], BF16, ph) for k in range(2)] for i in range(NB2)]
                zb_ = [[sb("zb%d_%d" % (i, k), [128, NSC, J], BF16, ph) for k in range(2)] for i in range(NB2)]
                sre_ = [sb("sre%d" % i, [128, NSC, J], BF16, ph) for i in range(NB2)]
                sim_ = [sb("sim%d" % i, [128, NSC, J], BF16, ph) for i in range(NB2)]
                c4_ = [[sb("c4%d_%d" % (i, k), [128, NSC], F32, ph) for k in range(4)] for i in range(NB2)]
                yfs = [sb("yfs%d" % i, [128, 2, J], F32, ph) for i in range(2)]
                yfl = [sb("yfl%d" % i, [128, 2, J], F32, ph) for i in range(2)]
                ypo = [sb("ypo%d" % i, [128, 2, J], BF16, ph) for i in range(2)]
                F2 = lambda t_: t_[:].rearrange("p s t -> p (s t)")
                BUre, BUim = ps2(0), ps2(2)
                BUt = ps[0].b
                Yb = ps[4]
                jobs = []
                n_it = 0
                for d in range(2):
                    order = list(range(GTILE)) if d == 0 else [1, 0] + list(range(GTILE - 1, 1, -1))
                    for oi, Tt in enumerate(order):
                        jobs.append(dict(d=d, Tt=Tt, n_it=n_it, first=(oi == 0)))
                        n_it += 1

                def stageA(jb, q):
                    d, Tt, n_it = jb["d"], jb["Tt"], jb["n_it"]
                    uT = uT_[n_it % 3]
                    wre, wim, ta = wre_[q], wim_[q], ta_[q]
                    pos_first = 0 if d == 0 else J - 1
                    if jb["first"]:
                        P.dve(lambda e: e.memset(car[0][:], 0.0), w=[car[0].b])
                        P.dve(lambda e: e.memset(car[1][:], 0.0), w=[car[1].b])
                    if Tt < 2:
                        P.dma("sp", uT[:], suC_d.ap[:, :, Tt * J:(Tt + 1) * J].rearrange("c p t -> p c t"), r=[suC_d.b], w=[uT.b])
                    else:
                        gt = Tt - 2
                        rk, lt = gt // LT, gt % LT
                        P.dma("sp", uT[:], suG_d.ap[rk * 512:(rk + 1) * 512, lt * J:(lt + 1) * J].rearrange("(c p) t -> p c t", p=128), r=[suG_d.b], w=[uT.b])
                    for scl in range(NSC):
                        cc = scl // 4
                        for ri in range(2):
                            BU = BUre if ri == 0 else BUim
                            P.pe(lambda e: e.matmul(BU[:, scl * J:(scl + 1) * J], BF[0][:, d * 2 + ri, scl, :], uT[:, cc, :], start=True, stop=False),
                                 r=[BF[0].b, uT.b], w=[BUt])
                            P.pe(lambda e: e.matmul(BU[:, scl * J:(scl + 1) * J], BF[1][:, d * 2 + ri, scl, :], uT[:, 2 + cc, :], start=False, stop=True),
                                 r=[BF[1].b, uT.b], w=[BUt])
                    Cc = cosB[:, d, :, :].rearrange("p s t -> p (s t)")
                    Ss = sinB[:, d, :, :].rearrange("p s t -> p (s t)")
                    bub = bub_[q]
                    P.act(lambda e: e.activation(out=F2(bub[0]), in_=BUre, func=AF.Copy), r=[BUt], w=[bub[0].b])
                    P.act(lambda e: e.activation(out=F2(bub[1]), in_=BUim, func=AF.Copy), r=[BUt], w=[bub[1].b])
                    P.dve(lambda e: e.tensor_tensor(out=F2(ta[0]), in0=F2(bub[0]), in1=Cc, op=ALU.mult), r=[bub[0].b, cosB.k(d)], w=[ta[0].b])
                    P.dve(lambda e: e.tensor_tensor(out=F2(ta[1]), in0=F2(bub[1]), in1=Ss, op=ALU.mult), r=[bub[1].b, sinB.k(d)], w=[ta[1].b])
                    P.dve(lambda e: e.tensor_tensor(out=F2(ta[2]), in0=F2(bub[1]), in1=Cc, op=ALU.mult), r=[bub[1].b, cosB.k(d)], w=[ta[2].b])
                    P.dve(lambda e: e.tensor_tensor(out=F2(ta[3]), in0=F2(bub[0]), in1=Ss, op=ALU.mult), r=[bub[0].b, sinB.k(d)], w=[ta[3].b])
                    P.dve(lambda e: e.tensor_tensor(out=F2(wre), in0=F2(ta[0]), in1=F2(ta[1]), op=ALU.add), r=[ta[0].b, ta[1].b], w=[wre.b])
                    P.dve(lambda e: e.tensor_tensor(out=F2(wim), in0=F2(ta[2]), in1=F2(ta[3]), op=ALU.subtract), r=[ta[2].b, ta[3].b], w=[wim.b])
                    P.pool(lambda e: e.tensor_tensor(out=wre[:, :, pos_first:pos_first + 1], in0=wre[:, :, pos_first:pos_first + 1],
                                                     in1=car[0][:, :].unsqueeze(2), op=ALU.add), r=[wre.b, car[0].b], w=[wre.b])
                    P.pool(lambda e: e.tensor_tensor(out=wim[:, :, pos_first:pos_first + 1], in0=wim[:, :, pos_first:pos_first + 1],
                                                     in1=car[1][:, :].unsqueeze(2), op=ALU.add), r=[wim.b, car[1].b], w=[wim.b])

                def stageC(jb, q):
                    d = jb["d"]
                    wre, wim, zre, zim, c4 = wre_[q], wim_[q], zre_[q], zim_[q], c4_[q]
                    Rr = Rm[:, d, :, :].rearrange("p s t -> p (s t)")
                    pos_last = J - 1 if d == 0 else 0
                    if d == 0:
                        P.dve(lambda e: e.tensor_tensor_scan(out=F2(zre), data0=Rr, data1=F2(wre), initial=0.0, op0=ALU.mult, op1=ALU.add),
                              r=[wre.b, Rm.k(d)], w=[zre.b])
                        P.dve(lambda e: e.tensor_tensor_scan(out=F2(zim), data0=Rr, data1=F2(wim), initial=0.0, op0=ALU.mult, op1=ALU.add),
                              r=[wim.b, Rm.k(d)], w=[zim.b])
                    else:
                        P.dve(lambda e: e.tensor_tensor_scan(out=F2(zre)[:, ::-1], data0=Rr[:, ::-1], data1=F2(wre)[:, ::-1], initial=0.0,
                                                             op0=ALU.mult, op1=ALU.add), r=[wre.b, Rm.k(d)], w=[zre.b])
                        P.dve(lambda e: e.tensor_tensor_scan(out=F2(zim)[:, ::-1], data0=Rr[:, ::-1], data1=F2(wim)[:, ::-1], initial=0.0,
                                                             op0=ALU.mult, op1=ALU.add), r=[wim.b, Rm.k(d)], w=[zim.b])
                    zlr = zre[:, :, pos_last:pos_last + 1].rearrange("p s o -> p (s o)")
                    zli = zim[:, :, pos_last:pos_last + 1].rearrange("p s o -> p (s o)")
                    Kr = kre[:, d, :]
                    Ki = kim[:, d, :]
                    P.pool(lambda e: e.tensor_tensor(out=c4[0][:], in0=zlr, in1=Kr, op=ALU.mult), r=[zre.b, tok], w=[c4[0].b])
                    P.pool(lambda e: e.tensor_tensor(out=c4[1][:], in0=zli, in1=Ki, op=ALU.mult), r=[zim.b, tok], w=[c4[1].b])
                    P.pool(lambda e: e.tensor_tensor(out=c4[2][:], in0=zli, in1=Kr, op=ALU.mult), r=[zim.b, tok], w=[c4[2].b])
                    P.pool(lambda e: e.tensor_tensor(out=c4[3][:], in0=zlr, in1=Ki, op=ALU.mult), r=[zre.b, tok], w=[c4[3].b])
                    P.pool(lambda e: e.tensor_tensor(out=car[0][:], in0=c4[0][:], in1=c4[1][:], op=ALU.subtract), r=[c4[0].b, c4[1].b], w=[car[0].b])
                    P.pool(lambda e: e.tensor_tensor(out=car[1][:], in0=c4[2][:], in1=c4[3][:], op=ALU.add), r=[c4[2].b, c4[3].b], w=[car[1].b])

                def stageB(jb, q):
                    d, Tt, n_it = jb["d"], jb["Tt"], jb["n_it"]
                    uT = uT_[n_it % 3]
                    zre, zim, ta, sr, si_ = zre_[q], zim_[q], ta_[q], sre_[q], sim_[q]
                    Cc = cosB[:, d, :, :].rearrange("p s t -> p (s t)")
                    Ss = sinB[:, d, :, :].rearrange("p s t -> p (s t)")
                    zb = zb_[q]
                    P.act(lambda e: e.activation(out=F2(zb[0]), in_=F2(zre), func=AF.Copy), r=[zre.b], w=[zb[0].b])
                    P.act(lambda e: e.activation(out=F2(zb[1]), in_=F2(zim), func=AF.Copy), r=[zim.b], w=[zb[1].b])
                    P.dve(lambda e: e.tensor_tensor(out=F2(ta[0]), in0=F2(zb[0]), in1=Cc, op=ALU.mult), r=[zb[0].b, cosB.k(d)], w=[ta[0].b])
                    P.dve(lambda e: e.tensor_tensor(out=F2(ta[1]), in0=F2(zb[1]), in1=Ss, op=ALU.mult), r=[zb[1].b, sinB.k(d)], w=[ta[1].b])
                    P.dve(lambda e: e.tensor_tensor(out=F2(ta[2]), in0=F2(zb[0]), in1=Ss, op=ALU.mult), r=[zb[0].b, sinB.k(d)], w=[ta[2].b])
                    P.dve(lambda e: e.tensor_tensor(out=F2(ta[3]), in0=F2(zb[1]), in1=Cc, op=ALU.mult), r=[zb[1].b, cosB.k(d)], w=[ta[3].b])
                    P.dve(lambda e: e.tensor_tensor(out=F2(sr), in0=F2(ta[0]), in1=F2(ta[1]), op=ALU.subtract), r=[ta[0].b, ta[1].b], w=[sr.b])
                    P.dve(lambda e: e.tensor_tensor(out=F2(si_), in0=F2(ta[2]), in1=F2(ta[3]), op=ALU.add), r=[ta[2].b, ta[3].b], w=[si_.b])
                    for cc in range(2):
                        first = True
                        if d == 1:
                            P.pe(lambda e: e.matmul(Yb[:, cc * J:(cc + 1) * J], diagF[0][:, cc, :], uT[:, cc, :], start=True, stop=False),
                                 r=[diagF[0].b, uT.b], w=[Yb.b])
                            P.pe(lambda e: e.matmul(Yb[:, cc * J:(cc + 1) * J], diagF[1][:, cc, :], uT[:, 2 + cc, :], start=False, stop=False),
                                 r=[diagF[1].b, uT.b], w=[Yb.b])
                            first = False
                        for q4 in range(4):
                            scl = cc * 4 + q4
                            P.pe(lambda e: e.matmul(Yb[:, cc * J:(cc + 1) * J], Cblk[:, d * 2 + 0, scl, :], sr[:, scl, :], start=first, stop=False),
                                 r=[Cblk.b, sr.b], w=[Yb.b])
                            first = False
                            P.pe(lambda e: e.matmul(Yb[:, cc * J:(cc + 1) * J], Cblk[:, d * 2 + 1, scl, :], si_[:, scl, :], start=False, stop=(q4 == 3)),
                                 r=[Cblk.b, si_.b], w=[Yb.b])
                    Yv = Yb[:, 0:2 * J].rearrange("p (c t) -> p c t", c=2)
                    if d == 0:
                        ys_ = yfs[n_it % 2]
                        P.act(lambda e: e.activation(out=ys_[:], in_=Yv, func=AF.Copy), r=[Yb.b], w=[ys_.b])
                        P.dma("sp", yf_d.ap[:, :, Tt * J:(Tt + 1) * J].rearrange("c p t -> p c t"), ys_[:], r=[ys_.b], w=[yf_d.k(Tt)])
                    else:
                        yl = yfl[n_it % 2]
                        yo = ypo[n_it % 2]
                        P.dma("sp", yl[:], yf_d.ap[:, :, Tt * J:(Tt + 1) * J].rearrange("c p t -> p c t"), r=[yf_d.k(Tt)], w=[yl.b])
                        P.dve(lambda e: e.tensor_tensor(out=yo[:], in0=Yv, in1=yl[:], op=ALU.add), r=[Yb.b, yl.b], w=[yo.b])
                        if Tt < 2:
                            P.dma("sp", ypLc_d.ap[:, Tt * J:(Tt + 1) * J].rearrange("(c p) t -> p c t", p=128), yo[:], r=[yo.b], w=[ypLc_d.b])
                        else:
                            P.dma("sp", ypLl_d.ap[:, (Tt - 2) * J:(Tt - 1) * J].rearrange("(c p) t -> p c t", p=128), yo[:], r=[yo.b], w=[ypLl_d.b])

                for i, jb in enumerate(jobs):
                    stageA(jb, i % 2)
                    if i > 0:
                        stageB(jobs[i - 1], (i - 1) % 2)
                    stageC(jb, i % 2)
                stageB(jobs[-1], (len(jobs) - 1) % 2)
                P.barrier()
            P.allgather(ypLc_d, ypGc_d, r=[ypLc_d.b], w=[ypGc_d.b])
            P.allgather(ypLl_d, ypGl_d, r=[ypLl_d.b], w=[ypGl_d.b])
            with ExitStack() as ph:
                wglu = sb("wglu", [128, 4, 512], BF16, ph)
                P.dma("pool", wglu[:], IN["s5_w_glu"][l].rearrange("(c p) n -> p c n", p=128), w=[wglu.b])
                yA_ = [sb("yA%d" % i, [128, 4, 512], BF16, ph) for i in range(2)]
                yB_ = [sb("yB%d" % i, [128, 4, 512], BF16, ph) for i in range(2)]
                ysel = sb("ysel", [128, 4, 512], F32, ph)
                g_ = [sb("gg%d" % i, [128, 4, 512], BF16, ph) for i in range(2)]
                sg = [sb("sgg%d" % i, [128, 512], F32, ph) for i in range(2)]
                yo_ = [sb("yoo%d" % i, [128, 4, 512], BF16, ph) for i in range(2)]
                for bi, (t0, N, isctx) in enumerate(BLOCKS):
                    if isctx and not need_ctx:
                        continue
                    yA, yB, g, yo = yA_[bi % 2], yB_[bi % 2], g_[bi % 2], yo_[bi % 2]
                    if isctx:
                        P.dma("sp", yA[:, :, :N], ypGc_d.ap[:, 0:N].rearrange("(c p) t -> p c t", p=128), r=[ypGc_d.b], w=[yA.b])
                        P.act(lambda e: e.activation(out=g[:, :, :N], in_=yA[:, :, :N], func=AF.Gelu), r=[yA.b], w=[g.b])
                    else:
                        l0 = t0 - NCTX
                        P.dma("sp", yA[:, :, :N], ypGl_d.ap[:, l0:l0 + N].rearrange("(c p) t -> p c t", p=128), r=[ypGl_d.b], w=[yA.b])
                        P.dma("sp", yB[:, :, :N], ypGl_d.ap[:, NLAT + l0:NLAT + l0 + N].rearrange("(c p) t -> p c t", p=128), r=[ypGl_d.b], w=[yB.b])
                        P.dve(lambda e: e.tensor_scalar(out=ysel[:, :, :N], in0=yA[:, :, :N], scalar1=flg[:, 0:1], scalar2=None, op0=ALU.mult),
                              r=[yA.b, flg.b], w=[ysel.b])
                        P.dve(lambda e: e.scalar_tensor_tensor(out=ysel[:, :, :N], in0=yB[:, :, :N], scalar=flg[:, 1:2], in1=ysel[:, :, :N],
                                                               op0=ALU.mult, op1=ALU.add), r=[yB.b, flg.b, ysel.b], w=[ysel.b])
                        P.act(lambda e: e.activation(out=g[:, :, :N], in_=ysel[:, :, :N], func=AF.Gelu), r=[ysel.b], w=[g.b])
                    for oc in range(4):
                        Gb = ps[oc % 2]
                        for kc in range(4):
                            P.pe(lambda e: e.matmul(Gb[:, :N], wglu[:, kc, oc * 128:(oc + 1) * 128], g[:, kc, :N], start=(kc == 0), stop=(kc == 3)),
                                 r=[wglu.b, g.b], w=[Gb.b])
                        sg_ = sg[oc % 2]
                        P.act(lambda e: e.activation(out=sg_[:, :N], in_=Gb[:, :N], func=AF.Sigmoid), r=[Gb.b], w=[sg_.b])
                        P.dve(lambda e: e.tensor_tensor(out=yo[:, oc, :N], in0=g[:, oc, :N], in1=sg_[:, :N], op=ALU.mult), r=[g.b, sg_.b], w=[yo.k(oc)])
                    P.dma("sp", ysT_d.ap[:, :, t0:t0 + N].rearrange("c p t -> p c t"), yo[:, :, :N], r=[yo.k(oc) for oc in range(4)], w=[ysT_d.k(bi)])
                P.barrier()

        def phase_M(l):
            need_ctx = l < DEPTH - 1
            with ExitStack() as ph:
                wg = sb("wg", [128, 8, 3072], BF16, ph)
                wpd = sb("wpd", [128, 4, 1024], BF16, ph)
                wps = sb("wps", [128, 4, 1024], BF16, ph)
                wpw = sb("wpw", [64, 8, 1024], BF16, ph)
                wout = sb("wout", [128, 8, 1024], BF16, ph)
                for ocb in range(4):
                    for br in range(3):
                        c0 = br * 1024 + ocb * 256
                        P.dma("pool", wg[:, :, c0:c0 + 256], IN["w_in"][l, :, 2816 + c0:2816 + c0 + 256].rearrange("(c p) n -> p c n", p=128),
                              w=[wg.k((br, ocb))])
                P.dma("pool", wpd[:], IN["w_proj_diff"][l].rearrange("(c p) n -> p c n", p=128), w=[wpd.b])
                P.dma("pool", wps[:], IN["w_proj_s5"][l].rearrange("(c p) n -> p c n", p=128), w=[wps.b])
                P.dma("pool", wpw[:], IN["w_proj_win"][l].rearrange("(h d) n -> d h n", d=64), w=[wpw.b])
                for c in range(0, 8, 2):
                    P.dma("pool", wout[:, c:c + 2, :], IN["w_out"][l, c * 128:(c + 2) * 128, :].rearrange("(c p) n -> p c n", p=128), w=[wout.k(c // 2)])
                a_ = [sb("am%d" % i, [128, 8, 512], BF16, ph) for i in range(2)]
                yd_ = [sb("ydm%d" % i, [128, 4, 512], BF16, ph) for i in range(2)]
                ys_ = [sb("ysm%d" % i, [128, 4, 512], BF16, ph) for i in range(2)]
                yw_ = [sb("ywm%d" % i, [64, 8, 512], BF16, ph) for i in range(2)]
                h_ = [sb("hm%d" % i, [128, 8, 512], F32, ph) for i in range(2)]
                mT = sb("mT", [128, 8, 512], BF16, ph)
                sig = [sb("sig%d" % i, [128, 512], F32, ph) for i in range(3)]
                mt = [sb("mt%d" % i, [128, 512], F32, ph) for i in range(3)]
                gcnt = [0]
                for bi, (t0, N, isctx) in enumerate(BLOCKS):
                    if isctx and not need_ctx:
                        continue
                    mi = 1 if isctx else 0
                    a, yd, ys, yw, h = a_[bi % 2], yd_[bi % 2], ys_[bi % 2], yw_[bi % 2], h_[bi % 2]
                    dv = lambda dt_: dt_.ap[:, :, t0:t0 + N].rearrange("c p t -> p c t")
                    tl = tiles_of(t0, N)
                    P.dma("sp", a[:, :, :N], dv(aT_d), r=[aT_d.k(bi)], w=[a.b])
                    P.dma("sp", yd[:, :, :N], dv(ydT_d), r=[ydT_d.k(bi)], w=[yd.b])
                    P.dma("sp", ys[:, :, :N], dv(ysT_d), r=[ysT_d.k(bi)], w=[ys.b])
                    P.dma("sp", yw[:, :, :N], ywT_d.ap[:, :, t0:t0 + N].rearrange("h d t -> d h t"), r=[ywT_d.k(bi)], w=[yw.b])
                    P.dma("sp", h[:, :, :N], dv(hT_d), r=[hT_d.k(t) for t in tl], w=[h.b])
                    for oc in range(8):
                        for br in range(3):
                            G = ps[gcnt[0] % 2]
                            gcnt[0] += 1
                            Pj = ps[2 + br]
                            for c in range(8):
                                P.pe(lambda e: e.matmul(G[:, :N], wg[:, c, br * 1024 + oc * 128:br * 1024 + (oc + 1) * 128], a[:, c, :N], start=(c == 0), stop=(c == 7)),
                                     r=[wg.k((br, oc // 2)), a.b], w=[G.b])
                            P.act(lambda e: e.activation(out=sig[br][:, :N], in_=G[:, :N], func=AF.Sigmoid), r=[G.b], w=[sig[br].b])
                            if br == 0:
                                for k in range(4):
                                    P.pe(lambda e: e.matmul(Pj[:, :N], wpd[:, k, oc * 128:(oc + 1) * 128], yd[:, k, :N], start=(k == 0), stop=(k == 3)),
                                         r=[wpd.b, yd.b], w=[Pj.b])
                            elif br == 1:
                                for k in range(4):
                                    P.pe(lambda e: e.matmul(Pj[:, :N], wps[:, k, oc * 128:(oc + 1) * 128], ys[:, k, :N], start=(k == 0), stop=(k == 3)),
                                         r=[wps.b, ys.b], w=[Pj.b])
                            else:
                                for k in range(8):
                                    P.pe(lambda e: e.matmul(Pj[:, :N], wpw[:, k, oc * 128:(oc + 1) * 128], yw[:, k, :N], start=(k == 0), stop=(k == 7)),
                                         r=[wpw.b, yw.b], w=[Pj.b])
                            P.dve(lambda e: e.tensor_tensor(out=mt[br][:, :N], in0=Pj[:, :N], in1=sig[br][:, :N], op=ALU.mult), r=[Pj.b, sig[br].b], w=[mt[br].b])
                        P.dve(lambda e: e.tensor_tensor(out=mt[0][:, :N], in0=mt[0][:, :N], in1=mt[1][:, :N], op=ALU.add), r=[mt[0].b, mt[1].b], w=[mt[0].b])
                        P.dve(lambda e: e.tensor_tensor(out=mT[:, oc, :N], in0=mt[0][:, :N], in1=mt[2][:, :N], op=ALU.add), r=[mt[0].b, mt[2].b], w=[mT.k(oc)])
                    for oc in range(8):
                        O = ps[5 + oc % 2]
                        for c in range(8):
                            P.pe(lambda e: e.matmul(O[:, :N], wout[:, c, oc * 128:(oc + 1) * 128], mT[:, c, :N], start=(c == 0), stop=(c == 7)),
                                 r=[wout.k(c // 2), mT.k(c)], w=[O.b])
                        P.dve(lambda e: e.scalar_tensor_tensor(out=h[:, oc, :N], in0=O[:, :N], scalar=modv[:, l, 16 + oc, mi:mi + 1], in1=h[:, oc, :N],
                                                               op0=ALU.mult, op1=ALU.add), r=[O.b, modv.b, h.b], w=[h.b])
                    P.dma("sp", dv(hT_d), h[:, :, :N], r=[h.b], w=[hT_d.k(t) for t in tl])
                P.barrier()

        def phase_F(l, last):
            need_ctx = l < DEPTH - 1
            with ExitStack() as ph:
                w1 = sb("w1", [128, 8, 4096], BF16, ph)
                w2 = sb("w2", [128, 32, 1024], BF16, ph)
                for cb in range(8):
                    P.dma("pool", w1[:, :, cb * 512:(cb + 1) * 512], IN["w_ff1"][l, :, cb * 512:(cb + 1) * 512].rearrange("(c p) n -> p c n", p=128),
                          w=[w1.k(cb)])
                for c in range(0, 32, 4):
                    P.dma("pool", w2[:, c:c + 4, :], IN["w_ff2"][l, c * 128:(c + 4) * 128, :].rearrange("(c p) n -> p c n", p=128), w=[w2.k(c // 4)])
                FN = 512
                h = sb("hf", [128, 8, FN], F32, ph)
                sq = sb("sqf", [128, 8, FN], BF16, ph)
                rt = sb("rtf", [128, FN], F32, ph)
                rstd = sb("rstdf", [128, FN], F32, ph)
                tmp = [sb("tmpf%d" % i, [128, FN], F32, ph) for i in range(2)]
                fT = sb("fT", [128, 8, FN], BF16, ph)
                rl_ = [sb("rlf%d" % i, [128, FN], BF16, ph) for i in range(2)]
                hid = sb("hid", [128, 32, FN], BF16, ph)
                class OTV:
                    def __init__(self, i):
                        self.i = i
                        self.b = sq.b

                    def __getitem__(self, idx):
                        return sq[:].rearrange("p c t -> p (c t)").bitcast(F32)[:, self.i * 1024:(self.i + 1) * 1024][idx]

                    def k(self, key):
                        return sq.b
                ot = [OTV(i) for i in range(2)]
                cnt = [0, 0]
                for bi, (t0, N, isctx) in enumerate(BLOCKS):
                    if isctx and not need_ctx:
                        continue
                    mi = 1 if isctx else 0
                    tl = tiles_of(t0, N)
                    dv = lambda dt_: dt_.ap[:, :, t0:t0 + N].rearrange("c p t -> p c t")
                    P.dma("sp", h[:, :, :N], dv(hT_d), r=[hT_d.k(t) for t in tl], w=[h.b] + [h.k(oc) for oc in range(8)])
                    P.act(lambda e: e.activation(out=sq[:, :, :N], in_=h[:, :, :N], func=AF.Square), r=[h.b], w=[sq.b])
                    for c in range(8):
                        P.pe(lambda e: e.matmul(ps[0][:, :N], ones_b[:], sq[:, c, :N], start=(c == 0), stop=(c == 7)), r=[sq.b, ones_b.b], w=[ps[0].b])
                    P.act(lambda e: e.activation(out=rt[:, :N], in_=ps[0][:, :N], func=AF.Sqrt, scale=1.0 / D, bias=EPS), r=[ps[0].b], w=[rt.b])
                    P.dve(lambda e: e.reciprocal(out=rstd[:, :N], in_=rt[:, :N]), r=[rt.b], w=[rstd.b])
                    for c in range(8):
                        tp = tmp[c % 2]
                        P.dve(lambda e: e.scalar_tensor_tensor(out=tp[:, :N], in0=h[:, c, :N], scalar=gm2[:, l, c, mi:mi + 1], in1=rstd[:, :N], op0=ALU.mult, op1=ALU.mult),
                              r=[h.b, gm2.b, rstd.b], w=[tp.b])
                        P.act(lambda e: e.activation(out=fT[:, c, :N], in_=tp[:, :N], func=AF.Identity, bias=modv[:, l, 24 + c, mi:mi + 1], scale=1.0),
                              r=[tp.b, modv.b], w=[fT.k(c)])
                    for fc in range(32):
                        pb = ps[1 + fc % 3]
                        for c in range(8):
                            P.pe(lambda e: e.matmul(pb[:, :N], w1[:, c, fc * 128:(fc + 1) * 128], fT[:, c, :N], start=(c == 0), stop=(c == 7)),
                                 r=[w1.k(fc // 4), fT.k(c)], w=[pb.b])
                        rl = rl_[fc % 2]
                        P.act(lambda e: e.activation(out=rl[:, :N], in_=pb[:, :N], func=AF.Relu), r=[pb.b], w=[rl.b])
                        P.dve(lambda e: e.tensor_tensor(out=hid[:, fc, :N], in0=rl[:, :N], in1=rl[:, :N], op=ALU.mult), r=[rl.b], w=[hid.k(fc // 4)])
                    for oc in range(8):
                        O = ps[4 + oc % 2]
                        for fc in range(32):
                            P.pe(lambda e: e.matmul(O[:, :N], w2[:, fc, oc * 128:(oc + 1) * 128], hid[:, fc, :N], start=(fc == 0), stop=(fc == 31)),
                                 r=[w2.k(fc // 4), hid.k(fc // 4)], w=[O.b])
                        P.dve(lambda e: e.scalar_tensor_tensor(out=h[:, oc, :N], in0=O[:, :N], scalar=modv[:, l, 40 + oc, mi:mi + 1], in1=h[:, oc, :N],
                                                               op0=ALU.mult, op1=ALU.add), r=[O.b, modv.b, h.b], w=[h.k(oc)])
                    if not last:
                        P.dma("sp", dv(hT_d), h[:, :, :N], r=[h.k(oc) for oc in range(8)] + [h.b], w=[hT_d.k(t) for t in tl])
                    else:
                        for j in range(N // 128):
                            o = ot[cnt[0] % 2]
                            cnt[0] += 1
                            for half in range(2):
                                pt = ps[6 + half]
                                for q in range(4):
                                    c = half * 4 + q
                                    P.pe(lambda e: e.transpose(pt[:, q * 128:(q + 1) * 128], h[:, c, j * 128:(j + 1) * 128], ident_f[:]),
                                         r=[h.k(c), h.b, ident_f.b], w=[pt.b])
                                if half == 0:
                                    P.act(lambda e: e.activation(out=o[:, 0:512], in_=pt[:], func=AF.Copy), r=[pt.b], w=[o.k(0)])
                                else:
                                    P.dve(lambda e: e.tensor_copy(out=o[:, 512:1024], in_=pt[:]), r=[pt.b], w=[o.k(1)])
                            row = t0 - NCTX + j * 128
                            P.dma("sp", out_d.ap[row:row + 128, :], o[:], r=[o.k(0), o.k(1)], w=[out_d.k(row)])
                P.barrier()

        for l in range(nl):
            phase_A(l)
            if stop_after == "A":
                break
            if stop_after not in ("W", "S"):
                phase_D(l)
            if stop_after == "D":
                break
            if stop_after != "S":
                phase_W(l)
            if stop_after == "W":
                break
            phase_S(l)
            if stop_after == "S":
                break
            phase_M(l)
            if stop_after == "M":
                break
            phase_F(l, last=(l == nl - 1) and final_out)
        P.barrier()
        stats = P.finalize(st)
        print("build stats", stats)
    return nc


S5_GROUP_AXIS = {"s5_lambda_re": 2, "s5_lambda_im": 2, "s5_log_dt": 2, "s5_b_re": 2, "s5_b_im": 2, "s5_c_re": 2, "s5_c_im": 2}


def make_in_maps(inputs, n_cores, nl=DEPTH):
    consts = host_consts()
    maps = []
    for core in range(n_cores):
        b, h = core // 2, core % 2
        m = {"x": np.ascontiguousarray(inputs["x"][b][h * NLAT:(h + 1) * NLAT]), "ctx": np.ascontiguousarray(inputs["ctx"][b]),
             "c": np.ascontiguousarray(inputs["c"][b]), "c_ctx": np.ascontiguousarray(inputs["c_ctx"])}
        for name, _ in WEIGHT_SPECS:
            w = inputs[name][:nl]
            if name in S5_GROUP_AXIS:
                w = np.take(w, np.arange(h * 16, (h + 1) * 16), axis=S5_GROUP_AXIS[name])
            elif name == "s5_d":
                w = w[:, h * 256:(h + 1) * 256]
            m[name] = np.ascontiguousarray(w)
        for name, _ in CONST_SPECS:
            v = consts.get(name)
            if name in ("rope_cos", "rope_sin"):
                v = np.ascontiguousarray(v[h * NLAT:(h + 1) * NLAT])
            elif name == "flags":
                v = np.zeros((128, 2), np.float32)
                v[:, h] = 1.0
            m[name] = v
        maps.append(m)
    return maps


def kernel(**inputs):
    n = 8
    nc = build(n_cores=n)
    res = run_bass_kernel_spmd(nc, make_in_maps(inputs, n), core_ids=list(range(n)))
    return np.stack([np.concatenate([res.results[2 * b]["out"], res.results[2 * b + 1]["out"]], 0) for b in range(4)], 0).astype(np.float32)
```

```python
import math
import types
import numpy as np
from contextlib import ExitStack
import concourse.bass as bass
import concourse.mybir as mybir
from concourse.bass_utils import run_bass_kernel_spmd

F32 = mybir.dt.float32
BF16 = mybir.dt.bfloat16
ALU = mybir.AluOpType
AF = mybir.ActivationFunctionType
AX = mybir.AxisListType

D = 1024
NCTX = 256
NLAT = 2048
NT = NCTX + NLAT
NTILE = NT // 128
GLAT = 4096
GNT = NCTX + GLAT
GTILE = GNT // 128
LT = NLAT // 128
DEPTH = 4
EPS = 1e-6
DIN = 5888
J = 128
PI = math.pi


class Buf:
    __slots__ = ("name", "w", "rc", "rd")

    def __init__(self, name=""):
        self.name = name
        self.w = None
        self.rc = {}
        self.rd = []


class Op:
    __slots__ = ("eng", "fn", "r", "w", "dma", "deps", "waits", "need_inc", "ev", "idx", "bar")

    def __init__(self, eng, fn, r, w, dma):
        self.eng = eng
        self.fn = fn
        self.r = r
        self.w = w
        self.dma = dma
        self.need_inc = False
        self.ev = None
        self.waits = None
        self.bar = False


class Prog:
    NDMA = 12
    LIMIT = 30000

    def __init__(self, nc):
        self.nc = nc
        self.ops = []
        self.cc_n = 0
        self.engs = {"pe": nc.tensor, "act": nc.scalar, "dve": nc.vector,
                     "pool": nc.gpsimd, "sp": nc.sync}

    @staticmethod
    def _freeze(fn):
        if fn is None or fn.__closure__ is None:
            return fn
        cells = []
        for c in fn.__closure__:
            try:
                cells.append(types.CellType(c.cell_contents))
            except ValueError:
                cells.append(c)
        return types.FunctionType(fn.__code__, fn.__globals__, fn.__name__, fn.__defaults__, tuple(cells))

    def add(self, eng, fn, r=(), w=(), dma=False):
        op = Op(eng, self._freeze(fn), tuple(r), tuple(w), dma)
        op.idx = len(self.ops)
        self.ops.append(op)
        return op

    def pe(self, fn, r=(), w=()): return self.add("pe", fn, r, w)
    def act(self, fn, r=(), w=()): return self.add("act", fn, r, w)
    def dve(self, fn, r=(), w=()): return self.add("dve", fn, r, w)
    def pool(self, fn, r=(), w=()): return self.add("pool", fn, r, w)

    def dma(self, q, out, in_, r=(), w=(), **kw):
        return self.add(q, lambda e: e.dma_start(out=out, in_=in_, **kw), r, w, dma=True)

    def allgather(self, in_dt, out_dt, r, w):
        self.cc_n += 1
        n = self.cc_n
        sem, flag, rg = self.cc_sem, self.cc_flag, self.cc_rg
        ia, oa = in_dt.ap, out_dt.ap

        def fn(E):
            E.collective_compute("AllGather", ALU.bypass, replica_groups=rg, ins=[ia.opt()], outs=[oa.opt()]).then_inc(sem)
            E.wait_ge(sem, n)
            return E.memset(flag, 0.0)
        return self.add("pool", fn, r, w)

    def barrier(self):
        for e in self.engs:
            op = self.add(e, None)
            op.bar = True

    def finalize(self, stack):
        nc = self.nc
        ops = self.ops
        dma_rr = {}
        dma_last = {}
        last_c = {}
        dma_since = []
        for i, op in enumerate(ops):
            deps = set()
            if op.bar:
                deps.update(last_c.values())
                deps.update(dma_since)
            for b in op.r:
                if b.w is not None:
                    deps.add(b.w)
            for b in op.w:
                if b.w is not None:
                    deps.add(b.w)
                deps.update(b.rc.values())
                deps.update(b.rd)
            for b in op.r:
                if op.dma:
                    b.rd.append(i)
                else:
                    b.rc[op.eng] = i
            for b in op.w:
                b.w = i
                b.rc = {}
                b.rd = []
            if op.dma:
                k = dma_rr.get(op.eng, 0)
                dma_rr[op.eng] = k + 1
                slot = (op.eng, k % self.NDMA)
                if slot in dma_last:
                    deps.add(dma_last[slot])
                dma_last[slot] = i
                op.ev = slot
                dma_since.append(i)
                if len(dma_since) > 3 * self.NDMA:
                    dma_since = dma_since[-3 * self.NDMA:]
            elif op.fn is not None:
                last_c[op.eng] = i
            deps.discard(i)
            op.deps = deps
        known_c = {e: {} for e in self.engs}
        known_d = {e: {} for e in self.engs}
        dma_cnt = {}
        for i, op in enumerate(ops):
            X = op.eng
            cand = {}
            dwaits = []
            for d in op.deps:
                od = ops[d]
                if od.dma:
                    slot, val = od.ev
                    if known_d[X].get(slot, 0) < val:
                        known_d[X][slot] = val
                        dwaits.append((slot, val))
                else:
                    if od.eng == "pe" and X == "pe" and not op.dma and not op.bar:
                        continue
                    if cand.get(od.eng, -1) < d:
                        cand[od.eng] = d
            cw = []
            for Y, d in cand.items():
                if known_c[X].get(Y, -1) < d:
                    known_c[X][Y] = d
                    ops[d].need_inc = True
                    cw.append(d)
            op.waits = (cw, dwaits)
            if op.dma:
                slot = op.ev
                v = dma_cnt.get(slot, 0) + 16
                dma_cnt[slot] = v
                op.ev = (slot, v)
        sem_c = {}
        cnt = {e: 0 for e in self.engs}
        epoch = {e: 0 for e in self.engs}

        def get_sem(key):
            if key not in sem_c:
                sem_c[key] = stack.enter_context(nc.semaphore("s_%s_%s" % key))
            return sem_c[key]

        for op in ops:
            if op.dma or op.fn is None:
                continue
            if op.need_inc:
                e = op.eng
                if cnt[e] >= self.LIMIT:
                    cnt[e] = 0
                    epoch[e] += 1
                cnt[e] += 1
                op.ev = ((e, "c%d" % epoch[e]), cnt[e])
        n_wait = 0
        for op in ops:
            E = self.engs[op.eng]
            cw, dwaits = op.waits
            for d in cw:
                key, val = ops[d].ev
                E.wait_ge(get_sem(key), val)
                n_wait += 1
            for slot, val in dwaits:
                E.wait_ge(get_sem((slot[0], "d%d" % slot[1])), val)
                n_wait += 1
            if op.fn is None:
                continue
            ins = op.fn(E)
            if op.dma:
                slot, val = op.ev
                ins.then_inc(get_sem((slot[0], "d%d" % slot[1])), 16)
            elif op.need_inc:
                key, val = op.ev
                ins.then_inc(get_sem(key), 1)
        self.stats = dict(n_ops=len(ops), n_wait=n_wait, n_sem=len(sem_c),
                          n_inc=sum(1 for o in ops if o.need_inc))
        return self.stats


class T:
    def __init__(self, nc, st, name, shape, dt, psum=False):
        if psum:
            self.t = st.enter_context(nc.psum_tensor(name, shape, dt))
        else:
            self.t = st.enter_context(nc.sbuf_tensor(name, shape, dt))
        self.b = Buf(name)
        self.sub = {}
        self.name = name

    def __getitem__(self, idx):
        return self.t[idx]

    def k(self, key):
        if key not in self.sub:
            self.sub[key] = Buf("%s.%s" % (self.name, key))
        return self.sub[key]


class DT:
    def __init__(self, ap, name):
        self.ap = ap
        self.name = name
        self.b = Buf(name)
        self.sub = {}

    def k(self, key):
        if key not in self.sub:
            self.sub[key] = Buf("%s.%s" % (self.name, key))
        return self.sub[key]


BLOCKS = [(0, NCTX, True)] + [(NCTX + 512 * i, 512, False) for i in range(NLAT // 512)]

WEIGHT_SPECS = [
    ("w_mod", [DEPTH, D, 6 * D]), ("b_mod", [DEPTH, 6 * D]), ("norm1_g", [DEPTH, D]), ("norm2_g", [DEPTH, D]),
    ("w_in", [DEPTH, D, DIN]),
    ("diff_q_norm_g", [DEPTH, 64]), ("diff_k_norm_g", [DEPTH, 64]),
    ("diff_lam_q1", [DEPTH, 64]), ("diff_lam_k1", [DEPTH, 64]), ("diff_lam_q2", [DEPTH, 64]), ("diff_lam_k2", [DEPTH, 64]),
    ("diff_out_norm_g", [DEPTH, 128]),
    ("s5_lambda_re", [DEPTH, 2, 16, 64]), ("s5_lambda_im", [DEPTH, 2, 16, 64]), ("s5_log_dt", [DEPTH, 2, 16]),
    ("s5_b_re", [DEPTH, 2, 16, 64, 16]), ("s5_b_im", [DEPTH, 2, 16, 64, 16]),
    ("s5_c_re", [DEPTH, 2, 16, 16, 64]), ("s5_c_im", [DEPTH, 2, 16, 16, 64]),
    ("s5_d", [DEPTH, 256]), ("s5_w_glu", [DEPTH, 512, 512]),
    ("win_q_norm_g", [DEPTH, 64]), ("win_k_norm_g", [DEPTH, 64]), ("win_sink", [DEPTH, 8]),
    ("w_proj_diff", [DEPTH, 512, D]), ("w_proj_s5", [DEPTH, 512, D]), ("w_proj_win", [DEPTH, 512, D]),
    ("w_out", [DEPTH, D, D]), ("w_ff1", [DEPTH, D, 4 * D]), ("w_ff2", [DEPTH, 4 * D, D]),
]


def host_consts():
    c = {}
    c["ident_f"] = np.eye(128, dtype=np.float32)
    n_freq = 16
    inv = (10000.0 ** (-np.arange(n_freq, dtype=np.float32) / n_freq)).astype(np.float32)
    r = np.repeat(np.arange(64, dtype=np.float32), 64)
    col = np.tile(np.arange(64, dtype=np.float32), 64)
    ang = np.concatenate([r[:, None] * inv, col[:, None] * inv], -1).astype(np.float32)
    cs, sn = np.cos(ang).astype(np.float32), np.sin(ang).astype(np.float32)
    c["rope_cos"] = np.concatenate([cs, cs], -1)
    c["rope_sin"] = np.concatenate([-sn, sn], -1)
    kk = np.arange(128)[:, None]
    qq = np.arange(128)[None, :]
    c["mask_prev"] = (kk >= qq).astype(np.float32)
    c["mask_next"] = (kk <= qq).astype(np.float32)
    tau = np.stack([np.arange(J), np.arange(J)[::-1]], 0).astype(np.float32)
    c["tau"] = np.broadcast_to(tau[None], (128, 2, J)).copy()
    gm = np.zeros((128, 8), np.float32)
    gm[np.arange(128), np.arange(128) // 16] = 1.0
    c["gmask"] = gm
    return c


CONST_SPECS = [("ident_f", [128, 128]), ("rope_cos", [NLAT, 64]), ("rope_sin", [NLAT, 64]),
               ("mask_prev", [128, 128]), ("mask_next", [128, 128]), ("tau", [128, 2, J]), ("gmask", [128, 8]), ("flags", [128, 2])]


def build(nl=DEPTH, dbg=(), stop_after=None, dlimit=None, final_out=True, n_cores=8):
    nc = bass.Bass("TRN2", target_bir_lowering=False)
    st = ExitStack()
    with st:
        P = Prog(nc)
        IN = {}

        def din(name, shape):
            IN[name] = nc.dram_tensor(name, list(shape), F32, kind="ExternalInput").ap()
            return IN[name]

        x_d = din("x", [NLAT, D])
        ctx_d = din("ctx", [NCTX, D])
        c_d = din("c", [D])
        cc_d = din("c_ctx", [D])
        for name, shape in WEIGHT_SPECS:
            din(name, [nl] + list(shape[1:]))
        for name, shape in CONST_SPECS:
            din(name, shape)
        out_d = DT(nc.dram_tensor("out", [NLAT, D], F32, kind="ExternalOutput").ap(), "out")

        def dscr(name, shape, dt):
            kind = "ExternalOutput" if name in dbg else "Internal"
            return DT(nc.dram_tensor(name, list(shape), dt, kind=kind).ap(), name)

        hT_d = dscr("hT", [8, 128, NT], F32)
        aT_d = dscr("aT", [8, 128, NT], BF16)
        qdT_d = dscr("qdT", [4, 128, NT], BF16)
        wqT_d = dscr("wqT", [4, 128, NT], BF16)
        kdC_d = dscr("kdC", [4, 128, NCTX], BF16)
        kdL_d = dscr("kdL", [512, NLAT], BF16)
        kdG_d = dscr("kdG", [1024, NLAT], BF16)
        vdC_d = dscr("vdC", [NCTX, 512], BF16)
        vdL_d = dscr("vdL", [NLAT, 512], BF16)
        vdG_d = dscr("vdG", [2 * NLAT, 512], BF16)
        suC_d = dscr("suC", [4, 128, NCTX], BF16)
        suL_d = dscr("suL", [512, NLAT], BF16)
        suG_d = dscr("suG", [1024, NLAT], BF16)
        wkC_d = dscr("wkC", [128, NCTX], BF16)
        wkL_d = dscr("wkL", [128, NLAT], BF16)
        wkG_d = dscr("wkG", [256, NLAT], BF16)
        wvC_d = dscr("wvC", [NCTX, 128], BF16)
        wvL_d = dscr("wvL", [NLAT, 128], BF16)
        wvG_d = dscr("wvG", [2 * NLAT, 128], BF16)
        ypLc_d = dscr("ypLc", [256, NCTX], BF16)
        ypLl_d = dscr("ypLl", [256, GLAT], BF16)
        ypGc_d = dscr("ypGc", [512, NCTX], BF16)
        ypGl_d = dscr("ypGl", [512, GLAT], BF16)
        ydT_d = dscr("ydT", [4, 128, NT], BF16)
        ysT_d = dscr("ysT", [4, 128, NT], BF16)
        ywT_d = dscr("ywT", [8, 64, NT], BF16)
        yf_d = dscr("yf", [2, 128, GNT], F32)
        modv_d = dscr("modv", [128, DEPTH, 48, 2], F32)

        uniq = [0]

        def sb(name, shape, dt, stack=st):
            uniq[0] += 1
            return T(nc, stack, "sb%d_%s" % (uniq[0], name), shape, dt)

        ident_f = sb("ident_f", [128, 128], F32)
        ident_b = sb("ident_b", [128, 128], BF16)
        ones_b = sb("ones_b", [128, 128], BF16)
        flg = sb("flags", [128, 2], F32)
        ccflag = sb("ccflag", [128, 4], F32)
        P.cc_sem = st.enter_context(nc.semaphore("cc_sem"))
        P.cc_flag = ccflag[:]
        P.cc_rg = [[2 * i, 2 * i + 1] for i in range(n_cores // 2)]
        modv = sb("modv_sb", [128, DEPTH, 48, 2], F32)
        gm1 = sb("gm1", [128, DEPTH, 8, 2], F32)
        gm2 = sb("gm2", [128, DEPTH, 8, 2], F32)
        psbig = st.enter_context(nc.psum_tensor("psbig", [128, 8, 512], F32))

        class PSV:
            def __init__(self, i):
                self.i = i
                self.b = Buf("ps%d" % i)

            def __getitem__(self, idx):
                return psbig[:, self.i, :][idx]

        ps = [PSV(i) for i in range(8)]

        def ps2(i):
            return psbig[:, i:i + 2, :].rearrange("p a b -> p (a b)")

        P.dma("sp", ident_f[:], IN["ident_f"], w=[ident_f.b])
        P.dma("sp", flg[:], IN["flags"], w=[flg.b])
        P.dma("pool", ident_b[:], IN["ident_f"], w=[ident_b.b])
        P.pool(lambda e: e.memset(ones_b[:], 1.0), w=[ones_b.b])

        with ExitStack() as ph:
            xt = [sb("xt%d" % i, [128, D], F32, ph) for i in range(2)]
            xT = [sb("xT%d" % i, [128, 8, 128], F32, ph) for i in range(2)]
            for t in range(NTILE):
                src = ctx_d[t * 128:(t + 1) * 128, :] if t < 2 else x_d[(t - 2) * 128:(t - 1) * 128, :]
                a = xt[t % 2]
                o = xT[t % 2]
                P.dma("sp", a[:], src, w=[a.b])
                for half in range(2):
                    pb = ps[(2 * t + half) % 4]
                    for q in range(4):
                        c = half * 4 + q
                        P.pe(lambda e, pb=pb, a=a, c=c, q=q: e.transpose(pb[:, q * 128:(q + 1) * 128], a[:, c * 128:(c + 1) * 128], ident_f[:]),
                             r=[a.b, ident_f.b], w=[pb.b])
                    if half == 0:
                        P.act(lambda e, pb=pb, o=o: e.activation(out=o[:, 0:4, :], in_=pb[:].rearrange("p (c t) -> p c t", c=4), func=AF.Copy),
                              r=[pb.b], w=[o.k(0)])
                    else:
                        P.dve(lambda e, pb=pb, o=o: e.tensor_copy(out=o[:, 4:8, :], in_=pb[:].rearrange("p (c t) -> p c t", c=4)),
                              r=[pb.b], w=[o.k(1)])
                P.dma("sp", hT_d.ap[:, :, t * 128:(t + 1) * 128].rearrange("c p t -> p c t"), o[:],
                      r=[o.k(0), o.k(1)], w=[hT_d.k(t)])

            cT = sb("cT", [128, 8, 2], F32, ph)
            scT = sb("scT", [128, 8, 2], BF16, ph)
            P.dma("sp", cT[:, :, 0], c_d.rearrange("(c p) -> p c", p=128), w=[cT.b], allow_slow_non_contiguous=True)
            P.dma("sp", cT[:, :, 1], cc_d.rearrange("(c p) -> p c", p=128), w=[cT.b], allow_slow_non_contiguous=True)
            P.act(lambda e: e.activation(out=scT[:], in_=cT[:], func=AF.Silu), r=[cT.b], w=[scT.b])
            wm = [sb("wm%d" % i, [128, 8, 1024], BF16, ph) for i in range(2)]
            bm = sb("bm", [128, DEPTH, 48], F32, ph)
            n1 = sb("n1g", [128, DEPTH, 8], F32, ph)
            n2 = sb("n2g", [128, DEPTH, 8], F32, ph)
            P.dma("sp", bm[:, :nl], IN["b_mod"].rearrange("l (j p) -> p l j", p=128), w=[bm.b], allow_slow_non_contiguous=True)
            P.dma("sp", n1[:, :nl], IN["norm1_g"].rearrange("l (j p) -> p l j", p=128), w=[n1.b], allow_slow_non_contiguous=True)
            P.dma("sp", n2[:, :nl], IN["norm2_g"].rearrange("l (j p) -> p l j", p=128), w=[n2.b], allow_slow_non_contiguous=True)
            pm = ps[4]
            it = 0
            for l in range(nl):
                for piece in range(6):
                    w_ = wm[it % 2]
                    it += 1
                    P.dma("pool", w_[:], IN["w_mod"][l, :, piece * 1024:(piece + 1) * 1024].rearrange("(c p) n -> p c n", p=128), w=[w_.b])
                    for jj in range(8):
                        j = piece * 8 + jj
                        for c in range(8):
                            P.pe(lambda e, w_=w_, jj=jj, c=c, j=j: e.matmul(pm[:, 2 * j:2 * j + 2], w_[:, c, jj * 128:(jj + 1) * 128], scT[:, c, :],
                                                                          start=(c == 0), stop=(c == 7)),
                                 r=[w_.b, scT.b], w=[pm.b])
                P.dve(lambda e, l=l: e.tensor_tensor(out=modv[:, l, :, :], in0=pm[:, 0:96].rearrange("p (j t) -> p j t", t=2),
                                                    in1=bm[:, l, :].unsqueeze(2).to_broadcast([128, 48, 2]), op=ALU.add),
                      r=[pm.b, bm.b], w=[modv.b])
                P.dve(lambda e, l=l: e.scalar_tensor_tensor(out=gm1[:, l, :, :], in0=modv[:, l, 8:16, :], scalar=1.0,
                                                           in1=n1[:, l, :].unsqueeze(2).to_broadcast([128, 8, 2]), op0=ALU.add, op1=ALU.mult),
                      r=[modv.b, n1.b], w=[gm1.b])
                P.dve(lambda e, l=l: e.scalar_tensor_tensor(out=gm2[:, l, :, :], in0=modv[:, l, 32:40, :], scalar=1.0,
                                                           in1=n2[:, l, :].unsqueeze(2).to_broadcast([128, 8, 2]), op0=ALU.add, op1=ALU.mult),
                      r=[modv.b, n2.b], w=[gm2.b])
            if "modv" in dbg:
                P.dma("sp", modv_d.ap, modv[:], r=[modv.b], w=[modv_d.b])
            P.barrier()

        def tiles_of(t0, N):
            return list(range(t0 // 128, (t0 + N) // 128))

        GROUPS = [("dq", 0, 512), ("dk", 512, 512), ("dv", 1024, 512), ("wq", 2048, 512), ("wkv", 2560, 256)]

        def phase_A(l):
            with ExitStack() as ph:
                win = sb("winA", [128, 8, 2816], BF16, ph)
                cos2 = sb("cos2", [128, LT, 64], F32, ph)
                sin2s = sb("sin2s", [128, LT, 64], F32, ph)
                P.dma("sp", cos2[:], IN["rope_cos"].rearrange("(t p) d -> p t d", p=128), w=[cos2.b])
                P.dma("sp", sin2s[:], IN["rope_sin"].rearrange("(t p) d -> p t d", p=128), w=[sin2s.b])
                for c in range(8):
                    rows = IN["w_in"][l, c * 128:(c + 1) * 128, :]
                    P.dma("pool", win[:, c, 0:2048], rows[:, 0:2048], w=[win.k(c)])
                    for kv in range(2):
                        P.dma("pool", win[:, c, 2048:2560].rearrange("p (g kv d) -> p kv g d", g=4, kv=2, d=64)[:, kv, :, :],
                              rows[:, 2048:2560].rearrange("p (kv g d) -> p kv g d", g=4, kv=2, d=64)[:, kv, :, :], w=[win.k(c)])
                    P.dma("pool", win[:, c, 2560:2816], rows[:, 2560:2816], w=[win.k(c)])
                graw = sb("graw", [128, 4, 64], F32, ph)
                gt = sb("gtab", [128, 4, 8, 64], F32, ph)
                for i, nm in enumerate(["diff_q_norm_g", "diff_k_norm_g", "win_q_norm_g", "win_k_norm_g"]):
                    P.dma("sp", graw[:, i, :], IN[nm][l].partition_broadcast(128), w=[graw.k(i)])
                    sc = 0.125 if i in (0, 2) else 1.0
                    P.dve(lambda e, i=i, sc=sc: e.tensor_scalar(out=gt[:, i, :, :], in0=graw[:, i, :].unsqueeze(1).to_broadcast([128, 8, 64]),
                                                              scalar1=sc, scalar2=None, op0=ALU.mult), r=[graw.k(i)], w=[gt.k(i)])
                hb = [sb("hb%d" % i, [128, 8, 512], F32, ph) for i in range(1)] * 2
                sq = sb("sq", [128, 8, 512], BF16, ph)
                rt = sb("rt", [128, 512], F32, ph)
                rstd = sb("rstd", [128, 512], F32, ph)
                tmp = [sb("tmp%d" % i, [128, 512], F32, ph) for i in range(2)]
                aT = [sb("aT%d" % i, [128, 8, 512], BF16, ph) for i in range(2)]
                NS = 4
                sqx = [sb("sqx%d" % i, [128, 512], F32, ph) for i in range(NS)]
                ss = [sb("ss%d" % i, [128, 8], F32, ph) for i in range(NS)]
                ss2 = [sb("ss2%d" % i, [128, 8], F32, ph) for i in range(NS)]
                rs = [sb("rs%d" % i, [128, 8], F32, ph) for i in range(NS)]
                xn = [sb("xn%d" % i, [128, 512], F32, ph) for i in range(NS)]
                xg = [sb("xg%d" % i, [128, 512], F32, ph) for i in range(NS)]
                r1 = [sb("r1%d" % i, [128, 512], F32, ph) for i in range(NS)]
                r2 = [sb("r2%d" % i, [128, 512], F32, ph) for i in range(NS)]
                qtm = [sb("qtm%d" % i, [128, 512], BF16, ph) for i in range(NS)]
                vtm = [sb("vtm%d" % i, [128, 512], BF16, ph) for i in range(2)]
                wvtm = [sb("wvtm%d" % i, [128, 128], BF16, ph) for i in range(2)]
                qdT_b = [sb("qdTb%d" % i, [128, 4, 512], BF16, ph) for i in range(2)]
                kdT_b = [sb("kdTb%d" % i, [128, 4, 512], BF16, ph) for i in range(2)]
                wqT_b = [sb("wqTb%d" % i, [128, 4, 512], BF16, ph) for i in range(2)]
                wkT_b = [sb("wkTb%d" % i, [128, 512], BF16, ph) for i in range(2)]
                suT_b = [sb("suTb%d" % i, [128, 4, 512], BF16, ph) for i in range(2)]
                pst = [ps[6], ps[7]]
                cnt = {"g": 0, "s": 0, "t": 0, "v": 0}

                def qk_post(pb, nh, gi, lat_tile):
                    W = nh * 64
                    si = cnt["s"] % NS
                    cnt["s"] += 1
                    v3 = lambda ap: ap.rearrange("p (h d) -> p h d", d=64)
                    P.act(lambda e: e.activation(out=sqx[si][:, :W], in_=pb[:, :W], func=AF.Square), r=[pb.b], w=[sqx[si].b])
                    P.dve(lambda e: e.tensor_reduce(out=ss[si][:, :nh], in_=v3(sqx[si][:, :W]), axis=AX.X, op=ALU.add), r=[sqx[si].b], w=[ss[si].b])
                    P.act(lambda e: e.activation(out=ss2[si][:, :nh], in_=ss[si][:, :nh], func=AF.Sqrt, scale=1.0 / 64, bias=EPS), r=[ss[si].b], w=[ss2[si].b])
                    P.dve(lambda e: e.reciprocal(out=rs[si][:, :nh], in_=ss2[si][:, :nh]), r=[ss2[si].b], w=[rs[si].b])
                    P.dve(lambda e: e.tensor_tensor(out=v3(xn[si][:, :W]), in0=v3(pb[:, :W]), in1=rs[si][:, :nh].unsqueeze(2).to_broadcast([128, nh, 64]), op=ALU.mult),
                          r=[pb.b, rs[si].b], w=[xn[si].b])
                    o = qtm[si]
                    if lat_tile is None:
                        P.dve(lambda e: e.tensor_tensor(out=v3(o[:, :W]), in0=v3(xn[si][:, :W]), in1=gt[:, gi, :nh, :], op=ALU.mult),
                               r=[xn[si].b, gt.k(gi)], w=[o.b])
                    else:
                        P.dve(lambda e: e.tensor_tensor(out=v3(xg[si][:, :W]), in0=v3(xn[si][:, :W]), in1=gt[:, gi, :nh, :], op=ALU.mult),
                               r=[xn[si].b, gt.k(gi)], w=[xg[si].b])
                        P.dve(lambda e: e.tensor_tensor(out=v3(r1[si][:, :W]), in0=v3(xg[si][:, :W]),
                                                        in1=cos2[:, lat_tile, :].unsqueeze(1).to_broadcast([128, nh, 64]), op=ALU.mult),
                              r=[xg[si].b, cos2.b], w=[r1[si].b])
                        v4 = lambda ap: ap.rearrange("p (h two d) -> p h two d", two=2, d=32)
                        P.dve(lambda e: e.tensor_tensor(out=v4(r2[si][:, :W]), in0=v4(xg[si][:, :W])[:, :, ::-1, :],
                                                         in1=sin2s[:, lat_tile, :].rearrange("p (two d) -> p two d", two=2).unsqueeze(1).to_broadcast([128, nh, 2, 32]),
                                                         op=ALU.mult),
                               r=[xg[si].b, sin2s.b], w=[r2[si].b])
                        P.dve(lambda e: e.tensor_tensor(out=o[:, :W], in0=r1[si][:, :W], in1=r2[si][:, :W], op=ALU.add),
                              r=[r1[si].b, r2[si].b], w=[o.b])
                    return o

                def transposes(src, nchunk, dst_ap, dst_tok):
                    pt = pst[cnt["t"] % 2]
                    use_act = cnt["t"] % 2 == 0
                    cnt["t"] += 1
                    ptv = pt[:].bitcast(BF16)
                    for k in range(nchunk):
                        P.pe(lambda e, k=k: e.transpose(ptv[:, k * 128:(k + 1) * 128], src[:, k * 128:(k + 1) * 128], ident_b[:]),
                             r=[src.b, ident_b.b], w=[pt.b])
                    inv = ptv[:, 0:nchunk * 128].rearrange("p (c t) -> p c t", c=nchunk)
                    if use_act:
                        P.act(lambda e: e.activation(out=dst_ap, in_=inv, func=AF.Copy), r=[pt.b], w=[dst_tok])
                    else:
                        P.dve(lambda e: e.tensor_copy(out=dst_ap, in_=inv), r=[pt.b], w=[dst_tok])

                for bi, (t0, N, isctx) in enumerate(BLOCKS):
                    h = hb[bi % 2]
                    a = aT[bi % 2]
                    mi = 1 if isctx else 0
                    P.dma("sp", h[:, :, :N], hT_d.ap[:, :, t0:t0 + N].rearrange("c p t -> p c t"),
                          r=[hT_d.k(t) for t in tiles_of(t0, N)], w=[h.b])
                    P.act(lambda e, h=h, N=N: e.activation(out=sq[:, :, :N], in_=h[:, :, :N], func=AF.Square), r=[h.b], w=[sq.b])
                    for c in range(8):
                        P.pe(lambda e, c=c, N=N: e.matmul(ps[0][:, :N], ones_b[:], sq[:, c, :N], start=(c == 0), stop=(c == 7)),
                             r=[sq.b, ones_b.b], w=[ps[0].b])
                    P.act(lambda e, N=N: e.activation(out=rt[:, :N], in_=ps[0][:, :N], func=AF.Sqrt, scale=1.0 / D, bias=EPS), r=[ps[0].b], w=[rt.b])
                    P.dve(lambda e, N=N: e.reciprocal(out=rstd[:, :N], in_=rt[:, :N]), r=[rt.b], w=[rstd.b])
                    for c in range(8):
                        tp = tmp[c % 2]
                        P.dve(lambda e, c=c, tp=tp, h=h, N=N, mi=mi: e.scalar_tensor_tensor(out=tp[:, :N], in0=h[:, c, :N], scalar=gm1[:, l, c, mi:mi + 1],
                                                                                       in1=rstd[:, :N], op0=ALU.mult, op1=ALU.mult),
                              r=[h.b, gm1.b, rstd.b], w=[tp.b])
                        P.act(lambda e, c=c, tp=tp, a=a, N=N, mi=mi: e.activation(out=a[:, c, :N], in_=tp[:, :N], func=AF.Identity,
                                                                             bias=modv[:, l, c, mi:mi + 1], scale=1.0),
                              r=[tp.b, modv.b], w=[a.k(c)])
                    P.dma("sp", aT_d.ap[:, :, t0:t0 + N].rearrange("c p t -> p c t"), a[:, :, :N],
                          r=[a.k(c) for c in range(8)], w=[aT_d.k(bi)])
                    qb, kb, wqb, wkb, sub = qdT_b[bi % 2], kdT_b[bi % 2], wqT_b[bi % 2], wkT_b[bi % 2], suT_b[bi % 2]
                    v3 = lambda ap: ap.rearrange("p (h d) -> p h d", d=64)
                    v4 = lambda ap: ap.rearrange("p (h two d) -> p h two d", two=2, d=32)
                    specs = [("dq", 8, 0), ("dk", 8, 1), ("wq", 8, 2), ("wkv", 2, 3)]
                    for j in range(N // 128):
                        tok = t0 // 128 + j
                        lat_tile = None if isctx else tok - 2
                        G = {}
                        for gi_, (kind, col0, W) in enumerate(GROUPS):
                            pb = ps[1 + gi_]
                            G[kind] = pb
                            for c in range(8):
                                P.pe(lambda e, pb=pb, c=c, j=j, a=a, col0=col0, W=W: e.matmul(pb[:, :W], a[:, c, j * 128:(j + 1) * 128], win[:, c, col0:col0 + W],
                                                                                         start=(c == 0), stop=(c == 7)),
                                     r=[a.k(c), win.k(c)], w=[pb.b])
                        vt = vtm[cnt["v"] % 2]
                        wt = wvtm[cnt["v"] % 2]
                        cnt["v"] += 1
                        pbv, pbw = G["dv"], G["wkv"]
                        P.act(lambda e: e.activation(out=vt[:], in_=pbv[:], func=AF.Copy), r=[pbv.b], w=[vt.b])
                        P.act(lambda e: e.activation(out=wt[:], in_=pbw[:, 128:256], func=AF.Copy), r=[pbw.b], w=[wt.b])
                        if isctx:
                            P.dma("sp", vdC_d.ap[tok * 128:(tok + 1) * 128, :], vt[:], r=[vt.b], w=[vdC_d.b])
                            P.dma("sp", wvC_d.ap[tok * 128:(tok + 1) * 128, :], wt[:], r=[wt.b], w=[wvC_d.b])
                        else:
                            P.dma("sp", vdL_d.ap[(tok - 2) * 128:(tok - 1) * 128, :], vt[:], r=[vt.b], w=[vdL_d.b])
                            P.dma("sp", wvL_d.ap[(tok - 2) * 128:(tok - 1) * 128, :], wt[:], r=[wt.b], w=[wvL_d.b])
                        for si, (kind, nh, gi) in enumerate(specs):
                            pb, W = G[kind], nh * 64
                            P.act(lambda e: e.activation(out=sqx[si][:, :W], in_=pb[:, :W], func=AF.Square), r=[pb.b], w=[sqx[si].b])
                        for si, (kind, nh, gi) in enumerate(specs):
                            W = nh * 64
                            P.dve(lambda e: e.tensor_reduce(out=ss[si][:, :nh], in_=v3(sqx[si][:, :W]), axis=AX.X, op=ALU.add), r=[sqx[si].b], w=[ss[si].b])
                        for si, (kind, nh, gi) in enumerate(specs):
                            P.act(lambda e: e.activation(out=ss2[si][:, :nh], in_=ss[si][:, :nh], func=AF.Sqrt, scale=1.0 / 64, bias=EPS), r=[ss[si].b], w=[ss2[si].b])
                        for si, (kind, nh, gi) in enumerate(specs):
                            P.dve(lambda e: e.reciprocal(out=rs[si][:, :nh], in_=ss2[si][:, :nh]), r=[ss2[si].b], w=[rs[si].b])
                        for si, (kind, nh, gi) in enumerate(specs):
                            pb, W = G[kind], nh * 64
                            P.dve(lambda e: e.tensor_tensor(out=v3(xn[si][:, :W]), in0=v3(pb[:, :W]), in1=rs[si][:, :nh].unsqueeze(2).to_broadcast([128, nh, 64]), op=ALU.mult),
                                  r=[pb.b, rs[si].b], w=[xn[si].b])
                        if lat_tile is None:
                            for si, (kind, nh, gi) in enumerate(specs):
                                W = nh * 64
                                P.dve(lambda e: e.tensor_tensor(out=v3(qtm[si][:, :W]), in0=v3(xn[si][:, :W]), in1=gt[:, gi, :nh, :], op=ALU.mult),
                                      r=[xn[si].b, gt.k(gi)], w=[qtm[si].b])
                        else:
                            for si, (kind, nh, gi) in enumerate(specs):
                                W = nh * 64
                                P.dve(lambda e: e.tensor_tensor(out=v3(xg[si][:, :W]), in0=v3(xn[si][:, :W]), in1=gt[:, gi, :nh, :], op=ALU.mult),
                                      r=[xn[si].b, gt.k(gi)], w=[xg[si].b])
                            for si, (kind, nh, gi) in enumerate(specs):
                                W = nh * 64
                                P.dve(lambda e: e.tensor_tensor(out=v3(r1[si][:, :W]), in0=v3(xg[si][:, :W]),
                                                                in1=cos2[:, lat_tile, :].unsqueeze(1).to_broadcast([128, nh, 64]), op=ALU.mult),
                                      r=[xg[si].b, cos2.b], w=[r1[si].b])
                            for si, (kind, nh, gi) in enumerate(specs):
                                W = nh * 64
                                P.dve(lambda e: e.tensor_tensor(out=v4(r2[si][:, :W]), in0=v4(xg[si][:, :W])[:, :, ::-1, :],
                                                                in1=sin2s[:, lat_tile, :].rearrange("p (two d) -> p two d", two=2).unsqueeze(1).to_broadcast([128, nh, 2, 32]),
                                                                op=ALU.mult),
                                      r=[xg[si].b, sin2s.b], w=[r2[si].b])
                            for si, (kind, nh, gi) in enumerate(specs):
                                W = nh * 64
                                P.dve(lambda e: e.tensor_tensor(out=qtm[si][:, :W], in0=r1[si][:, :W], in1=r2[si][:, :W], op=ALU.add),
                                      r=[r1[si].b, r2[si].b], w=[qtm[si].b])
                        pv6 = ps[6][:].bitcast(BF16)
                        pv7 = ps[7][:].bitcast(BF16)
                        for k in range(4):
                            P.pe(lambda e: e.transpose(pv6[:, k * 128:(k + 1) * 128], qtm[0][:, k * 128:(k + 1) * 128], ident_b[:]), r=[qtm[0].b, ident_b.b], w=[ps[6].b])
                        for k in range(4):
                            P.pe(lambda e: e.transpose(pv6[:, (4 + k) * 128:(5 + k) * 128], qtm[1][:, k * 128:(k + 1) * 128], ident_b[:]), r=[qtm[1].b, ident_b.b], w=[ps[6].b])
                        for k in range(4):
                            P.pe(lambda e: e.transpose(pv7[:, k * 128:(k + 1) * 128], qtm[2][:, k * 128:(k + 1) * 128], ident_b[:]), r=[qtm[2].b, ident_b.b], w=[ps[7].b])
                        P.pe(lambda e: e.transpose(pv7[:, 512:640], qtm[3][:, 0:128], ident_b[:]), r=[qtm[3].b, ident_b.b], w=[ps[7].b])
                        P.act(lambda e: e.activation(out=qb[:, :, j * 128:(j + 1) * 128], in_=pv6[:, 0:512].rearrange("p (c t) -> p c t", c=4), func=AF.Copy), r=[ps[6].b], w=[qb.k(j)])
                        P.act(lambda e: e.activation(out=kb[:, :, j * 128:(j + 1) * 128], in_=pv6[:, 512:1024].rearrange("p (c t) -> p c t", c=4), func=AF.Copy), r=[ps[6].b], w=[kb.k(j)])
                        P.dve(lambda e: e.tensor_copy(out=wqb[:, :, j * 128:(j + 1) * 128], in_=pv7[:, 0:512].rearrange("p (c t) -> p c t", c=4)), r=[ps[7].b], w=[wqb.k(j)])
                        P.dve(lambda e: e.tensor_copy(out=wkb[:, j * 128:(j + 1) * 128], in_=pv7[:, 512:640]), r=[ps[7].b], w=[wkb.k(j)])
                    for cc in range(4):
                        pb = ps[4 + cc % 2]
                        for c in range(8):
                            P.pe(lambda e, pb=pb, c=c, cc=cc, a=a, N=N: e.matmul(pb[:, :N], win[:, c, 1536 + cc * 128:1536 + (cc + 1) * 128], a[:, c, :N],
                                                                            start=(c == 0), stop=(c == 7)),
                                 r=[a.k(c), win.k(c)], w=[pb.b])
                        P.act(lambda e, pb=pb, cc=cc, sub=sub, N=N: e.activation(out=sub[:, cc, :N], in_=pb[:, :N], func=AF.Copy), r=[pb.b], w=[sub.k(cc)])
                    nj = N // 128
                    dview = lambda dt_: dt_.ap[:, :, t0:t0 + N].rearrange("c p t -> p c t")
                    P.dma("sp", dview(qdT_d), qb[:, :, :N], r=[qb.k(j) for j in range(nj)], w=[qdT_d.k(bi)])
                    l0 = t0 - NCTX
                    if isctx:
                        P.dma("sp", kdC_d.ap.rearrange("c p t -> p c t"), kb[:, :, :N], r=[kb.k(j) for j in range(nj)], w=[kdC_d.b])
                    else:
                        P.dma("sp", kdL_d.ap[:, l0:l0 + N].rearrange("(c p) t -> p c t", p=128), kb[:, :, :N], r=[kb.k(j) for j in range(nj)], w=[kdL_d.b])
                    P.dma("sp", dview(wqT_d), wqb[:, :, :N], r=[wqb.k(j) for j in range(nj)], w=[wqT_d.k(bi)])
                    if isctx:
                        P.dma("sp", wkC_d.ap, wkb[:, :N], r=[wkb.k(j) for j in range(nj)], w=[wkC_d.b])
                        P.dma("sp", suC_d.ap.rearrange("c p t -> p c t"), sub[:, :, :N], r=[sub.k(cc) for cc in range(4)], w=[suC_d.b])
                    else:
                        P.dma("sp", wkL_d.ap[:, l0:l0 + N], wkb[:, :N], r=[wkb.k(j) for j in range(nj)], w=[wkL_d.b])
                        P.dma("sp", suL_d.ap[:, l0:l0 + N].rearrange("(c p) t -> p c t", p=128), sub[:, :, :N], r=[sub.k(cc) for cc in range(4)], w=[suL_d.b])
                P.barrier()
                for a_, g_ in ((kdL_d, kdG_d), (vdL_d, vdG_d), (wkL_d, wkG_d), (wvL_d, wvG_d), (suL_d, suG_d)):
                    P.allgather(a_, g_, r=[a_.b], w=[g_.b])

        def phase_D(l):
            need_ctx = l < DEPTH - 1
            lam_init = 0.8 - 0.6 * math.exp(-0.3 * l)
            with ExitStack() as ph:
                kT = sb("kT", [128, 4, GNT], BF16, ph)
                V = sb("V", [128, GTILE, 512], BF16, ph)
                for c in range(4):
                    P.dma("sp", kT[:, c, 0:NCTX], kdC_d.ap[c], r=[kdC_d.b], w=[kT.k(c)])
                    for rk in range(2):
                        P.dma("sp", kT[:, c, NCTX + rk * NLAT:NCTX + (rk + 1) * NLAT], kdG_d.ap[rk * 512 + c * 128:rk * 512 + (c + 1) * 128, :],
                              r=[kdG_d.b], w=[kT.k(c)])
                P.dma("sp", V[:, 0:2, :], vdC_d.ap.rearrange("(t p) f -> p t f", p=128), r=[vdC_d.b], w=[V.k(0)])
                vv = vdG_d.ap.rearrange("(t p) f -> p t f", p=128)
                for q4 in range(0, 32, 8):
                    P.dma("sp", V[:, 2 + q4:2 + q4 + 8, :], vv[:, q4:q4 + 8, :], r=[vdG_d.b], w=[V.k(0)])
                Vtok = [V.k(0) for q4 in range(0, GTILE, 9)]
                lq = sb("lamv", [128, 4, 64], F32, ph)
                for i, nm in enumerate(["diff_lam_q1", "diff_lam_k1", "diff_lam_q2", "diff_lam_k2"]):
                    P.dma("sp", lq[:, i, :], IN[nm][l].partition_broadcast(128), w=[lq.b])
                lprod = sb("lprod", [128, 2, 64], F32, ph)
                lsum = sb("lsum", [128, 2], F32, ph)
                lexp = sb("lexp", [128, 2], F32, ph)
                nlam = sb("nlam", [128, 1], F32, ph)
                gout = sb("gout", [128, 1], F32, ph)
                P.dve(lambda e: e.tensor_tensor(out=lprod[:], in0=lq[:, 0::2, :], in1=lq[:, 1::2, :], op=ALU.mult), r=[lq.b], w=[lprod.b])
                P.dve(lambda e: e.tensor_reduce(out=lsum[:], in_=lprod[:], axis=AX.X, op=ALU.add), r=[lprod.b], w=[lsum.b])
                P.act(lambda e: e.activation(out=lexp[:], in_=lsum[:], func=AF.Exp), r=[lsum.b], w=[lexp.b])
                P.dve(lambda e: e.tensor_tensor(out=nlam[:], in0=lexp[:, 1:2], in1=lexp[:, 0:1], op=ALU.subtract), r=[lexp.b], w=[nlam.b])
                P.dve(lambda e: e.tensor_scalar(out=nlam[:], in0=nlam[:], scalar1=-lam_init, scalar2=None, op0=ALU.add), r=[nlam.b], w=[nlam.b])
                P.dma("sp", gout[:], IN["diff_out_norm_g"][l].rearrange("(p o) -> p o", o=1), w=[gout.b])
                P.dve(lambda e: e.tensor_scalar(out=gout[:], in0=gout[:], scalar1=1.0 - lam_init, scalar2=None, op0=ALU.mult), r=[gout.b], w=[gout.b])
                qTb = [sb("qTb%d" % i, [128, 4, 512], BF16, ph) for i in range(2)]
                ydb_ = [sb("ydb%d" % i, [128, 4, 512], BF16, ph) for i in range(2)]
                pT = [sb("pT%d" % i, [128, 2, 512], BF16, ph) for i in range(2)]
                rz = [sb("rz%d" % i, [128, 512], F32, ph) for i in range(2)]
                oz = sb("oz", [128, 4, 512], F32, ph)
                t1 = sb("t1", [128, 512], F32, ph)
                t2 = sb("t2", [128, 512], F32, ph)
                o_ = sb("o_", [128, 512], F32, ph)
                osq = sb("osq", [128, 512], BF16, ph)
                ort = sb("ort", [128, 512], F32, ph)
                orst = sb("orst", [128, 512], F32, ph)
                gi = [0]
                for bi, (t0, N, isctx) in enumerate(BLOCKS):
                    if isctx and not need_ctx:
                        continue
                    if dlimit is not None and bi not in dlimit:
                        continue
                    ktiles = [0, 1] if isctx else list(range(GTILE))
                    q = qTb[bi % 2]
                    ydb = ydb_[bi % 2]
                    P.dma("sp", q[:, :, :N], qdT_d.ap[:, :, t0:t0 + N].rearrange("c p t -> p c t"), r=[qdT_d.k(bi)], w=[q.b])
                    for h in range(4):
                        half = h % 2
                        items = [(kt, m) for kt in ktiles for m in range(2)]
                        base = gi[0]
                        gi[0] += len(items)

                        SB = [(ps[0], ps[1]), (ps[2], ps[3])]
                        npair = len(ktiles)
                        pbase = gi[0]
                        gi[0] += npair

                        def qk(i):
                            kt = ktiles[i]
                            for m in range(2):
                                c = 2 * m + h // 2
                                sbank = SB[(pbase + i) % 2][m]
                                P.pe(lambda e: e.matmul(sbank[:, :N], kT[half * 64:(half + 1) * 64, c, kt * 128:(kt + 1) * 128],
                                                        q[half * 64:(half + 1) * 64, c, :N], start=True, stop=True),
                                     r=[kT.k(c), q.b], w=[SB[(pbase + i) % 2][0].b])

                        def pv(i):
                            kt = ktiles[i]
                            sb2 = SB[(pbase + i) % 2]
                            p_ = pT[(pbase + i) % 2]
                            bk = 2 * ((pbase + i) % 2)
                            P.act(lambda e: e.activation(out=p_[:, :, :N], in_=psbig[:, bk:bk + 2, :N], func=AF.Exp), r=[sb2[0].b], w=[p_.b])
                            first = (i == 0)
                            last = (i == npair - 1)
                            for m in range(2):
                                P.pe(lambda e: e.matmul(ps[4 + m][:, :N], V[:, kt, h * 128:(h + 1) * 128], p_[:, m, :N], start=first, stop=last),
                                     r=[Vtok[kt // 9], p_.b], w=[ps[4].b])
                                P.pe(lambda e: e.matmul(ps[6 + m][:, :N], ones_b[:], p_[:, m, :N], start=first, stop=last),
                                     r=[ones_b.b, p_.b], w=[ps[4].b])

                        qk(0)
                        for i in range(npair):
                            if i + 1 < npair:
                                qk(i + 1)
                            pv(i)
                        P.dve(lambda e: e.tensor_copy(out=oz[:, :, :N], in_=psbig[:, 4:8, :N]), r=[ps[4].b], w=[oz.b])
                        P.dve(lambda e: e.reciprocal(out=rz[0][:, :N], in_=oz[:, 2, :N]), r=[oz.b], w=[rz[0].b])
                        P.dve(lambda e: e.reciprocal(out=rz[1][:, :N], in_=oz[:, 3, :N]), r=[oz.b], w=[rz[1].b])
                        P.dve(lambda e: e.tensor_tensor(out=t1[:, :N], in0=oz[:, 0, :N], in1=rz[0][:, :N], op=ALU.mult), r=[oz.b, rz[0].b], w=[t1.b])
                        P.dve(lambda e: e.tensor_tensor(out=t2[:, :N], in0=oz[:, 1, :N], in1=rz[1][:, :N], op=ALU.mult), r=[oz.b, rz[1].b], w=[t2.b])
                        P.dve(lambda e: e.scalar_tensor_tensor(out=o_[:, :N], in0=t2[:, :N], scalar=nlam[:, 0:1], in1=t1[:, :N], op0=ALU.mult, op1=ALU.add),
                              r=[t1.b, t2.b, nlam.b], w=[o_.b])
                        P.act(lambda e: e.activation(out=osq[:, :N], in_=o_[:, :N], func=AF.Square), r=[o_.b], w=[osq.b])
                        nb = SB[gi[0] % 2]
                        P.pe(lambda e: e.matmul(nb[1][:, :N], ones_b[:], osq[:, :N], start=True, stop=True), r=[ones_b.b, osq.b], w=[nb[0].b])
                        P.act(lambda e: e.activation(out=ort[:, :N], in_=nb[1][:, :N], func=AF.Sqrt, scale=1.0 / 128, bias=EPS), r=[nb[0].b], w=[ort.b])
                        gi[0] += 1
                        P.dve(lambda e: e.reciprocal(out=orst[:, :N], in_=ort[:, :N]), r=[ort.b], w=[orst.b])
                        P.dve(lambda e, h=h: e.scalar_tensor_tensor(out=ydb[:, h, :N], in0=o_[:, :N], scalar=gout[:, 0:1], in1=orst[:, :N], op0=ALU.mult, op1=ALU.mult),
                              r=[o_.b, gout.b, orst.b], w=[ydb.k(h)])
                    P.dma("sp", ydT_d.ap[:, :, t0:t0 + N].rearrange("c p t -> p c t"), ydb[:, :, :N], r=[ydb.k(h) for h in range(4)], w=[ydT_d.k(bi)])
                P.barrier()

        def phase_W(l):
            need_ctx = l < DEPTH - 1
            with ExitStack() as ph:
                ET = LT + 4
                wkT = sb("wkT", [128, ET * 128], BF16, ph)
                wv = sb("wv", [128, ET, 128], BF16, ph)
                P.dma("sp", wkT[:, 0:NCTX], wkC_d.ap, r=[wkC_d.b], w=[wkT.b])
                P.dma("sp", wkT[:, 2 * 128:3 * 128], wkG_d.ap[0:128, NLAT - 128:NLAT], r=[wkG_d.b], w=[wkT.b])
                P.dma("sp", wkT[:, 3 * 128:(3 + LT) * 128], wkL_d.ap, r=[wkL_d.b], w=[wkT.b])
                P.dma("sp", wkT[:, (3 + LT) * 128:(4 + LT) * 128], wkG_d.ap[128:256, 0:128], r=[wkG_d.b], w=[wkT.b])
                P.dma("sp", wv[:, 0:2, :], wvC_d.ap.rearrange("(t p) f -> p t f", p=128), r=[wvC_d.b], w=[wv.b])
                P.dma("sp", wv[:, 2, :], wvG_d.ap[NLAT - 128:NLAT, :], r=[wvG_d.b], w=[wv.b])
                P.dma("sp", wv[:, 3:3 + LT, :], wvL_d.ap.rearrange("(t p) f -> p t f", p=128), r=[wvL_d.b], w=[wv.b])
                P.dma("sp", wv[:, 3 + LT, :], wvG_d.ap[NLAT:NLAT + 128, :], r=[wvG_d.b], w=[wv.b])
                mk = sb("wmask", [128, 4, 128], BF16, ph)
                mkf = sb("wmaskf", [128, 2, 128], F32, ph)
                P.dma("pool", mk[:, 0, :], IN["mask_prev"], w=[mk.b])
                P.dma("pool", mk[:, 1, :], IN["mask_next"], w=[mk.b])
                P.dma("sp", mkf[:, 0, :], IN["mask_prev"], w=[mkf.b])
                P.dma("sp", mkf[:, 1, :], IN["mask_next"], w=[mkf.b])
                P.dve(lambda e: e.tensor_scalar(out=mk[:, 2, :], in0=mkf[:, 0, :], scalar1=flg[:, 1:2], scalar2=None, op0=ALU.mult), r=[mkf.b, flg.b, mk.b], w=[mk.b])
                P.dve(lambda e: e.tensor_scalar(out=mk[:, 3, :], in0=mkf[:, 1, :], scalar1=flg[:, 0:1], scalar2=None, op0=ALU.mult), r=[mkf.b, flg.b, mk.b], w=[mk.b])
                esink = sb("esink", [64, 8], F32, ph)
                P.dma("sp", esink[:], IN["win_sink"][l].partition_broadcast(64), w=[esink.b])
                P.act(lambda e: e.activation(out=esink[:], in_=esink[:], func=AF.Exp), r=[esink.b], w=[esink.b])
                wqb_ = [sb("wqb%d" % i, [128, 4, 512], BF16, ph) for i in range(2)]
                ywb_ = [sb("ywb%d" % i, [64, 8, 512], BF16, ph) for i in range(2)]
                pw = [sb("pw%d" % i, [128, 5, 512], BF16, ph) for i in range(2)]
                zs = sb("zs", [64, 512], F32, ph)
                rzw = sb("rzw", [64, 512], F32, ph)
                u = [0]
                for bi, (t0, N, isctx) in enumerate(BLOCKS):
                    if isctx and not need_ctx:
                        continue
                    wqb = wqb_[bi % 2]
                    ywb = ywb_[bi % 2]
                    P.dma("sp", wqb[:, :, :N], wqT_d.ap[:, :, t0:t0 + N].rearrange("g p t -> p g t"), r=[wqT_d.k(bi)], w=[wqb.b])
                    for j in range(N // 128):
                        Tt = t0 // 128 + j
                        if isctx:
                            keys = [(0, None), (1, None)]
                        else:
                            lt = Tt - 2
                            keys = [(0, None), (1, None), (2 + lt, 2 if lt == 0 else 0), (3 + lt, None), (4 + lt, 3 if lt == LT - 1 else 1)]
                        nk = len(keys)
                        for kv in range(2):
                            p_ = pw[u[0] % 2]
                            u[0] += 1
                            for idx, (kt, mm) in enumerate(keys):
                                P.pe(lambda e: e.matmul(ps[idx][:, :], wkT[kv * 64:(kv + 1) * 64, kt * 128:(kt + 1) * 128],
                                                        wqb[kv * 64:(kv + 1) * 64, :, j * 128:(j + 1) * 128], start=True, stop=True),
                                     r=[wkT.b, wqb.b], w=[ps[0].b])
                            for idx, (kt, mm) in enumerate(keys):
                                P.act(lambda e: e.activation(out=p_[:, idx, :], in_=ps[idx][:, :], func=AF.Exp), r=[ps[0].b], w=[p_.b])
                            for idx, (kt, mm) in enumerate(keys):
                                if mm is not None:
                                    P.pool(lambda e: e.tensor_tensor(out=p_[:, idx, :].rearrange("p (g q) -> p g q", g=4),
                                                                     in0=p_[:, idx, :].rearrange("p (g q) -> p g q", g=4),
                                                                     in1=mk[:, mm, :].unsqueeze(1).to_broadcast([128, 4, 128]), op=ALU.mult),
                                           r=[p_.b, mk.b], w=[p_.b])
                            for idx, (kt, mm) in enumerate(keys):
                                P.pe(lambda e: e.matmul(ps[5][0:64, :], wv[:, kt, kv * 64:(kv + 1) * 64], p_[:, idx, :], start=(idx == 0), stop=(idx == nk - 1)),
                                     r=[wv.b, p_.b], w=[ps[5].b])
                                P.pe(lambda e: e.matmul(ps[6][0:64, :], ones_b[:, 0:64], p_[:, idx, :], start=(idx == 0), stop=(idx == nk - 1)),
                                     r=[ones_b.b, p_.b], w=[ps[5].b])
                            P.dve(lambda e: e.tensor_tensor(out=zs[:].rearrange("p (g q) -> p g q", g=4), in0=ps[6][0:64, :].rearrange("p (g q) -> p g q", g=4),
                                                            in1=esink[:, kv * 4:(kv + 1) * 4].unsqueeze(2).to_broadcast([64, 4, 128]), op=ALU.add),
                                  r=[ps[5].b, esink.b], w=[zs.b])
                            P.dve(lambda e: e.reciprocal(out=rzw[:], in_=zs[:]), r=[zs.b], w=[rzw.b])
                            P.dve(lambda e: e.tensor_tensor(out=ywb[:, kv * 4:(kv + 1) * 4, j * 128:(j + 1) * 128],
                                                            in0=ps[5][0:64, :].rearrange("p (g q) -> p g q", g=4),
                                                            in1=rzw[:].rearrange("p (g q) -> p g q", g=4), op=ALU.mult),
                                  r=[ps[5].b, rzw.b], w=[ywb.k((j, kv))])
                    P.dma("sp", ywT_d.ap[:, :, t0:t0 + N].rearrange("h d t -> d h t"), ywb[:, :, :N],
                          r=[ywb.k((j, kv)) for j in range(N // 128) for kv in range(2)], w=[ywT_d.k(bi)])
                P.barrier()

        I32 = mybir.dt.int32
        C1 = 6.28125
        C2 = 2 * PI - C1

        NSC = 8

        def phase_S(l):
            need_ctx = l < DEPTH - 1
            with ExitStack() as ph:
                tok = Buf("s5prm")
                def sm(name, shape=(128, 2, NSC), dt=F32):
                    return sb(name, list(shape), dt, ph)
                lre, lim, ldt, dtt, th, rl, rr, are, aim = [sm(n) for n in ("lre", "lim", "ldt", "dtt", "th", "rl", "rr", "are", "aim")]
                den, nre, fre, fim, kre, kim, u1, u2, thj = [sm(n) for n in ("den", "nre", "fre", "fim", "kre", "kim", "u1", "u2", "thj")]
                P.dma("sp", lre[:], IN["s5_lambda_re"][l].rearrange("d (sc g2) p -> (g2 p) d sc", g2=2), w=[tok], allow_slow_non_contiguous=True)
                P.dma("sp", lim[:], IN["s5_lambda_im"][l].rearrange("d (sc g2) p -> (g2 p) d sc", g2=2), w=[tok], allow_slow_non_contiguous=True)
                ldv = IN["s5_log_dt"][l].rearrange("d (sc g2) -> g2 d sc", g2=2)
                for g2 in range(2):
                    P.dma("sp", ldt[g2 * 64:(g2 + 1) * 64, :, :], ldv[g2].partition_broadcast(64), w=[tok], allow_slow_non_contiguous=True)
                tau = sm("tau", (128, 2, J))
                gmk = sm("gmk", (128, 2, 8))
                P.dma("sp", tau[:], IN["tau"], w=[tok])
                P.dma("sp", gmk[:, 0, :], IN["gmask"], w=[tok])
                P.dve(lambda e: e.tensor_scalar(out=gmk[:, 1, :], in0=gmk[:, 0, :], scalar1=-1.0, scalar2=None, op0=ALU.mult), r=[tok], w=[tok])
                NE = NSC * J
                NP = 2 * NSC
                cosT = sb("cosT", [128, 2, NSC, J], F32, ph)
                sinT = sb("sinT", [128, 2, NSC, J], F32, ph)
                Rm = sb("Rm", [128, 2, NSC, J], F32, ph)
                cosB = sb("cosB", [128, 2, NSC, J], BF16, ph)
                sinB = sb("sinB", [128, 2, NSC, J], BF16, ph)
                BF = [sb("BblkF%d" % i, [128, 4, NSC, 128], BF16, ph) for i in range(2)]
                Cblk = sb("Cblk", [128, 4, NSC, 128], BF16, ph)
                dsk = sb("dsk", [128, 2], F32, ph)
                diagF = [sb("diagF%d" % i, [128, 2, 128], BF16, ph) for i in range(2)]
                pp = ExitStack()
                Bblk = sb("Bblk", [128, 4, NSC, 128], BF16, pp)
                diagD = sb("diagD", [128, 2, 128], F32, pp)
                rv = sb("rv", [128, NE], F32, pp)
                rki = sb("rki", [128, NE], I32, pp)
                rkf = sb("rkf", [128, NE], F32, pp)
                rm = sb("rm", [128, NE], F32, pp)
                ang = sb("ang", [128, NE], F32, pp)

                def sin_of(dst, src, n, add, rt, wt):
                    V_ = rv[:, :n]; KI = rki[:, :n]; KF = rkf[:, :n]; M_ = rm[:, :n]
                    tk = rv.b
                    P.dve(lambda e: e.tensor_scalar(out=V_, in0=src, scalar1=add, scalar2=1.0 / (2 * PI), op0=ALU.add, op1=ALU.mult), r=rt, w=[tk])
                    P.dve(lambda e: e.tensor_copy(out=KI, in_=V_), r=[tk], w=[tk])
                    P.dve(lambda e: e.tensor_copy(out=KF, in_=KI), r=[tk], w=[tk])
                    P.dve(lambda e: e.scalar_tensor_tensor(out=V_, in0=KF, scalar=-C1, in1=src, op0=ALU.mult, op1=ALU.add), r=[tk] + rt, w=[tk])
                    P.dve(lambda e: e.scalar_tensor_tensor(out=V_, in0=KF, scalar=-C2, in1=V_, op0=ALU.mult, op1=ALU.add), r=[tk], w=[tk])
                    if add != 0.0:
                        P.dve(lambda e: e.tensor_scalar(out=V_, in0=V_, scalar1=add, scalar2=None, op0=ALU.add), r=[tk], w=[tk])
                    P.dve(lambda e: e.tensor_scalar(out=M_, in0=V_, scalar1=PI, scalar2=-2 * PI, op0=ALU.is_gt, op1=ALU.mult), r=[tk], w=[tk])
                    P.dve(lambda e: e.tensor_tensor(out=V_, in0=V_, in1=M_, op=ALU.add), r=[tk], w=[tk])
                    P.dve(lambda e: e.tensor_scalar(out=M_, in0=V_, scalar1=-PI, scalar2=2 * PI, op0=ALU.is_lt, op1=ALU.mult), r=[tk], w=[tk])
                    P.dve(lambda e: e.tensor_tensor(out=V_, in0=V_, in1=M_, op=ALU.add), r=[tk], w=[tk])
                    P.act(lambda e: e.activation(out=dst, in_=V_, func=AF.Sin), r=[tk], w=wt)

                fl = lambda t_: t_[:].rearrange("p d s -> p (d s)")
                TT = lambda o_, a_, b_, op: P.dve(lambda e: e.tensor_tensor(out=fl(o_), in0=fl(a_), in1=fl(b_), op=op), r=[tok], w=[tok])
                P.act(lambda e: e.activation(out=fl(dtt), in_=fl(ldt), func=AF.Exp), r=[tok], w=[tok])
                TT(th, lim, dtt, ALU.mult)
                TT(rl, lre, dtt, ALU.mult)
                P.act(lambda e: e.activation(out=fl(rr), in_=fl(rl), func=AF.Exp), r=[tok], w=[tok])
                sin_of(fl(u1), fl(th), NP, 0.0, [tok], [tok])
                sin_of(fl(u2), fl(th), NP, PI / 2, [tok], [tok])
                TT(aim, rr, u1, ALU.mult)
                TT(are, rr, u2, ALU.mult)
                P.dve(lambda e: e.tensor_scalar(out=fl(thj), in0=fl(th), scalar1=float(J), scalar2=None, op0=ALU.mult), r=[tok], w=[tok])
                sin_of(fl(u1), fl(thj), NP, 0.0, [tok], [tok])
                sin_of(fl(u2), fl(thj), NP, PI / 2, [tok], [tok])
                TT(kim, rr, u1, ALU.mult)
                TT(kre, rr, u2, ALU.mult)
                TT(den, lre, lre, ALU.mult)
                TT(u1, lim, lim, ALU.mult)
                TT(den, den, u1, ALU.add)
                P.dve(lambda e: e.reciprocal(out=fl(den), in_=fl(den)), r=[tok], w=[tok])
                P.dve(lambda e: e.tensor_scalar(out=fl(nre), in0=fl(are), scalar1=-1.0, scalar2=None, op0=ALU.add), r=[tok], w=[tok])
                TT(u1, nre, lre, ALU.mult)
                TT(u2, aim, lim, ALU.mult)
                TT(fre, u1, u2, ALU.add)
                TT(fre, fre, den, ALU.mult)
                TT(u1, aim, lre, ALU.mult)
                TT(u2, nre, lim, ALU.mult)
                TT(fim, u1, u2, ALU.subtract)
                TT(fim, fim, den, ALU.mult)
                for d in range(2):
                    P.dve(lambda e: e.tensor_tensor(out=ang[:].rearrange("p (s t) -> p s t", s=NSC), in0=th[:, d, :].unsqueeze(2).to_broadcast([128, NSC, J]),
                                                    in1=tau[:, d, :].unsqueeze(1).to_broadcast([128, NSC, J]), op=ALU.mult), r=[tok, rv.b], w=[ang.b])
                    sin_of(sinT[:, d, :, :].rearrange("p s t -> p (s t)"), ang[:], NE, 0.0, [ang.b], [sinT.k(d)])
                    sin_of(cosT[:, d, :, :].rearrange("p s t -> p (s t)"), ang[:], NE, PI / 2, [ang.b], [cosT.k(d)])
                    P.dve(lambda e: e.tensor_copy(out=Rm[:, d, :, :], in_=rr[:, d, :].unsqueeze(2).to_broadcast([128, NSC, J])), r=[tok], w=[Rm.k(d)])
                    P.act(lambda e: e.activation(out=cosB[:, d, :, :], in_=cosT[:, d, :, :], func=AF.Copy), r=[cosT.k(d)], w=[cosB.k(d)])
                    P.act(lambda e: e.activation(out=sinB[:, d, :, :], in_=sinT[:, d, :, :], func=AF.Copy), r=[sinT.k(d)], w=[sinB.k(d)])
                    pos = 0 if d == 0 else J - 1
                    P.dve(lambda e: e.memset(Rm[:, d, :, pos:pos + 1], 0.0), r=[], w=[Rm.k(d)])
                XY = sb("XY", [128, 4, NSC, 128], BF16, pp)
                bnat = [sb("bnat%d" % i, [128, 2, NSC, 16], F32, pp) for i in range(2)]
                bb = [sb("bb%d" % i, [128, 2, NSC, 16], F32, pp) for i in range(2)]
                bt = [sb("bt%d" % i, [128, 2, NSC, 16], F32, pp) for i in range(2)]
                P.dma("sp", bnat[0][:], IN["s5_b_re"][l].rearrange("d (sc g2) p h -> (g2 p) d sc h", g2=2), w=[bnat[0].b])
                P.dma("sp", bnat[1][:], IN["s5_b_im"][l].rearrange("d (sc g2) p h -> (g2 p) d sc h", g2=2), w=[bnat[1].b])
                bc = lambda f_: f_[:].unsqueeze(3).to_broadcast([128, 2, NSC, 16])
                P.dve(lambda e: e.tensor_tensor(out=bt[0][:], in0=bnat[0][:], in1=bc(fre), op=ALU.mult), r=[bnat[0].b, tok], w=[bt[0].b])
                P.dve(lambda e: e.tensor_tensor(out=bt[1][:], in0=bnat[1][:], in1=bc(fim), op=ALU.mult), r=[bnat[1].b, tok], w=[bt[1].b])
                P.dve(lambda e: e.tensor_tensor(out=bb[0][:], in0=bt[0][:], in1=bt[1][:], op=ALU.subtract), r=[bt[0].b, bt[1].b], w=[bb[0].b])
                P.dve(lambda e: e.tensor_tensor(out=bt[0][:], in0=bnat[1][:], in1=bc(fre), op=ALU.mult), r=[bnat[1].b, tok, bb[0].b], w=[bt[0].b])
                P.dve(lambda e: e.tensor_tensor(out=bt[1][:], in0=bnat[0][:], in1=bc(fim), op=ALU.mult), r=[bnat[0].b, tok, bb[0].b], w=[bt[1].b])
                P.dve(lambda e: e.tensor_tensor(out=bb[1][:], in0=bt[0][:], in1=bt[1][:], op=ALU.add), r=[bt[0].b, bt[1].b], w=[bb[1].b])
                P.pool(lambda e: e.memset(XY[:], 0.0), w=[XY.b])
                for ri in range(2):
                    for j in range(4):
                        for g2 in range(2):
                            P.dve(lambda e: e.tensor_copy(out=XY[g2 * 64:(g2 + 1) * 64, ri::2, j::4, (2 * j + g2) * 16:(2 * j + g2 + 1) * 16],
                                                          in_=bb[ri][g2 * 64:(g2 + 1) * 64, :, j::4, :]), r=[bb[ri].b, XY.b], w=[XY.b])

                def xpose_all(dst):
                    for k in range(4):
                        pt = ps[6 + k % 2]
                        ptv = pt[:].bitcast(BF16)
                        for sc in range(NSC):
                            P.pe(lambda e: e.transpose(ptv[:, sc * 128:(sc + 1) * 128], XY[:, k, sc, :], ident_b[:]), r=[XY.b, ident_b.b], w=[pt.b])
                        P.act(lambda e: e.activation(out=dst[:, k, :, :], in_=ptv.rearrange("p (s c) -> p s c", s=NSC), func=AF.Copy),
                              r=[pt.b], w=[dst.b])

                xpose_all(Bblk)
                for w_ in range(2):
                    P.dve(lambda e: e.tensor_scalar(out=BF[w_][:].rearrange("p k s c -> p (k s c)"), in0=Bblk[:].rearrange("p k s c -> p (k s c)"),
                                                    scalar1=flg[:, w_:w_ + 1], scalar2=None, op0=ALU.mult), r=[Bblk.b, flg.b], w=[BF[w_].b])
                cnat = [sb("cnat%d" % i, [128, 2, 2, 64], F32, pp) for i in range(2)]
                P.dma("sp", cnat[0][:], IN["s5_c_re"][l].rearrange("d (cc gl) h p -> (gl h) d cc p", gl=8), w=[cnat[0].b])
                P.dma("sp", cnat[1][:], IN["s5_c_im"][l].rearrange("d (cc gl) h p -> (gl h) d cc p", gl=8), w=[cnat[1].b])
                for ri in range(2):
                    for j in range(4):
                        for g2 in range(2):
                            P.dve(lambda e: e.tensor_scalar(out=XY[:, ri::2, j::4, g2 * 64:(g2 + 1) * 64], in0=cnat[ri][:],
                                                            scalar1=gmk[:, ri, 2 * j + g2:2 * j + g2 + 1], scalar2=None, op0=ALU.mult),
                                  r=[cnat[ri].b, tok, XY.b, Bblk.b], w=[XY.b])
                xpose_all(Cblk)
                P.dma("sp", dsk[:], IN["s5_d"][l].rearrange("(cc p) -> p cc", p=128), w=[dsk.b], allow_slow_non_contiguous=True)
                for cc in range(2):
                    P.dve(lambda e: e.tensor_scalar(out=diagD[:, cc, :], in0=ident_f[:], scalar1=dsk[:, cc:cc + 1], scalar2=None, op0=ALU.mult),
                          r=[dsk.b, ident_f.b], w=[diagD.b])
                for w_ in range(2):
                    P.dve(lambda e: e.tensor_scalar(out=diagF[w_][:], in0=diagD[:], scalar1=flg[:, w_:w_ + 1], scalar2=None, op0=ALU.mult),
                          r=[diagD.b, flg.b], w=[diagF[w_].b])
                P.barrier()
                pp.close()
                uT_ = [sb("uT%d" % i, [128, 4, J], BF16, ph) for i in range(3)]
                car = [sb("car%d" % i, [128, NSC], F32, ph) for i in range(2)]
                NB2 = 2
                wre_ = [sb("wre%d" % i, [128, NSC, J], F32, ph) for i in range(NB2)]
                wim_ = [sb("wim%d" % i, [128, NSC, J], F32, ph) for i in range(NB2)]
                zre_ = [sb("zre%d" % i, [128, NSC, J], F32, ph) for i in range(NB2)]
                zim_ = [sb("zim%d" % i, [128, NSC, J], F32, ph) for i in range(NB2)]
                ta_ = [[sb("ta%d_%d" % (i, k), [128, NSC, J], BF16, ph) for k in range(4)] for i in range(NB2)]
                bub_ = [[sb("bub%d_%d" % (i, k), [128, NSC, J], BF16, ph) for k in range(2)] for i in range(NB2)]
                zb_ = [[sb("zb%d_%d" % (i, k), [128, NSC, J], BF16, ph) for k in range(2)] for i in range(NB2)]
                sre_ = [sb("sre%d" % i, [128, NSC, J], BF16, ph) for i in range(NB2)]
                sim_ = [sb("sim%d" % i, [128, NSC, J], BF16, ph) for i in range(NB2)]
                c4_ = [[sb("c4%d_%d" % (i, k), [128, NSC], F32, ph) for k in range(4)] for i in range(NB2)]
                yfs = [sb("yfs%d" % i, [128, 2, J], F32, ph) for i in range(2)]
                yfl = [sb("yfl%d" % i, [128, 2, J], F32, ph) for i in range(2)]
                ypo = [sb("ypo%d" % i, [128, 2, J], BF16, ph) for i in range(2)]
                F2 = lambda t_: t_[:].rearrange("p s t -> p (s t)")
                BUre, BUim = ps2(0), ps2(2)
                BUt = ps[0].b
                Yb = ps[4]
                jobs = []
                n_it = 0
                for d in range(2):
                    order = list(range(GTILE)) if d == 0 else [1, 0] + list(range(GTILE - 1, 1, -1))
                    for oi, Tt in enumerate(order):
                        jobs.append(dict(d=d, Tt=Tt, n_it=n_it, first=(oi == 0)))
                        n_it += 1

                def stageA(jb, q):
                    d, Tt, n_it = jb["d"], jb["Tt"], jb["n_it"]
                    uT = uT_[n_it % 3]
                    wre, wim, ta = wre_[q], wim_[q], ta_[q]
                    pos_first = 0 if d == 0 else J - 1
                    if jb["first"]:
                        P.dve(lambda e: e.memset(car[0][:], 0.0), w=[car[0].b])
                        P.dve(lambda e: e.memset(car[1][:], 0.0), w=[car[1].b])
                    if Tt < 2:
                        P.dma("sp", uT[:], suC_d.ap[:, :, Tt * J:(Tt + 1) * J].rearrange("c p t -> p c t"), r=[suC_d.b], w=[uT.b])
                    else:
                        gt = Tt - 2
                        rk, lt = gt // LT, gt % LT
                        P.dma("sp", uT[:], suG_d.ap[rk * 512:(rk + 1) * 512, lt * J:(lt + 1) * J].rearrange("(c p) t -> p c t", p=128), r=[suG_d.b], w=[uT.b])
                    for scl in range(NSC):
                        cc = scl // 4
                        for ri in range(2):
                            BU = BUre if ri == 0 else BUim
                            P.pe(lambda e: e.matmul(BU[:, scl * J:(scl + 1) * J], BF[0][:, d * 2 + ri, scl, :], uT[:, cc, :], start=True, stop=False),
                                 r=[BF[0].b, uT.b], w=[BUt])
                            P.pe(lambda e: e.matmul(BU[:, scl * J:(scl + 1) * J], BF[1][:, d * 2 + ri, scl, :], uT[:, 2 + cc, :], start=False, stop=True),
                                 r=[BF[1].b, uT.b], w=[BUt])
                    Cc = cosB[:, d, :, :].rearrange("p s t -> p (s t)")
                    Ss = sinB[:, d, :, :].rearrange("p s t -> p (s t)")
                    bub = bub_[q]
                    P.act(lambda e: e.activation(out=F2(bub[0]), in_=BUre, func=AF.Copy), r=[BUt], w=[bub[0].b])
                    P.act(lambda e: e.activation(out=F2(bub[1]), in_=BUim, func=AF.Copy), r=[BUt], w=[bub[1].b])
                    P.dve(lambda e: e.tensor_tensor(out=F2(ta[0]), in0=F2(bub[0]), in1=Cc, op=ALU.mult), r=[bub[0].b, cosB.k(d)], w=[ta[0].b])
                    P.dve(lambda e: e.tensor_tensor(out=F2(ta[1]), in0=F2(bub[1]), in1=Ss, op=ALU.mult), r=[bub[1].b, sinB.k(d)], w=[ta[1].b])
                    P.dve(lambda e: e.tensor_tensor(out=F2(ta[2]), in0=F2(bub[1]), in1=Cc, op=ALU.mult), r=[bub[1].b, cosB.k(d)], w=[ta[2].b])
                    P.dve(lambda e: e.tensor_tensor(out=F2(ta[3]), in0=F2(bub[0]), in1=Ss, op=ALU.mult), r=[bub[0].b, sinB.k(d)], w=[ta[3].b])
                    P.dve(lambda e: e.tensor_tensor(out=F2(wre), in0=F2(ta[0]), in1=F2(ta[1]), op=ALU.add), r=[ta[0].b, ta[1].b], w=[wre.b])
                    P.dve(lambda e: e.tensor_tensor(out=F2(wim), in0=F2(ta[2]), in1=F2(ta[3]), op=ALU.subtract), r=[ta[2].b, ta[3].b], w=[wim.b])
                    P.pool(lambda e: e.tensor_tensor(out=wre[:, :, pos_first:pos_first + 1], in0=wre[:, :, pos_first:pos_first + 1],
                                                     in1=car[0][:, :].unsqueeze(2), op=ALU.add), r=[wre.b, car[0].b], w=[wre.b])
                    P.pool(lambda e: e.tensor_tensor(out=wim[:, :, pos_first:pos_first + 1], in0=wim[:, :, pos_first:pos_first + 1],
                                                     in1=car[1][:, :].unsqueeze(2), op=ALU.add), r=[wim.b, car[1].b], w=[wim.b])

                def stageC(jb, q):
                    d = jb["d"]
                    wre, wim, zre, zim, c4 = wre_[q], wim_[q], zre_[q], zim_[q], c4_[q]
                    Rr = Rm[:, d, :, :].rearrange("p s t -> p (s t)")
                    pos_last = J - 1 if d == 0 else 0
                    if d == 0:
                        P.dve(lambda e: e.tensor_tensor_scan(out=F2(zre), data0=Rr, data1=F2(wre), initial=0.0, op0=ALU.mult, op1=ALU.add),
                              r=[wre.b, Rm.k(d)], w=[zre.b])
                        P.dve(lambda e: e.tensor_tensor_scan(out=F2(zim), data0=Rr, data1=F2(wim), initial=0.0, op0=ALU.mult, op1=ALU.add),
                              r=[wim.b, Rm.k(d)], w=[zim.b])
                    else:
                        P.dve(lambda e: e.tensor_tensor_scan(out=F2(zre)[:, ::-1], data0=Rr[:, ::-1], data1=F2(wre)[:, ::-1], initial=0.0,
                                                             op0=ALU.mult, op1=ALU.add), r=[wre.b, Rm.k(d)], w=[zre.b])
                        P.dve(lambda e: e.tensor_tensor_scan(out=F2(zim)[:, ::-1], data0=Rr[:, ::-1], data1=F2(wim)[:, ::-1], initial=0.0,
                                                             op0=ALU.mult, op1=ALU.add), r=[wim.b, Rm.k(d)], w=[zim.b])
                    zlr = zre[:, :, pos_last:pos_last + 1].rearrange("p s o -> p (s o)")
                    zli = zim[:, :, pos_last:pos_last + 1].rearrange("p s o -> p (s o)")
                    Kr = kre[:, d, :]
                    Ki = kim[:, d, :]
                    P.pool(lambda e: e.tensor_tensor(out=c4[0][:], in0=zlr, in1=Kr, op=ALU.mult), r=[zre.b, tok], w=[c4[0].b])
                    P.pool(lambda e: e.tensor_tensor(out=c4[1][:], in0=zli, in1=Ki, op=ALU.mult), r=[zim.b, tok], w=[c4[1].b])
                    P.pool(lambda e: e.tensor_tensor(out=c4[2][:], in0=zli, in1=Kr, op=ALU.mult), r=[zim.b, tok], w=[c4[2].b])
                    P.pool(lambda e: e.tensor_tensor(out=c4[3][:], in0=zlr, in1=Ki, op=ALU.mult), r=[zre.b, tok], w=[c4[3].b])
                    P.pool(lambda e: e.tensor_tensor(out=car[0][:], in0=c4[0][:], in1=c4[1][:], op=ALU.subtract), r=[c4[0].b, c4[1].b], w=[car[0].b])
                    P.pool(lambda e: e.tensor_tensor(out=car[1][:], in0=c4[2][:], in1=c4[3][:], op=ALU.add), r=[c4[2].b, c4[3].b], w=[car[1].b])

                def stageB(jb, q):
                    d, Tt, n_it = jb["d"], jb["Tt"], jb["n_it"]
                    uT = uT_[n_it % 3]
                    zre, zim, ta, sr, si_ = zre_[q], zim_[q], ta_[q], sre_[q], sim_[q]
                    Cc = cosB[:, d, :, :].rearrange("p s t -> p (s t)")
                    Ss = sinB[:, d, :, :].rearrange("p s t -> p (s t)")
                    zb = zb_[q]
                    P.act(lambda e: e.activation(out=F2(zb[0]), in_=F2(zre), func=AF.Copy), r=[zre.b], w=[zb[0].b])
                    P.act(lambda e: e.activation(out=F2(zb[1]), in_=F2(zim), func=AF.Copy), r=[zim.b], w=[zb[1].b])
                    P.dve(lambda e: e.tensor_tensor(out=F2(ta[0]), in0=F2(zb[0]), in1=Cc, op=ALU.mult), r=[zb[0].b, cosB.k(d)], w=[ta[0].b])
                    P.dve(lambda e: e.tensor_tensor(out=F2(ta[1]), in0=F2(zb[1]), in1=Ss, op=ALU.mult), r=[zb[1].b, sinB.k(d)], w=[ta[1].b])
                    P.dve(lambda e: e.tensor_tensor(out=F2(ta[2]), in0=F2(zb[0]), in1=Ss, op=ALU.mult), r=[zb[0].b, sinB.k(d)], w=[ta[2].b])
                    P.dve(lambda e: e.tensor_tensor(out=F2(ta[3]), in0=F2(zb[1]), in1=Cc, op=ALU.mult), r=[zb[1].b, cosB.k(d)], w=[ta[3].b])
                    P.dve(lambda e: e.tensor_tensor(out=F2(sr), in0=F2(ta[0]), in1=F2(ta[1]), op=ALU.subtract), r=[ta[0].b, ta[1].b], w=[sr.b])
                    P.dve(lambda e: e.tensor_tensor(out=F2(si_), in0=F2(ta[2]), in1=F2(ta[3]), op=ALU.add), r=[ta[2].b, ta[3].b], w=[si_.b])
                    for cc in range(2):
                        first = True
                        if d == 1:
                            P.pe(lambda e: e.matmul(Yb[:, cc * J:(cc + 1) * J], diagF[0][:, cc, :], uT[:, cc, :], start=True, stop=False),
                                 r=[diagF[0].b, uT.b], w=[Yb.b])
                            P.pe(lambda e: e.matmul(Yb[:, cc * J:(cc + 1) * J], diagF[1][:, cc, :], uT[:, 2 + cc, :], start=False, stop=False),
                                 r=[diagF[1].b, uT.b], w=[Yb.b])
                            first = False
                        for q4 in range(4):
                            scl = cc * 4 + q4
                            P.pe(lambda e: e.matmul(Yb[:, cc * J:(cc + 1) * J], Cblk[:, d * 2 + 0, scl, :], sr[:, scl, :], start=first, stop=False),
                                 r=[Cblk.b, sr.b], w=[Yb.b])
                            first = False
                            P.pe(lambda e: e.matmul(Yb[:, cc * J:(cc + 1) * J], Cblk[:, d * 2 + 1, scl, :], si_[:, scl, :], start=False, stop=(q4 == 3)),
                                 r=[Cblk.b, si_.b], w=[Yb.b])
                    Yv = Yb[:, 0:2 * J].rearrange("p (c t) -> p c t", c=2)
                    if d == 0:
                        ys_ = yfs[n_it % 2]
                        P.act(lambda e: e.activation(out=ys_[:], in_=Yv, func=AF.Copy), r=[Yb.b], w=[ys_.b])
                        P.dma("sp", yf_d.ap[:, :, Tt * J:(Tt + 1) * J].rearrange("c p t -> p c t"), ys_[:], r=[ys_.b], w=[yf_d.k(Tt)])
                    else:
                        yl = yfl[n_it % 2]
                        yo = ypo[n_it % 2]
                        P.dma("sp", yl[:], yf_d.ap[:, :, Tt * J:(Tt + 1) * J].rearrange("c p t -> p c t"), r=[yf_d.k(Tt)], w=[yl.b])
                        P.dve(lambda e: e.tensor_tensor(out=yo[:], in0=Yv, in1=yl[:], op=ALU.add), r=[Yb.b, yl.b], w=[yo.b])
                        if Tt < 2:
                            P.dma("sp", ypLc_d.ap[:, Tt * J:(Tt + 1) * J].rearrange("(c p) t -> p c t", p=128), yo[:], r=[yo.b], w=[ypLc_d.b])
                        else:
                            P.dma("sp", ypLl_d.ap[:, (Tt - 2) * J:(Tt - 1) * J].rearrange("(c p) t -> p c t", p=128), yo[:], r=[yo.b], w=[ypLl_d.b])

                for i, jb in enumerate(jobs):
                    stageA(jb, i % 2)
                    if i > 0:
                        stageB(jobs[i - 1], (i - 1) % 2)
                    stageC(jb, i % 2)
                stageB(jobs[-1], (len(jobs) - 1) % 2)
                P.barrier()
            P.allgather(ypLc_d, ypGc_d, r=[ypLc_d.b], w=[ypGc_d.b])
            P.allgather(ypLl_d, ypGl_d, r=[ypLl_d.b], w=[ypGl_d.b])
            with ExitStack() as ph:
                wglu = sb("wglu", [128, 4, 512], BF16, ph)
                P.dma("pool", wglu[:], IN["s5_w_glu"][l].rearrange("(c p) n -> p c n", p=128), w=[wglu.b])
                yA_ = [sb("yA%d" % i, [128, 4, 512], BF16, ph) for i in range(2)]
                yB_ = [sb("yB%d" % i, [128, 4, 512], BF16, ph) for i in range(2)]
                ysel = sb("ysel", [128, 4, 512], F32, ph)
                g_ = [sb("gg%d" % i, [128, 4, 512], BF16, ph) for i in range(2)]
                sg = [sb("sgg%d" % i, [128, 512], F32, ph) for i in range(2)]
                yo_ = [sb("yoo%d" % i, [128, 4, 512], BF16, ph) for i in range(2)]
                for bi, (t0, N, isctx) in enumerate(BLOCKS):
                    if isctx and not need_ctx:
                        continue
                    yA, yB, g, yo = yA_[bi % 2], yB_[bi % 2], g_[bi % 2], yo_[bi % 2]
                    if isctx:
                        P.dma("sp", yA[:, :, :N], ypGc_d.ap[:, 0:N].rearrange("(c p) t -> p c t", p=128), r=[ypGc_d.b], w=[yA.b])
                        P.act(lambda e: e.activation(out=g[:, :, :N], in_=yA[:, :, :N], func=AF.Gelu), r=[yA.b], w=[g.b])
                    else:
                        l0 = t0 - NCTX
                        P.dma("sp", yA[:, :, :N], ypGl_d.ap[:, l0:l0 + N].rearrange("(c p) t -> p c t", p=128), r=[ypGl_d.b], w=[yA.b])
                        P.dma("sp", yB[:, :, :N], ypGl_d.ap[:, NLAT + l0:NLAT + l0 + N].rearrange("(c p) t -> p c t", p=128), r=[ypGl_d.b], w=[yB.b])
                        P.dve(lambda e: e.tensor_scalar(out=ysel[:, :, :N], in0=yA[:, :, :N], scalar1=flg[:, 0:1], scalar2=None, op0=ALU.mult),
                              r=[yA.b, flg.b], w=[ysel.b])
                        P.dve(lambda e: e.scalar_tensor_tensor(out=ysel[:, :, :N], in0=yB[:, :, :N], scalar=flg[:, 1:2], in1=ysel[:, :, :N],
                                                               op0=ALU.mult, op1=ALU.add), r=[yB.b, flg.b, ysel.b], w=[ysel.b])
                        P.act(lambda e: e.activation(out=g[:, :, :N], in_=ysel[:, :, :N], func=AF.Gelu), r=[ysel.b], w=[g.b])
                    for oc in range(4):
                        Gb = ps[oc % 2]
                        for kc in range(4):
                            P.pe(lambda e: e.matmul(Gb[:, :N], wglu[:, kc, oc * 128:(oc + 1) * 128], g[:, kc, :N], start=(kc == 0), stop=(kc == 3)),
                                 r=[wglu.b, g.b], w=[Gb.b])
                        sg_ = sg[oc % 2]
                        P.act(lambda e: e.activation(out=sg_[:, :N], in_=Gb[:, :N], func=AF.Sigmoid), r=[Gb.b], w=[sg_.b])
                        P.dve(lambda e: e.tensor_tensor(out=yo[:, oc, :N], in0=g[:, oc, :N], in1=sg_[:, :N], op=ALU.mult), r=[g.b, sg_.b], w=[yo.k(oc)])
                    P.dma("sp", ysT_d.ap[:, :, t0:t0 + N].rearrange("c p t -> p c t"), yo[:, :, :N], r=[yo.k(oc) for oc in range(4)], w=[ysT_d.k(bi)])
                P.barrier()

        def phase_M(l):
            need_ctx = l < DEPTH - 1
            with ExitStack() as ph:
                wg = sb("wg", [128, 8, 3072], BF16, ph)
                wpd = sb("wpd", [128, 4, 1024], BF16, ph)
                wps = sb("wps", [128, 4, 1024], BF16, ph)
                wpw = sb("wpw", [64, 8, 1024], BF16, ph)
                wout = sb("wout", [128, 8, 1024], BF16, ph)
                for c in range(8):
                    P.dma("pool", wg[:, c, :], IN["w_in"][l, c * 128:(c + 1) * 128, 2816:5888], w=[wg.k(c)])
                P.dma("pool", wpd[:], IN["w_proj_diff"][l].rearrange("(c p) n -> p c n", p=128), w=[wpd.b])
                P.dma("pool", wps[:], IN["w_proj_s5"][l].rearrange("(c p) n -> p c n", p=128), w=[wps.b])
                P.dma("pool", wpw[:], IN["w_proj_win"][l].rearrange("(h d) n -> d h n", d=64), w=[wpw.b])
                for c in range(0, 8, 2):
                    P.dma("pool", wout[:, c:c + 2, :], IN["w_out"][l, c * 128:(c + 2) * 128, :].rearrange("(c p) n -> p c n", p=128), w=[wout.k(c // 2)])
                a_ = [sb("am%d" % i, [128, 8, 512], BF16, ph) for i in range(2)]
                yd_ = [sb("ydm%d" % i, [128, 4, 512], BF16, ph) for i in range(2)]
                ys_ = [sb("ysm%d" % i, [128, 4, 512], BF16, ph) for i in range(2)]
                yw_ = [sb("ywm%d" % i, [64, 8, 512], BF16, ph) for i in range(2)]
                h_ = [sb("hm%d" % i, [128, 8, 512], F32, ph) for i in range(2)]
                mT = sb("mT", [128, 8, 512], BF16, ph)
                sig = [sb("sig%d" % i, [128, 512], F32, ph) for i in range(3)]
                mt = [sb("mt%d" % i, [128, 512], F32, ph) for i in range(3)]
                gcnt = [0]
                for bi, (t0, N, isctx) in enumerate(BLOCKS):
                    if isctx and not need_ctx:
                        continue
                    mi = 1 if isctx else 0
                    a, yd, ys, yw, h = a_[bi % 2], yd_[bi % 2], ys_[bi % 2], yw_[bi % 2], h_[bi % 2]
                    dv = lambda dt_: dt_.ap[:, :, t0:t0 + N].rearrange("c p t -> p c t")
                    tl = tiles_of(t0, N)
                    P.dma("sp", a[:, :, :N], dv(aT_d), r=[aT_d.k(bi)], w=[a.b])
                    P.dma("sp", yd[:, :, :N], dv(ydT_d), r=[ydT_d.k(bi)], w=[yd.b])
                    P.dma("sp", ys[:, :, :N], dv(ysT_d), r=[ysT_d.k(bi)], w=[ys.b])
                    P.dma("sp", yw[:, :, :N], ywT_d.ap[:, :, t0:t0 + N].rearrange("h d t -> d h t"), r=[ywT_d.k(bi)], w=[yw.b])
                    P.dma("sp", h[:, :, :N], dv(hT_d), r=[hT_d.k(t) for t in tl], w=[h.b])
                    for oc in range(8):
                        for br in range(3):
                            G = ps[gcnt[0] % 2]
                            gcnt[0] += 1
                            Pj = ps[2 + br]
                            for c in range(8):
                                P.pe(lambda e: e.matmul(G[:, :N], wg[:, c, br * 1024 + oc * 128:br * 1024 + (oc + 1) * 128], a[:, c, :N], start=(c == 0), stop=(c == 7)),
                                     r=[wg.k(c), a.b], w=[G.b])
                            P.act(lambda e: e.activation(out=sig[br][:, :N], in_=G[:, :N], func=AF.Sigmoid), r=[G.b], w=[sig[br].b])
                            if br == 0:
                                for k in range(4):
                                    P.pe(lambda e: e.matmul(Pj[:, :N], wpd[:, k, oc * 128:(oc + 1) * 128], yd[:, k, :N], start=(k == 0), stop=(k == 3)),
                                         r=[wpd.b, yd.b], w=[Pj.b])
                            elif br == 1:
                                for k in range(4):
                                    P.pe(lambda e: e.matmul(Pj[:, :N], wps[:, k, oc * 128:(oc + 1) * 128], ys[:, k, :N], start=(k == 0), stop=(k == 3)),
                                         r=[wps.b, ys.b], w=[Pj.b])
                            else:
                                for k in range(8):
                                    P.pe(lambda e: e.matmul(Pj[:, :N], wpw[:, k, oc * 128:(oc + 1) * 128], yw[:, k, :N], start=(k == 0), stop=(k == 7)),
                                         r=[wpw.b, yw.b], w=[Pj.b])
                            P.dve(lambda e: e.tensor_tensor(out=mt[br][:, :N], in0=Pj[:, :N], in1=sig[br][:, :N], op=ALU.mult), r=[Pj.b, sig[br].b], w=[mt[br].b])
                        P.dve(lambda e: e.tensor_tensor(out=mt[0][:, :N], in0=mt[0][:, :N], in1=mt[1][:, :N], op=ALU.add), r=[mt[0].b, mt[1].b], w=[mt[0].b])
                        P.dve(lambda e: e.tensor_tensor(out=mT[:, oc, :N], in0=mt[0][:, :N], in1=mt[2][:, :N], op=ALU.add), r=[mt[0].b, mt[2].b], w=[mT.k(oc)])
                    for oc in range(8):
                        O = ps[5 + oc % 2]
                        for c in range(8):
                            P.pe(lambda e: e.matmul(O[:, :N], wout[:, c, oc * 128:(oc + 1) * 128], mT[:, c, :N], start=(c == 0), stop=(c == 7)),
                                 r=[wout.k(c // 2), mT.k(c)], w=[O.b])
                        P.dve(lambda e: e.scalar_tensor_tensor(out=h[:, oc, :N], in0=O[:, :N], scalar=modv[:, l, 16 + oc, mi:mi + 1], in1=h[:, oc, :N],
                                                               op0=ALU.mult, op1=ALU.add), r=[O.b, modv.b, h.b], w=[h.b])
                    P.dma("sp", dv(hT_d), h[:, :, :N], r=[h.b], w=[hT_d.k(t) for t in tl])
                P.barrier()

        def phase_F(l, last):
            need_ctx = l < DEPTH - 1
            with ExitStack() as ph:
                w1 = sb("w1", [128, 8, 4096], BF16, ph)
                w2 = sb("w2", [128, 32, 1024], BF16, ph)
                for c in range(8):
                    P.dma("pool", w1[:, c, :], IN["w_ff1"][l, c * 128:(c + 1) * 128, :], w=[w1.k(c)])
                for c in range(0, 32, 4):
                    P.dma("pool", w2[:, c:c + 4, :], IN["w_ff2"][l, c * 128:(c + 4) * 128, :].rearrange("(c p) n -> p c n", p=128), w=[w2.k(c // 4)])
                FN = 512
                h = sb("hf", [128, 8, FN], F32, ph)
                sq = sb("sqf", [128, 8, FN], BF16, ph)
                rt = sb("rtf", [128, FN], F32, ph)
                rstd = sb("rstdf", [128, FN], F32, ph)
                tmp = [sb("tmpf%d" % i, [128, FN], F32, ph) for i in range(2)]
                fT = sb("fT", [128, 8, FN], BF16, ph)
                rl_ = [sb("rlf%d" % i, [128, FN], BF16, ph) for i in range(2)]
                hid = sb("hid", [128, 32, FN], BF16, ph)
                class OTV:
                    def __init__(self, i):
                        self.i = i
                        self.b = sq.b

                    def __getitem__(self, idx):
                        return sq[:].rearrange("p c t -> p (c t)").bitcast(F32)[:, self.i * 1024:(self.i + 1) * 1024][idx]

                    def k(self, key):
                        return sq.b
                ot = [OTV(i) for i in range(2)]
                cnt = [0, 0]
                for bi, (t0, N, isctx) in enumerate(BLOCKS):
                    if isctx and not need_ctx:
                        continue
                    mi = 1 if isctx else 0
                    tl = tiles_of(t0, N)
                    dv = lambda dt_: dt_.ap[:, :, t0:t0 + N].rearrange("c p t -> p c t")
                    P.dma("sp", h[:, :, :N], dv(hT_d), r=[hT_d.k(t) for t in tl], w=[h.b] + [h.k(oc) for oc in range(8)])
                    P.act(lambda e: e.activation(out=sq[:, :, :N], in_=h[:, :, :N], func=AF.Square), r=[h.b], w=[sq.b])
                    for c in range(8):
                        P.pe(lambda e: e.matmul(ps[0][:, :N], ones_b[:], sq[:, c, :N], start=(c == 0), stop=(c == 7)), r=[sq.b, ones_b.b], w=[ps[0].b])
                    P.act(lambda e: e.activation(out=rt[:, :N], in_=ps[0][:, :N], func=AF.Sqrt, scale=1.0 / D, bias=EPS), r=[ps[0].b], w=[rt.b])
                    P.dve(lambda e: e.reciprocal(out=rstd[:, :N], in_=rt[:, :N]), r=[rt.b], w=[rstd.b])
                    for c in range(8):
                        tp = tmp[c % 2]
                        P.dve(lambda e: e.scalar_tensor_tensor(out=tp[:, :N], in0=h[:, c, :N], scalar=gm2[:, l, c, mi:mi + 1], in1=rstd[:, :N], op0=ALU.mult, op1=ALU.mult),
                              r=[h.b, gm2.b, rstd.b], w=[tp.b])
                        P.act(lambda e: e.activation(out=fT[:, c, :N], in_=tp[:, :N], func=AF.Identity, bias=modv[:, l, 24 + c, mi:mi + 1], scale=1.0),
                              r=[tp.b, modv.b], w=[fT.k(c)])
                    for fc in range(32):
                        pb = ps[1 + fc % 3]
                        for c in range(8):
                            P.pe(lambda e: e.matmul(pb[:, :N], w1[:, c, fc * 128:(fc + 1) * 128], fT[:, c, :N], start=(c == 0), stop=(c == 7)),
                                 r=[w1.k(c), fT.k(c)], w=[pb.b])
                        rl = rl_[fc % 2]
                        P.act(lambda e: e.activation(out=rl[:, :N], in_=pb[:, :N], func=AF.Relu), r=[pb.b], w=[rl.b])
                        P.dve(lambda e: e.tensor_tensor(out=hid[:, fc, :N], in0=rl[:, :N], in1=rl[:, :N], op=ALU.mult), r=[rl.b], w=[hid.k(fc // 4)])
                    for oc in range(8):
                        O = ps[4 + oc % 2]
                        for fc in range(32):
                            P.pe(lambda e: e.matmul(O[:, :N], w2[:, fc, oc * 128:(oc + 1) * 128], hid[:, fc, :N], start=(fc == 0), stop=(fc == 31)),
                                 r=[w2.k(fc // 4), hid.k(fc // 4)], w=[O.b])
                        P.dve(lambda e: e.scalar_tensor_tensor(out=h[:, oc, :N], in0=O[:, :N], scalar=modv[:, l, 40 + oc, mi:mi + 1], in1=h[:, oc, :N],
                                                               op0=ALU.mult, op1=ALU.add), r=[O.b, modv.b, h.b], w=[h.k(oc)])
                    if not last:
                        P.dma("sp", dv(hT_d), h[:, :, :N], r=[h.k(oc) for oc in range(8)] + [h.b], w=[hT_d.k(t) for t in tl])
                    else:
                        for j in range(N // 128):
                            o = ot[cnt[0] % 2]
                            cnt[0] += 1
                            for half in range(2):
                                pt = ps[6 + half]
                                for q in range(4):
                                    c = half * 4 + q
                                    P.pe(lambda e: e.transpose(pt[:, q * 128:(q + 1) * 128], h[:, c, j * 128:(j + 1) * 128], ident_f[:]),
                                         r=[h.k(c), h.b, ident_f.b], w=[pt.b])
                                if half == 0:
                                    P.act(lambda e: e.activation(out=o[:, 0:512], in_=pt[:], func=AF.Copy), r=[pt.b], w=[o.k(0)])
                                else:
                                    P.dve(lambda e: e.tensor_copy(out=o[:, 512:1024], in_=pt[:]), r=[pt.b], w=[o.k(1)])
                            row = t0 - NCTX + j * 128
                            P.dma("sp", out_d.ap[row:row + 128, :], o[:], r=[o.k(0), o.k(1)], w=[out_d.k(row)])
                P.barrier()

        for l in range(nl):
            phase_A(l)
            if stop_after == "A":
                break
            if stop_after not in ("W", "S"):
                phase_D(l)
            if stop_after == "D":
                break
            if stop_after != "S":
                phase_W(l)
            if stop_after == "W":
                break
            phase_S(l)
            if stop_after == "S":
                break
            phase_M(l)
            if stop_after == "M":
                break
            phase_F(l, last=(l == nl - 1) and final_out)
        P.barrier()
        stats = P.finalize(st)
        print("build stats", stats)
    return nc


S5_GROUP_AXIS = {"s5_lambda_re": 2, "s5_lambda_im": 2, "s5_log_dt": 2, "s5_b_re": 2, "s5_b_im": 2, "s5_c_re": 2, "s5_c_im": 2}


def make_in_maps(inputs, n_cores, nl=DEPTH):
    consts = host_consts()
    maps = []
    for core in range(n_cores):
        b, h = core // 2, core % 2
        m = {"x": np.ascontiguousarray(inputs["x"][b][h * NLAT:(h + 1) * NLAT]), "ctx": np.ascontiguousarray(inputs["ctx"][b]),
             "c": np.ascontiguousarray(inputs["c"][b]), "c_ctx": np.ascontiguousarray(inputs["c_ctx"])}
        for name, _ in WEIGHT_SPECS:
            w = inputs[name][:nl]
            if name in S5_GROUP_AXIS:
                w = np.take(w, np.arange(h * 16, (h + 1) * 16), axis=S5_GROUP_AXIS[name])
            elif name == "s5_d":
                w = w[:, h * 256:(h + 1) * 256]
            m[name] = np.ascontiguousarray(w)
        for name, _ in CONST_SPECS:
            v = consts.get(name)
            if name in ("rope_cos", "rope_sin"):
                v = np.ascontiguousarray(v[h * NLAT:(h + 1) * NLAT])
            elif name == "flags":
                v = np.zeros((128, 2), np.float32)
                v[:, h] = 1.0
            m[name] = v
        maps.append(m)
    return maps


def kernel(**inputs):
    n = 8
    nc = build(n_cores=n)
    res = run_bass_kernel_spmd(nc, make_in_maps(inputs, n), core_ids=list(range(n)))
    return np.stack([np.concatenate([res.results[2 * b]["out"], res.results[2 * b + 1]["out"]], 0) for b in range(4)], 0).astype(np.float32)
```
